# Optimizing a Trainium2 kernel written in Bass

```python
import jax, jax.numpy as jnp
from jax import lax
import numpy as np

D_MODEL = 1024
BATCH = 2
SEQ = 8192
DEPTH = 4

EPS = 1e-6
D_LRU = 384
LRU_HEADS = 6
LRU_HEAD_DIM = D_LRU // LRU_HEADS
CONV_WIDTH = 4
LRU_C = 8.0
MLA_HEADS = 6
QK_NOPE_DIM = 64
QK_ROPE_DIM = 32
V_HEAD_DIM = 64
D_MLA = MLA_HEADS * V_HEAD_DIM
Q_LORA_RANK = 384
KV_LORA_RANK = 256
ROPE_BASE = 10000.0
Q_BLOCK = 128
POOL_WINDOWS = (2, 4, 8, 16)
POOL_GROUP_DIM = 64
D_POOL = POOL_GROUP_DIM * len(POOL_WINDOWS)
D_MIX = D_LRU + D_MLA + D_POOL
IN_SIZES = (D_LRU, D_LRU, Q_LORA_RANK, KV_LORA_RANK, QK_ROPE_DIM, D_MLA, D_POOL, D_POOL)
D_IN = sum(IN_SIZES)

kernel_name = "hybrid_lru_mla_pool_trunk"


def rms_norm(x, g):
    xf = x.astype(jnp.float32)
    y = xf * lax.rsqrt(jnp.mean(xf * xf, axis=-1, keepdims=True) + EPS)
    return (y * g.astype(jnp.float32)).astype(x.dtype)


def rope_tables(seq_len):
    pos = jnp.arange(seq_len, dtype=jnp.float32)
    inv_freq = ROPE_BASE ** (-jnp.arange(0, QK_ROPE_DIM, 2, dtype=jnp.float32) / QK_ROPE_DIM)
    ang = pos[:, None] * inv_freq[None, :]
    return jnp.cos(ang), jnp.sin(ang)


def apply_rope(x, cos, sin):
    shape = (1, x.shape[1]) + (1,) * (x.ndim - 3) + (QK_ROPE_DIM // 2,)
    c = cos.reshape(shape)
    s = sin.reshape(shape)
    xf = x.astype(jnp.float32)
    x1, x2 = jnp.split(xf, 2, axis=-1)
    return jnp.concatenate([x1 * c - x2 * s, x1 * s + x2 * c], axis=-1).astype(x.dtype)


def causal_depthwise_conv(x, w, b):
    S = x.shape[1]
    xp = jnp.pad(x, ((0, 0), (CONV_WIDTH - 1, 0), (0, 0)))
    y = b
    for k in range(CONV_WIDTH):
        y = y + xp[:, k:k + S, :] * w[k]
    return y


def rg_lru(x, w_r, b_r, w_i, b_i, lam):
    B, S, _ = x.shape
    xf = x.astype(jnp.float32)
    xh = xf.reshape(B, S, LRU_HEADS, LRU_HEAD_DIM)
    r = jax.nn.sigmoid(jnp.einsum('bshi,hij->bshj', xh, w_r.astype(jnp.float32)).reshape(B, S, D_LRU) + b_r.astype(jnp.float32))
    i = jax.nn.sigmoid(jnp.einsum('bshi,hij->bshj', xh, w_i.astype(jnp.float32)).reshape(B, S, D_LRU) + b_i.astype(jnp.float32))
    log_a = -LRU_C * r * jax.nn.softplus(-lam.astype(jnp.float32))
    a = jnp.exp(log_a)
    u = jnp.sqrt(-jnp.expm1(2.0 * log_a)) * (i * xf)

    def combine(left, right):
        a_l, h_l = left
        a_r, h_r = right
        return a_l * a_r, a_r * h_l + h_r

    _, h = lax.associative_scan(combine, (a, u), axis=1)
    return h.astype(x.dtype)


def mla(c_q, c_kv, k_r, q_norm_g, w_uq, kv_norm_g, w_ukv, cos, sin):
    B, S, _ = c_q.shape
    q = (rms_norm(c_q, q_norm_g) @ w_uq).reshape(B, S, MLA_HEADS, QK_NOPE_DIM + QK_ROPE_DIM)
    q_nope = q[..., :QK_NOPE_DIM]
    q_rope = apply_rope(q[..., QK_NOPE_DIM:], cos, sin)
    kv = (rms_norm(c_kv, kv_norm_g) @ w_ukv).reshape(B, S, MLA_HEADS, QK_NOPE_DIM + V_HEAD_DIM)
    k_nope = kv[..., :QK_NOPE_DIM]
    v = kv[..., QK_NOPE_DIM:]
    k_rope = apply_rope(k_r, cos, sin)
    scale = (QK_NOPE_DIM + QK_ROPE_DIM) ** -0.5
    outs = []
    for blk in range(S // Q_BLOCK):
        q0 = blk * Q_BLOCK
        q1 = q0 + Q_BLOCK
        s = (jnp.einsum('bqhd,bkhd->bhqk', q_nope[:, q0:q1], k_nope[:, :q1])
             + jnp.einsum('bqhr,bkr->bhqk', q_rope[:, q0:q1], k_rope[:, :q1]))
        s = s.astype(jnp.float32) * scale
        mask = jnp.arange(q1)[None, :] <= (q0 + jnp.arange(Q_BLOCK))[:, None]
        s = jnp.where(mask, s, -1e30)
        p = jax.nn.softmax(s, axis=-1).astype(v.dtype)
        outs.append(jnp.einsum('bhqk,bkhd->bqhd', p, v[:, :q1]))
    return jnp.concatenate(outs, axis=1).reshape(B, S, D_MLA)


def multi_scale_pool(x, w_pool, pool_scale):
    B, S, _ = x.shape
    xf = x.astype(jnp.float32)
    n_seen = jnp.arange(1, S + 1, dtype=jnp.float32)
    outs = []
    for g, w in enumerate(POOL_WINDOWS):
        xg = xf[..., g * POOL_GROUP_DIM:(g + 1) * POOL_GROUP_DIM]
        cs = jnp.cumsum(xg, axis=1)
        lower = jnp.pad(cs, ((0, 0), (w, 0), (0, 0)))[:, :S]
        cnt = jnp.minimum(n_seen, float(w))[None, :, None]
        outs.append((cs - lower) / cnt - xg)
    pooled = jnp.stack(outs, axis=2)
    y = jnp.einsum('bsgi,gij->bsgj', pooled, w_pool.astype(jnp.float32)).reshape(B, S, D_POOL)
    return (y * pool_scale.astype(jnp.float32)).astype(x.dtype)


def setup_inputs(seed: int = 0) -> dict:
    key = jax.random.key(seed)
    ks = jax.random.split(key, 20)

    def nrm(k, shape, scale):
        return scale * jax.random.normal(k, shape, jnp.float32)

    u = jax.random.uniform(ks[9], (DEPTH, D_LRU), jnp.float32, minval=0.9, maxval=0.999)
    s = u ** (1.0 / LRU_C)
    lru_lambda = jnp.log(s) - jnp.log1p(-s)
    return {
        "x": nrm(ks[0], (BATCH, SEQ, D_MODEL), 1.0),
        "norm_g": 1.0 + nrm(ks[1], (DEPTH, D_MODEL), 0.02),
        "w_in": nrm(ks[2], (DEPTH, D_MODEL, D_IN), D_MODEL ** -0.5),
        "conv_w": nrm(ks[3], (DEPTH, CONV_WIDTH, D_LRU), CONV_WIDTH ** -0.5),
        "conv_b": nrm(ks[4], (DEPTH, D_LRU), 0.01),
        "w_rg": nrm(ks[5], (DEPTH, LRU_HEADS, LRU_HEAD_DIM, LRU_HEAD_DIM), LRU_HEAD_DIM ** -0.5),
        "b_rg": nrm(ks[6], (DEPTH, D_LRU), 0.01),
        "w_ig": nrm(ks[7], (DEPTH, LRU_HEADS, LRU_HEAD_DIM, LRU_HEAD_DIM), LRU_HEAD_DIM ** -0.5),
        "b_ig": nrm(ks[8], (DEPTH, D_LRU), 0.01),
        "lru_lambda": lru_lambda,
        "q_norm_g": 1.0 + nrm(ks[10], (DEPTH, Q_LORA_RANK), 0.02),
        "w_uq": nrm(ks[11], (DEPTH, Q_LORA_RANK, MLA_HEADS * (QK_NOPE_DIM + QK_ROPE_DIM)), Q_LORA_RANK ** -0.5),
        "kv_norm_g": 1.0 + nrm(ks[12], (DEPTH, KV_LORA_RANK), 0.02),
        "w_ukv": nrm(ks[13], (DEPTH, KV_LORA_RANK, MLA_HEADS * (QK_NOPE_DIM + V_HEAD_DIM)), KV_LORA_RANK ** -0.5),
        "w_pool": nrm(ks[14], (DEPTH, len(POOL_WINDOWS), POOL_GROUP_DIM, POOL_GROUP_DIM), POOL_GROUP_DIM ** -0.5),
        "pool_scale": 1.0 + nrm(ks[15], (DEPTH, D_POOL), 0.1),
        "w_out": nrm(ks[16], (DEPTH, D_MIX, D_MODEL), D_MIX ** -0.5),
        "final_norm_g": 1.0 + nrm(ks[17], (D_MODEL,), 0.02),
    }


def reference(x, norm_g, w_in, conv_w, conv_b, w_rg, b_rg, w_ig, b_ig, lru_lambda,
              q_norm_g, w_uq, kv_norm_g, w_ukv, w_pool, pool_scale, w_out, final_norm_g):
    S = x.shape[1]
    cos, sin = rope_tables(S)
    offsets = np.cumsum(IN_SIZES)[:-1].tolist()
    for l in range(DEPTH):
        h = rms_norm(x, norm_g[l])
        z = h @ w_in[l]
        za, ga, cq, ckv, kr, gb, zc, gc = jnp.split(z, offsets, axis=-1)
        xa = causal_depthwise_conv(za, conv_w[l], conv_b[l])
        ya = rg_lru(xa, w_rg[l], b_rg[l], w_ig[l], b_ig[l], lru_lambda[l]) * jax.nn.silu(ga)
        yb = mla(cq, ckv, kr, q_norm_g[l], w_uq[l], kv_norm_g[l], w_ukv[l], cos, sin) * jax.nn.silu(gb)
        yc = multi_scale_pool(zc, w_pool[l], pool_scale[l]) * jax.nn.silu(gc)
        y = jnp.concatenate([ya, yb, yc], axis=-1)
        x = x + y @ w_out[l]
    return rms_norm(x, final_norm_g)
```

```python
import numpy as np
import ml_dtypes
import concourse.bass as bass
import concourse.mybir as mybir
from concourse.bass_utils import run_bass_kernel_spmd

F32 = mybir.dt.float32
BF16 = mybir.dt.bfloat16
AF = mybir.ActivationFunctionType
ALU = mybir.AluOpType

DEPTH = 4
D = 1024
S = 8192
B = 2
TT = 512
NTILE = S // TT
NG = 4
NW = NTILE // NG
TOKC = NW * TT
NCORE = B * NG
R3 = 6 * 96 + 128 * 6
GROUPS = [[0, 1, 2, 3], [4, 5, 6, 7]]
EPS = 1e-6
SCALE = 96.0 ** -0.5
WINC = 2368
O_ZA, O_GA, O_CQ, O_CKV, O_GB, O_ZC, O_GC, O_KR, O_KRS = 0, 384, 768, 1152, 1408, 1792, 2048, 2304, 2336

_P = {}
_off = 0
for _n, _w in [("normg", DEPTH * 8), ("fng", 8), ("convw", DEPTH * 12), ("convb", DEPTH * 3),
               ("brg", DEPTH * 3), ("big", DEPTH * 3), ("lam", DEPTH * 3), ("qng", DEPTH * 3),
               ("kvng", DEPTH * 2), ("pscale", DEPTH * 2), ("invw", 2)]:
    _P[_n] = _off
    _off += _w
NPAR = _off
CP_W, CP_EB, CP_F, CP_IT, CP_G, NCPAR = 0, 4, 8, 12, 44, 48


class Ctx:
    def __init__(self, nc):
        self.nc = nc
        self.sems = {}
        self.eng = {}
        for name, h in [("pe", nc.tensor), ("act", nc.scalar), ("dve", nc.vector),
                        ("pool", nc.gpsimd), ("sp", nc.sync)]:
            self.eng[name] = dict(h=h, sem="e_" + name, count=0, waited={})
            self._sem("e_" + name)
        self.res_w = {}
        self.res_r = {}
        self.dcount = {}

    def _sem(self, name):
        if name not in self.sems:
            self.sems[name] = self.nc.semaphore(name).__enter__()
        return self.sems[name]

    def _deps(self, reads, writes):
        deps = []
        for k in reads:
            if k in self.res_w:
                deps.append(self.res_w[k])
        for k in writes:
            if k in self.res_w:
                deps.append(self.res_w[k])
            deps.extend(self.res_r.get(k, ()))
        return deps

    def _wait(self, e, deps):
        E = self.eng[e]
        need = {}
        for (sem, val, src) in deps:
            if src == "pe" and e == "pe":
                continue
            if src == "dma":
                val = max(val, self.dcount[sem])
            if val > need.get(sem, 0):
                need[sem] = val
        for sem, val in need.items():
            if E["waited"].get(sem, 0) < val:
                E["h"].wait_ge(self.sems[sem], val)
                E["waited"][sem] = val

    def _record(self, tok, reads, writes):
        for k in reads:
            self.res_r.setdefault(k, []).append(tok)
        for k in writes:
            self.res_w[k] = tok
            self.res_r[k] = []

    def op(self, e, fn, reads=(), writes=(), sig=True):
        self._wait(e, self._deps(reads, writes))
        E = self.eng[e]
        ins = fn()
        if sig:
            E["count"] += 1
            ins.then_inc(self.sems[E["sem"]], 1)
            tok = (E["sem"], E["count"], e)
        else:
            tok = (E["sem"], E["count"] + 1, e)
        self._record(tok, reads, writes)

    def dma(self, q, out, in_, reads, writes, semkey):
        self._wait(q, self._deps(reads, writes))
        name = "d_" + semkey
        self._sem(name)
        self.dcount[name] = self.dcount.get(name, 0) + 16
        self.eng[q]["h"].dma_start(out=out, in_=in_).then_inc(self.sems[name], 16)
        self._record((name, self.dcount[name], "dma"), reads, writes)

    def finish(self, e):
        deps = list(self.res_w.values())
        for v in self.res_r.values():
            deps.extend(v)
        self._wait(e, deps)


def build_program(depth=DEPTH, nwave=NW, dbg=False):
    nc = bass.Bass("TRN2", target_bir_lowering=False)
    C = Ctx(nc)

    def din(name, shape, dt=F32):
        return nc.dram_tensor(name, shape, dt, kind="ExternalInput").ap()

    xT = din("xT", [128, 8, TOKC])
    w_in_d = din("w_in", [DEPTH, 128, 8, WINC])
    w_out_d = din("w_out", [DEPTH, 128, 8, D])
    w_uqA_d = din("w_uqA", [DEPTH, 128, 3, 576])
    w_uqB_d = din("w_uqB", [DEPTH, 128, 3, 576])
    w_uk_d = din("w_uk", [DEPTH, 128, 2, 384])
    w_uv_d = din("w_uv", [DEPTH, 128, 2, 384])
    w_rg_d = din("w_rg", [DEPTH, 128, 3, 128])
    w_ig_d = din("w_ig", [DEPTH, 128, 3, 128])
    w_pool_d = din("w_pool", [DEPTH, 128, 2, 128])
    params_d = din("params", [128, NPAR])
    cos_d = din("cosT", [32, TOKC])
    sin_d = din("sinT", [32, TOKC])
    cpar_d = din("cpar", [128, NCPAR])
    emask_d = din("emask", [128, 896], BF16)
    ident_d = din("ident", [128, 128], BF16)
    outT = nc.dram_tensor("outT", [128, 8, TOKC], F32, kind="ExternalOutput").ap()
    xs = [nc.dram_tensor(f"xs{i}", [128, 8, TOKC], F32).ap() for i in range(2)]
    send1 = [nc.dram_tensor(f"send1_{m}", [128, 64], F32).ap() for m in range(NW)]
    recv1 = [nc.dram_tensor(f"recv1_{m}", [NG * 128, 64], F32).ap() for m in range(NW)]
    send2 = [nc.dram_tensor(f"send2_{m}", [128, 8], F32).ap() for m in range(NW)]
    recv2 = [nc.dram_tensor(f"recv2_{m}", [NG * 128, 8], F32).ap() for m in range(NW)]
    send3 = [[[nc.dram_tensor(f"send3_{p}_{m}_{q}", [128, 2048], BF16).ap() for q in range(3)]
              for m in range(NW)] for p in range(2)]
    recv3 = [[[nc.dram_tensor(f"recv3_{p}_{m}_{q}", [NG * 128, 2048], BF16).ap() for q in range(3)]
              for m in range(NW)] for p in range(2)]

    def sb(name, shape, dt=F32):
        return nc.sbuf_tensor(name, shape, dt).__enter__()

    xt = sb("xt", [128, 8, TT])
    hT = sb("hT", [128, 8, TT], BF16)
    yg = sb("yg", [128, 8, TT], BF16)
    win = sb("win", [128, 8, WINC], BF16)
    wout = sb("wout", [128, 8, D], BF16)
    wuqA = sb("wuqA", [128, 3, 576], BF16)
    wuqB = sb("wuqB", [128, 3, 576], BF16)
    wuk = sb("wuk", [128, 2, 384], BF16)
    wuv = sb("wuv", [128, 2, 384], BF16)
    wrg = sb("wrg", [128, 3, 128], BF16)
    wig = sb("wig", [128, 3, 128], BF16)
    wpool = sb("wpool", [128, 2, 128], BF16)
    par = sb("par", [128, NPAR])
    emask = sb("emask_sb", [128, 896], BF16)
    ident = sb("ident_sb", [128, 128], BF16)
    Ig = sb("Ig", [128, NG, 128], BF16)
    ones_f = sb("ones_f", [128, 128])
    eps_t = sb("eps_t", [128, 1])
    sq = [sb(f"sq{i}", [128, TT]) for i in range(2)]
    rs = sb("rs", [128, TT])
    zapad = sb("zapad", [128, 3, TT + 3])
    zcpad = sb("zcpad", [128, 2, TT + 15])
    cqraw = sb("cqraw", [128, 3, TT])
    cqn = sb("cqn", [128, 3, TT], BF16)
    ckvraw = sb("ckvraw", [128, 2, TT])
    ckvn = sb("ckvn", [128, 2, TT], BF16)
    xa = sb("xa", [128, TT])
    xab = sb("xab", [128, TT], BF16)
    r_t = sb("r_t", [128, TT])
    i_t = sb("i_t", [128, TT])
    a_t = sb("a_t", [128, TT])
    a2_t = sb("a2_t", [128, TT])
    u_t = sb("u_t", [128, TT])
    hloc = sb("hloc", [128, 3, TT])
    Ab = sb("Ab", [128, 3, TT])
    zeros_t = sb("zeros_t", [128, TT])
    cpar = sb("cpar_sb", [128, NCPAR])
    s1 = sb("s1", [128, 64])
    H1 = sb("H1", [128, NG, 64])
    Hprev3 = sb("Hprev3", [128, 64])
    halo = sb("halo", [128, 64])
    s2 = sb("s2", [128, 8])
    C2 = sb("C2", [128, NG, 8])
    Sch = sb("Sch", [128, NG + 1, 3])
    carry = sb("carry", [128, 3])
    lsp = sb("lsp", [128, 12])
    nbias = sb("nbias", [128, 6])
    sA = sb("sA", [128, TT + 15])
    sB = sb("sB", [128, TT + 15])
    pooled = sb("pooled", [128, TT], BF16)
    tmp16 = sb("tmp16", [128, 16])
    QT = sb("QT", [128, 6, TT], BF16)
    cosb = sb("cosb", [128, TT])
    sinb = sb("sinb", [128, TT])
    t1, t2 = r_t, i_t
    kst = sb("kst", [128, 3, TT], BF16)
    krst = sb("krst", [128, TT], BF16)
    vst = sb("vst", [128, 2, 3, 4, 128], BF16)
    NR = 3
    kring = [sb(f"kring{i}", [128, 2 * TT], BF16) for i in range(NR)]
    vring = [sb(f"vring{i}", [128, 2 * TT], BF16) for i in range(NR)]
    NP = 4
    pring = [sb(f"pring{i}", [128, TT], BF16) for i in range(NP)]
    lsh, otmp = sq[0], sq[1]
    ps = [nc.psum_tensor(f"ps{i}", [128, TT], F32).__enter__() for i in range(8)]

    pe, act, dve, pool = nc.tensor, nc.scalar, nc.vector, nc.gpsimd
    state = dict(gen=0, sb=0, pr=0, ring=0, sqi=0)

    def gbank():
        while True:
            if state.get("attn"):
                i = 5 + state["gen"] % 3
            else:
                i = state["gen"] % 8
            state["gen"] += 1
            if i != state.get("reserved"):
                return ps[i], ("ps", i)

    def P_(name, l=None, width=1, idx=0):
        o = _P[name] + (0 if l is None else l * width) + idx
        return par[:, o:o + 1]

    C.dma("sp", par[:], params_d, [], ["par"], "par")
    C.dma("sp", emask[:], emask_d, [], ["emask"], "emask")
    C.dma("sp", cpar[:], cpar_d, [], ["cpar"], "cpar")
    C.dma("sp", ident[:], ident_d, [], ["ident"], "ident")
    for jp in range(NG):
        C.op("dve", lambda: dve.tensor_scalar(out=Ig[:, jp, :], in0=ident[:], scalar1=cpar[:, CP_G + jp:CP_G + jp + 1],
                                              scalar2=None, op0=ALU.mult), ["ident", "cpar"], ["Ig"])
    C.op("dve", lambda: dve.memset(zeros_t[:], 0.0), [], ["zeros_t"])
    C.op("dve", lambda: dve.memset(xab[:], 0.0), [], ["xab"])
    for p_ in range(2):
        for m_ in range(nwave):
            for q_ in range(3):
                for hf in range(2):
                    C.dma("sp", send3[p_][m_][q_][96:128, hf * 512:(hf + 1) * 512], xab[96:128, :], ["xab"],
                          [("s3z", p_, m_, q_, hf)], "xab")
    C.op("dve", lambda: dve.memset(s1[:], 0.0), [], ["s1"])
    C.op("dve", lambda: dve.memset(s2[:], 0.0), [], ["s2"])
    C.op("dve", lambda: dve.memset(ones_f[:], 1.0), [], ["ones_f"])
    C.op("dve", lambda: dve.memset(eps_t[:], EPS), [], ["eps_t"])
    C.op("dve", lambda: dve.memset(vst[:, 0, :, :, 64:128], 1.0), [], ["vst"])
    C.op("dve", lambda: dve.memset(vst[:, 1, :, :, 0:64], 1.0), [], ["vst"])

    def sumsq_rstd(srcs, nfeat, rkeys):
        bank, bkey = gbank()
        n = len(srcs)
        for i, s_ap in enumerate(srcs):
            q = sq[state["sqi"] % 2]
            qk = ("sq", state["sqi"] % 2)
            state["sqi"] += 1
            rk = rkeys[i] if (len(rkeys) == n and isinstance(rkeys[0], list)) else rkeys
            C.op("act", lambda: act.activation(out=q[:], in_=s_ap, func=AF.Square), rk, [qk])
            C.op("pe", lambda: pe.matmul(bank[:], lhsT=ones_f[:], rhs=q[:], start=(i == 0), stop=(i == n - 1)),
                 [qk, "ones_f"], [bkey], sig=(i == n - 1) or True)
        C.op("act", lambda: act.activation(out=rs[:], in_=bank[:], func=AF.Ln, scale=1.0 / nfeat,
                                           bias=eps_t[:, 0:1]), [bkey, "eps_t"], ["rs"])
        C.op("act", lambda: act.activation(out=rs[:], in_=rs[:], func=AF.Exp, scale=-0.5), ["rs"], ["rs"])

    def xnorm_chunk(l_, T_, kc, q_="sp"):
        tok_ = slice(T_ * TT, (T_ + 1) * TT)
        src_ = xT if l_ == 0 else xs[(l_ - 1) % 2]
        if kc == 0:
            bank, bkey = gbank()
            state["reserved"] = int(bkey[1])
            state["xn"] = (l_, T_, bank, bkey)
        _, _, bank, bkey = state["xn"]
        C.dma(q_, xt[:, kc, :], src_[:, kc, tok_], [("X", l_, T_, kc)], [("xt", kc)], f"xt{kc}")
        q = sq[state["sqi"] % 2]
        qk = ("sq", state["sqi"] % 2)
        state["sqi"] += 1
        C.op("act", lambda: act.activation(out=q[:], in_=xt[:, kc, :], func=AF.Square), [("xt", kc)], [qk])
        C.op("pe", lambda: pe.matmul(bank[:], lhsT=ones_f[:], rhs=q[:], start=(kc == 0), stop=(kc == 7)),
             [qk, "ones_f"], [bkey])

    def xnorm_finish():
        _, _, bank, bkey = state["xn"]
        C.op("act", lambda: act.activation(out=rs[:], in_=bank[:], func=AF.Ln, scale=1.0 / float(D),
                                           bias=eps_t[:, 0:1]), [bkey, "eps_t"], ["rs"])
        C.op("act", lambda: act.activation(out=rs[:], in_=rs[:], func=AF.Exp, scale=-0.5), ["rs"], ["rs"])
        state["reserved"] = None
        state["xn"] = None

    for l in range(depth):
        for kc in range(8):
            C.dma("pool", win[:, kc, :], w_in_d[l, :, kc, :], [], [("win", kc)], "win")
        for kc in range(8):
            C.dma("pool", wout[:, kc, :], w_out_d[l, :, kc, :], [], [("wout", kc)], "wout")
        for (t_sb, t_d, key) in [(wuqA, w_uqA_d, "wuqA"), (wuqB, w_uqB_d, "wuqB"), (wuk, w_uk_d, "wuk"),
                                 (wuv, w_uv_d, "wuv"), (wrg, w_rg_d, "wrg"), (wig, w_ig_d, "wig"),
                                 (wpool, w_pool_d, "wpool")]:
            C.dma("pool", t_sb[:], t_d[l], [], [key], key)
        C.op("act", lambda: act.activation(out=lsp[:, 0:3], in_=par[:, _P["lam"] + 3 * l:_P["lam"] + 3 * l + 3],
                                           func=AF.Exp, scale=-1.0), ["par"], ["lsp"])
        C.op("act", lambda: act.activation(out=lsp[:, 0:3], in_=lsp[:, 0:3], func=AF.Ln, bias=1.0),
             ["lsp"], ["lsp"])
        C.op("dve", lambda: dve.tensor_scalar(out=lsp[:, 3:6], in0=lsp[:, 0:3], scalar1=-8.0, scalar2=None,
                                              op0=ALU.mult), ["lsp"], ["lsp"])
        C.op("dve", lambda: dve.tensor_scalar(out=lsp[:, 6:9], in0=lsp[:, 0:3], scalar1=-16.0, scalar2=None,
                                              op0=ALU.mult), ["lsp"], ["lsp"])
        C.op("dve", lambda: dve.tensor_scalar(out=nbias[:, 0:3], in0=par[:, _P["brg"] + 3 * l:_P["brg"] + 3 * l + 3],
                                              scalar1=-1.0, scalar2=None, op0=ALU.mult), ["par", "nbias"], ["nbias"])
        C.op("dve", lambda: dve.tensor_scalar(out=nbias[:, 3:6], in0=par[:, _P["big"] + 3 * l:_P["big"] + 3 * l + 3],
                                              scalar1=-1.0, scalar2=None, op0=ALU.mult), ["par", "nbias"], ["nbias"])

        C.op("dve", lambda: dve.memset(Hprev3[:], 0.0), ["Hprev3"], ["Hprev3"])
        C.op("dve", lambda: dve.memset(Sch[:, 0, :], 0.0), ["Sch"], ["Sch"])
        for T in range(nwave):
            tok = slice(T * TT, (T + 1) * TT)
            pty = l % 2
            if not (state.get("xn") and state["xn"][0] == l and state["xn"][1] == T):
                for kc in range(8):
                    xnorm_chunk(l, T, kc)
            C.dma("sp", cosb[64:96, :], cos_d[:, tok], [], ["cosb"], "cosb")
            C.dma("sp", sinb[64:96, :], sin_d[:, tok], [], ["sinb"], "sinb")
            xnorm_finish()
            for kc in range(8):
                C.op("dve", lambda: dve.scalar_tensor_tensor(out=hT[:, kc, :], in0=xt[:, kc, :],
                                                             scalar=P_("normg", l, 8, kc), in1=rs[:],
                                                             op0=ALU.mult, op1=ALU.mult),
                     [("xt", kc), "rs", "par"], [("hT", kc)])
            def inproj(col0, M):
                bank, bkey = gbank()
                for kc in range(8):
                    C.op("pe", lambda: pe.matmul(bank[0:M, :], lhsT=win[:, kc, col0:col0 + M], rhs=hT[:, kc, :],
                                                 start=(kc == 0), stop=(kc == 7)),
                         [("win", kc), ("hT", kc)], [bkey], sig=(kc == 7))
                return bank, bkey

            for c in range(3):
                bank, bkey = inproj(O_ZA + 128 * c, 128)
                C.op("act", lambda: act.activation(out=zapad[:, c, 3:TT + 3], in_=bank[:], func=AF.Copy),
                     [bkey], ["zapad"])
            for c in range(2):
                bank, bkey = inproj(O_ZC + 128 * c, 128)
                C.op("act", lambda: act.activation(out=zcpad[:, c, 15:TT + 15], in_=bank[:], func=AF.Copy),
                     [bkey], ["zcpad"])
            C.op("dve", lambda: dve.tensor_copy(out=s1[:, 0:9].rearrange("p (c k) -> p c k", k=3),
                                                in_=zapad[:, :, TT:TT + 3]), ["zapad"], ["s1"])
            C.op("dve", lambda: dve.tensor_copy(out=s1[:, 9:39].rearrange("p (c k) -> p c k", k=15),
                                                in_=zcpad[:, :, TT:TT + 15]), ["zcpad"], ["s1"])
            C.dma("sp", send1[T], s1[:], ["s1"], [("send1", T)], "s1")
            C.op("pool", lambda: pool.collective_compute("AllGather", ALU.bypass, replica_groups=GROUPS,
                                                         ins=[send1[T]], outs=[recv1[T]]),
                 [("send1", T)], [("recv1", T)])
            C.dma("pool", H1[:], recv1[T].rearrange("(r p) n -> p r n", p=128), [("recv1", T)], ["H1"], "H1")

            for c in range(2):
                bank, bkey = inproj(O_CKV + 128 * c, 128)
                C.op("act", lambda: act.activation(out=ckvraw[:, c, :], in_=bank[:], func=AF.Copy),
                     [bkey], ["ckvraw"])
            bankA, kA = inproj(O_KR - 64, 96)
            bankB, kB = inproj(O_KRS - 64, 96)
            C.op("dve", lambda: dve.tensor_tensor(out=t1[64:96, :], in0=bankA[64:96, :], in1=cosb[64:96, :], op=ALU.mult),
                 [kA, "cosb"], ["r_t"])
            C.op("dve", lambda: dve.tensor_tensor(out=t2[64:96, :], in0=bankB[64:96, :], in1=sinb[64:96, :], op=ALU.mult),
                 [kB, "sinb"], ["i_t"])
            C.op("dve", lambda: dve.tensor_tensor(out=krst[64:96, :], in0=t1[64:96, :], in1=t2[64:96, :], op=ALU.add),
                 ["r_t", "i_t"], ["krst"])
            for h in range(6):
                C.dma("sp", send3[pty][T][h // 2][64:96, (h % 2) * 512:(h % 2) * 512 + 512], krst[64:96, :], ["krst"],
                      [("s3", h // 2, "r", h % 2)], "krst")
            sumsq_rstd([ckvraw[:, c, :] for c in range(2)], 256.0, ["ckvraw"])
            for c in range(2):
                C.op("dve", lambda: dve.scalar_tensor_tensor(out=ckvn[:, c, :], in0=ckvraw[:, c, :],
                                                             scalar=P_("kvng", l, 2, c), in1=rs[:],
                                                             op0=ALU.mult, op1=ALU.mult),
                     ["ckvraw", "rs", "par"], ["ckvn"])
            for c in range(3):
                bank, bkey = inproj(O_CQ + 128 * c, 128)
                C.op("act", lambda: act.activation(out=cqraw[:, c, :], in_=bank[:], func=AF.Copy),
                     [bkey], ["cqraw"])
            for p in range(3):
                bank, bkey = gbank()
                for c in range(2):
                    C.op("pe", lambda: pe.matmul(bank[:], lhsT=wuk[:, c, p * 128:(p + 1) * 128], rhs=ckvn[:, c, :],
                                                 start=(c == 0), stop=(c == 1)), ["wuk", "ckvn"], [bkey], sig=(c == 1))
                C.op("act", lambda: act.activation(out=kst[:, p, :], in_=bank[:], func=AF.Copy), [bkey], ["kst"])
            for p in range(3):
                C.dma("sp", send3[pty][T][p][0:64, 0:512], kst[0:64, p, :], ["kst"], [("s3", p, "n", 0)], "kst")
                C.dma("sp", send3[pty][T][p][0:64, 512:1024], kst[64:128, p, :], ["kst"], [("s3", p, "n", 1)], "kst")
            for blk in range(4):
                bank, bkey = gbank()
                for c in range(2):
                    C.op("pe", lambda: pe.matmul(bank[:, 0:384], lhsT=ckvn[:, c, blk * 128:(blk + 1) * 128], rhs=wuv[:, c, :],
                                                 start=(c == 0), stop=(c == 1)), ["wuv", "ckvn"], [bkey], sig=(c == 1))
                C.op("act", lambda: act.activation(out=vst[:, 0, :, blk, 0:64],
                                                   in_=bank[:, 0:192].rearrange("p (a d) -> p a d", d=64), func=AF.Copy),
                     [bkey], ["vst"])
                C.op("act", lambda: act.activation(out=vst[:, 1, :, blk, 64:128],
                                                   in_=bank[:, 192:384].rearrange("p (a d) -> p a d", d=64), func=AF.Copy),
                     [bkey], ["vst"])
            for q in range(3):
                C.dma("sp", send3[pty][T][q][:, 1024:2048].rearrange("p (a c) -> p a c", a=2),
                      vst[:, :, q, :, :].rearrange("p a k d -> p a (k d)"), ["vst"], [("s3", q, "v")], "vst")
            for q in range(3):
                s3keys = [("s3", q, "r", 0), ("s3", q, "r", 1), ("s3", q, "n", 0), ("s3", q, "n", 1), ("s3", q, "v"),
                          ("s3z", pty, T, q, 0), ("s3z", pty, T, q, 1)]
                C.op("pool", lambda: pool.collective_compute("AllGather", ALU.bypass, replica_groups=GROUPS,
                                                             ins=[send3[pty][T][q]], outs=[recv3[pty][T][q]]),
                     s3keys, [("recv3", T, q)])

            sumsq_rstd([cqraw[:, c, :] for c in range(3)], 384.0, ["cqraw"])
            for c in range(3):
                C.op("dve", lambda: dve.scalar_tensor_tensor(out=cqn[:, c, :], in0=cqraw[:, c, :],
                                                             scalar=P_("qng", l, 3, c), in1=rs[:],
                                                             op0=ALU.mult, op1=ALU.mult),
                     ["cqraw", "rs", "par"], ["cqn"])
            for h in range(6):
                bankA, kA = gbank()
                for c in range(3):
                    C.op("pe", lambda: pe.matmul(bankA[0:96, :], lhsT=wuqA[:, c, h * 96:(h + 1) * 96], rhs=cqn[:, c, :],
                                                 start=(c == 0), stop=(c == 2)), ["wuqA", "cqn"], [kA], sig=(c == 2))
                bankB, kB = gbank()
                for c in range(3):
                    C.op("pe", lambda: pe.matmul(bankB[0:96, :], lhsT=wuqB[:, c, h * 96:(h + 1) * 96], rhs=cqn[:, c, :],
                                                 start=(c == 0), stop=(c == 2)), ["wuqB", "cqn"], [kB], sig=(c == 2))
                C.op("act", lambda: act.activation(out=QT[0:64, h, :], in_=bankA[0:64, :], func=AF.Copy),
                     [kA], [("QT", h)])
                C.op("dve", lambda: dve.tensor_tensor(out=t1[64:96, :], in0=bankA[64:96, :], in1=cosb[64:96, :], op=ALU.mult),
                     [kA, "cosb"], ["r_t"])
                C.op("dve", lambda: dve.tensor_tensor(out=t2[64:96, :], in0=bankB[64:96, :], in1=sinb[64:96, :], op=ALU.mult),
                     [kB, "sinb"], ["i_t"])
                C.op("dve", lambda: dve.tensor_tensor(out=QT[64:96, h, :], in0=t1[64:96, :], in1=t2[64:96, :], op=ALU.add),
                     ["r_t", "i_t"], [("QT", h)])
            for (goff, ych, n) in [(O_GA, 0, 3), (O_GB, 3, 3), (O_GC, 6, 2)]:
                for c in range(n):
                    bank, bkey = inproj(goff + 128 * c, 128)
                    C.op("act", lambda: act.activation(out=yg[:, ych + c, :], in_=bank[:], func=AF.Silu),
                         [bkey], [("yg", ych + c)])


            def side_gen():
                C.op("dve", lambda: dve.tensor_scalar(out=halo[:], in0=Hprev3[:], scalar1=cpar[:, CP_W + 3:CP_W + 4],
                                                      scalar2=None, op0=ALU.mult), ["Hprev3", "cpar"], ["halo"])
                yield
                for k in range(3):
                    C.op("dve", lambda: dve.scalar_tensor_tensor(out=halo[:], in0=H1[:, k, :],
                                                                 scalar=cpar[:, CP_W + k:CP_W + k + 1], in1=halo[:],
                                                                 op0=ALU.mult, op1=ALU.add), ["H1", "cpar", "halo"], ["halo"])
                    yield
                C.op("dve", lambda: dve.tensor_copy(out=Hprev3[:], in_=H1[:, 3, :]), ["H1", "halo"], ["Hprev3"])
                yield
                C.op("dve", lambda: dve.tensor_copy(out=zapad[:, :, 0:3], in_=halo[:, 0:9].rearrange("p (c k) -> p c k", k=3)),
                     ["halo", "s1"], ["zapad"])
                yield
                C.op("dve", lambda: dve.tensor_copy(out=zcpad[:, :, 0:15], in_=halo[:, 9:39].rearrange("p (c k) -> p c k", k=15)),
                     ["halo", "s1"], ["zcpad"])
                yield

                for c in range(3):
                    cw = _P["convw"] + l * 12 + c * 4
                    C.op("dve", lambda: dve.tensor_scalar(out=xa[:], in0=zapad[:, c, 0:TT], scalar1=par[:, cw:cw + 1],
                                                          scalar2=P_("convb", l, 3, c), op0=ALU.mult, op1=ALU.add),
                         ["zapad", "par"], ["xa"])
                    yield
                    for k in range(1, 4):
                        C.op("dve", lambda: dve.scalar_tensor_tensor(out=xa[:], in0=zapad[:, c, k:k + TT],
                                                                     scalar=par[:, cw + k:cw + k + 1], in1=xa[:],
                                                                     op0=ALU.mult, op1=ALU.add),
                             ["zapad", "par", "xa"], ["xa"])
                        yield
                    C.op("dve", lambda: dve.tensor_copy(out=xab[:], in_=xa[:]), ["xa"], ["xab"])
                    yield
                    bank_r, kr_ = gbank()
                    C.op("pe", lambda: pe.matmul(bank_r[:], lhsT=wrg[:, c, :], rhs=xab[:], start=True, stop=True),
                         ["wrg", "xab"], [kr_])
                    yield
                    bank_i, ki_ = gbank()
                    C.op("pe", lambda: pe.matmul(bank_i[:], lhsT=wig[:, c, :], rhs=xab[:], start=True, stop=True),
                         ["wig", "xab"], [ki_])
                    yield
                    for (dst, bnk, bk, col) in ((r_t, bank_r, kr_, c), (i_t, bank_i, ki_, 3 + c)):
                        dk = "r_t" if dst is r_t else "i_t"
                        C.op("act", lambda: act.activation(out=dst[:], in_=bnk[:], func=AF.Exp, scale=-1.0,
                                                           bias=nbias[:, col:col + 1]), [bk, "nbias"], [dk])
                        yield
                        C.op("act", lambda: act.activation(out=dst[:], in_=dst[:], func=AF.Ln, bias=1.0), [dk], [dk])
                        yield
                        C.op("act", lambda: act.activation(out=dst[:], in_=dst[:], func=AF.Exp, scale=-1.0), [dk], [dk])
                        yield
                    C.op("act", lambda: act.activation(out=a_t[:], in_=r_t[:], func=AF.Exp, scale=lsp[:, 3 + c:4 + c]),
                         ["r_t", "lsp"], ["a_t"])
                    yield
                    C.op("act", lambda: act.activation(out=a2_t[:], in_=r_t[:], func=AF.Exp, scale=lsp[:, 6 + c:7 + c]),
                         ["r_t", "lsp"], ["a2_t"])
                    yield
                    C.op("act", lambda: act.activation(out=a2_t[:], in_=a2_t[:], func=AF.Ln, scale=-1.0, bias=1.0),
                         ["a2_t"], ["a2_t"])
                    yield
                    C.op("act", lambda: act.activation(out=a2_t[:], in_=a2_t[:], func=AF.Exp, scale=0.5),
                         ["a2_t"], ["a2_t"])
                    yield
                    C.op("dve", lambda: dve.tensor_tensor(out=u_t[:], in0=i_t[:], in1=xa[:], op=ALU.mult),
                         ["i_t", "xa"], ["u_t"])
                    yield
                    C.op("dve", lambda: dve.tensor_tensor(out=u_t[:], in0=u_t[:], in1=a2_t[:], op=ALU.mult),
                         ["u_t", "a2_t"], ["u_t"])
                    yield
                    C.op("dve", lambda: dve.tensor_tensor_scan(out=hloc[:, c, :], data0=a_t[:], data1=u_t[:], initial=0.0,
                                                               op0=ALU.mult, op1=ALU.add),
                         ["a_t", "u_t"], [("hloc", c)])
                    yield
                    C.op("dve", lambda: dve.tensor_tensor_scan(out=Ab[:, c, :], data0=a_t[:], data1=zeros_t[:], initial=1.0,
                                                               op0=ALU.mult, op1=ALU.add),
                         ["a_t", "zeros_t"], [("Ab", c)])
                    yield
                    C.op("dve", lambda: dve.tensor_copy(out=s2[:, c:c + 1], in_=hloc[:, c, TT - 1:TT]), [("hloc", c)], ["s2"])
                    yield
                    C.op("dve", lambda: dve.tensor_copy(out=s2[:, 3 + c:4 + c], in_=Ab[:, c, TT - 1:TT]), [("Ab", c)], ["s2"])
                    yield

                C.dma("pool", send2[T], s2[:], ["s2"], [("send2", T)], "s2")
                yield
                C.op("pool", lambda: pool.collective_compute("AllGather", ALU.bypass, replica_groups=GROUPS,
                                                             ins=[send2[T]], outs=[recv2[T]]),
                     [("send2", T)], [("recv2", T)])
                yield
                C.dma("pool", C2[:], recv2[T].rearrange("(r p) n -> p r n", p=128), [("recv2", T)], ["C2"], "C2")
                yield
                for c in range(2):
                    z = zcpad[:, c, :]
                    W = TT + 15
                    C.op("dve", lambda: dve.tensor_tensor(out=sA[:, 1:W], in0=z[:, 1:W], in1=z[:, 0:W - 1], op=ALU.add),
                         ["zcpad"], ["sA"])
                    yield
                    C.op("dve", lambda: dve.tensor_tensor(out=sB[:, 3:W], in0=sA[:, 3:W], in1=sA[:, 1:W - 2], op=ALU.add),
                         ["sA"], ["sB"])
                    yield
                    if c == 0:
                        lo, hi = sA, sB
                    else:
                        C.op("dve", lambda: dve.tensor_tensor(out=sA[:, 7:W], in0=sB[:, 7:W], in1=sB[:, 3:W - 4], op=ALU.add),
                             ["sB"], ["sA"])
                        yield
                        C.op("dve", lambda: dve.tensor_tensor(out=sB[:, 15:W], in0=sA[:, 15:W], in1=sA[:, 7:W - 8], op=ALU.add),
                             ["sA"], ["sB"])
                        yield
                        lo, hi = sA, sB
                    iw = _P["invw"] + c
                    for (p0, p1, stg) in [(0, 64, lo), (64, 128, hi)]:
                        C.op("dve", lambda: dve.scalar_tensor_tensor(out=pooled[p0:p1, :], in0=stg[p0:p1, 15:W],
                                                                     scalar=par[p0:p1, iw:iw + 1], in1=z[p0:p1, 15:W],
                                                                     op0=ALU.mult, op1=ALU.subtract),
                             ["sA", "sB", "zcpad", "par"], ["pooled"])
                        yield
                        if T == 0:
                            it = CP_IT + 16 * c
                            C.op("dve", lambda: dve.tensor_tensor(out=tmp16[p0:p1, :], in0=stg[p0:p1, 15:31],
                                                                  in1=cpar[p0:p1, it:it + 16], op=ALU.mult),
                                 ["sA", "sB", "cpar"], ["tmp16"])
                            yield
                            C.op("dve", lambda: dve.tensor_tensor(out=pooled[p0:p1, 0:16], in0=tmp16[p0:p1, :],
                                                                  in1=z[p0:p1, 15:31], op=ALU.subtract),
                                 ["tmp16", "zcpad"], ["pooled"])
                            yield
                    bank, bkey = gbank()
                    C.op("pe", lambda: pe.matmul(bank[:], lhsT=wpool[:, c, :], rhs=pooled[:], start=True, stop=True),
                         ["wpool", "pooled"], [bkey])
                    yield
                    C.op("dve", lambda: dve.scalar_tensor_tensor(out=yg[:, 6 + c, :], in0=bank[:],
                                                                 scalar=P_("pscale", l, 2, c), in1=yg[:, 6 + c, :],
                                                                 op0=ALU.mult, op1=ALU.mult),
                         [bkey, "par", ("yg", 6 + c)], [("yg", 6 + c)])
                    yield


            steps = []
            for ph, mps in ((1, list(range(T))), (2, [T])):
                for q in range(3):
                    kts = [(mp, jp) for mp in mps for jp in range(NG)]
                    for n_, kt in enumerate(kts):
                        for hh in range(2):
                            for kb in range(4):
                                steps.append((2 * q + hh, kt, kb, n_ == 0 and kb == 0, n_ == len(kts) - 1 and kb == 3, ph))
            part = [cqraw[:, 0, :], cqraw[:, 1, :], cqraw[:, 2, :], ckvraw[:, 0, :], ckvraw[:, 1, :], rs[:]]
            pkey = ["cqraw", "cqraw", "cqraw", "ckvraw", "ckvraw", "rs"]
            LA = 2
            info = {}

            def emit_qk(i):
                h, kt, kb, first, last, ph = steps[i]
                mp, jp = kt
                par_, pair = h % 2, h // 2
                if kb == 0 and par_ == 0:
                    slot = state["ring"] % NR
                    state["ring"] += 1
                    rkey = ("ring", slot)
                    src = recv3[pty][mp][pair]
                    C.dma("sp", kring[slot][0:96, :], src[jp * 128:jp * 128 + 96, 0:1024],
                          [("recv3", mp, pair)], [rkey], f"ring{slot}")
                    C.dma("sp", vring[slot][:], src[jp * 128:(jp + 1) * 128, 1024:2048],
                          [("recv3", mp, pair)], [rkey], f"ring{slot}")
                    info[(pair, kt, ph)] = slot
                slot = info[(h // 2, kt, ph)]
                rkey = ("ring", slot)
                si = 2 + state["sb"] % 3
                state["sb"] += 1
                pi = state["pr"] % NP
                state["pr"] += 1
                info[i] = (si, pi)
                if mp == T:
                    C.op("pe", lambda: pe.matmul(ps[si][:], lhsT=kring[slot][0:96, par_ * 512 + kb * 128:par_ * 512 + (kb + 1) * 128],
                                                 rhs=QT[0:96, h, :], start=True, stop=False),
                         [rkey, ("QT", h)], [("ps", si)], sig=False)
                    C.op("pe", lambda: pe.matmul(ps[si][:], lhsT=Ig[:, jp, :], rhs=emask[:, 384 - 128 * kb:896 - 128 * kb],
                                                 start=False, stop=True),
                         ["Ig", "emask"], [("ps", si)])
                    C.op("act", lambda: act.activation(out=pring[pi][:], in_=ps[si][:], func=AF.Exp, scale=SCALE,
                                                       bias=cpar[:, CP_EB + jp:CP_EB + jp + 1]),
                         [("ps", si), "cpar"], [("P", pi)])
                else:
                    C.op("pe", lambda: pe.matmul(ps[si][:], lhsT=kring[slot][0:96, par_ * 512 + kb * 128:par_ * 512 + (kb + 1) * 128],
                                                 rhs=QT[0:96, h, :], start=True, stop=True),
                         [rkey, ("QT", h)], [("ps", si)])
                    C.op("act", lambda: act.activation(out=pring[pi][:], in_=ps[si][:], func=AF.Exp, scale=SCALE),
                         [("ps", si)], [("P", pi)])

            def emit_pv(i):
                h, kt, kb, first, last, ph = steps[i]
                par_, pair = h % 2, h // 2
                slot = info[(h // 2, kt, ph)]
                rkey = ("ring", slot)
                si, pi = info[i]
                ob = ps[h % 2]
                okey = ("ps", h % 2)
                C.op("pe", lambda: pe.matmul(ob[:], lhsT=vring[slot][:, par_ * 512 + kb * 128:par_ * 512 + (kb + 1) * 128], rhs=pring[pi][:],
                                             start=first, stop=last),
                     [rkey, ("P", pi)], [okey])
                if last and ph == 1:
                    C.op("dve", lambda: dve.tensor_copy(out=part[h], in_=ob[:]), [okey, pkey[h]], [pkey[h]])
                if last and ph == 2:
                    if T > 0:
                        C.op("dve", lambda: dve.tensor_tensor(out=part[h], in0=ob[:], in1=part[h], op=ALU.add),
                             [okey, pkey[h]], [pkey[h]])
                        src, skey = part[h], pkey[h]
                    else:
                        src, skey = ob, okey
                    if par_ == 0:
                        o0, o1, l0, l1 = 0, 64, 64, 128
                    else:
                        o0, o1, l0, l1 = 64, 128, 0, 64
                    C.op("dve", lambda: dve.tensor_copy(out=lsh[o0:o1, :], in_=src[l0:l1, :]), [skey], [("sq", 0)])
                    C.op("dve", lambda: dve.reciprocal(out=lsh[o0:o1, :], in_=lsh[o0:o1, :]), [("sq", 0)], [("sq", 0)])
                    C.op("dve", lambda: dve.tensor_tensor(out=otmp[o0:o1, :], in0=src[o0:o1, :], in1=lsh[o0:o1, :],
                                                          op=ALU.mult), [skey, ("sq", 0)], [("sq", 1)])
                    C.op("dve", lambda: dve.tensor_tensor(out=yg[o0:o1, 3 + pair, :], in0=otmp[o0:o1, :],
                                                          in1=yg[o0:o1, 3 + pair, :], op=ALU.mult),
                         [("sq", 1), ("yg", 3 + pair)], [("yg", 3 + pair)])

            nst = len(steps)
            state["attn"] = True
            sg = side_gen()
            for i in range(nst + LA):
                if i < nst:
                    emit_qk(i)
                for _ in range(3):
                    next(sg, None)
                if i >= LA:
                    emit_pv(i - LA)
            for _ in sg:
                pass
            state["attn"] = False

            for k in range(NG):
                C.op("dve", lambda: dve.tensor_tensor(out=Sch[:, k + 1, :], in0=C2[:, k, 3:6], in1=Sch[:, k, :], op=ALU.mult),
                     ["C2", "Sch"], ["Sch"])
                C.op("dve", lambda: dve.tensor_tensor(out=Sch[:, k + 1, :], in0=Sch[:, k + 1, :], in1=C2[:, k, 0:3], op=ALU.add),
                     ["C2", "Sch"], ["Sch"])
            C.op("dve", lambda: dve.tensor_scalar(out=carry[:], in0=Sch[:, 0, :], scalar1=cpar[:, CP_W + 3:CP_W + 4],
                                                  scalar2=None, op0=ALU.mult), ["Sch", "cpar"], ["carry"])
            for k in range(3):
                C.op("dve", lambda: dve.scalar_tensor_tensor(out=carry[:], in0=Sch[:, k + 1, :],
                                                             scalar=cpar[:, CP_W + k:CP_W + k + 1], in1=carry[:],
                                                             op0=ALU.mult, op1=ALU.add), ["Sch", "cpar", "carry"], ["carry"])
            C.op("dve", lambda: dve.tensor_copy(out=Sch[:, 0, :], in_=Sch[:, NG, :]), ["Sch", "carry"], ["Sch"])
            for c in range(3):
                C.op("dve", lambda: dve.scalar_tensor_tensor(out=hloc[:, c, :], in0=Ab[:, c, :], scalar=carry[:, c:c + 1],
                                                             in1=hloc[:, c, :], op0=ALU.mult, op1=ALU.add),
                     [("Ab", c), ("hloc", c), "carry"], [("hloc", c)])
                C.op("dve", lambda: dve.tensor_tensor(out=yg[:, c, :], in0=hloc[:, c, :], in1=yg[:, c, :], op=ALU.mult),
                     [("hloc", c), ("yg", c)], [("yg", c)])

            for oc in range(8):
                bank, bkey = gbank()
                for kc in range(8):
                    C.op("pe", lambda: pe.matmul(bank[:], lhsT=wout[:, kc, oc * 128:(oc + 1) * 128], rhs=yg[:, kc, :],
                                                 start=(kc == 0), stop=(kc == 7)),
                         [("wout", kc), ("yg", kc)], [bkey], sig=(kc == 7))
                C.op("dve", lambda: dve.tensor_tensor(out=xt[:, oc, :], in0=xt[:, oc, :], in1=bank[:], op=ALU.add),
                     [("xt", oc), bkey], [("xt", oc)])
                if l < depth - 1:
                    C.dma("sp", xs[l % 2][:, oc, tok], xt[:, oc, :], [("xt", oc)], [("X", l + 1, T, oc)], f"xt{oc}")
                    nl, nT = (l, T + 1) if T + 1 < nwave else (l + 1, 0)
                    xnorm_chunk(nl, nT, oc, "pool")
            if l == depth - 1:
                sumsq_rstd([xt[:, kc, :] for kc in range(8)], float(D), [[("xt", kc)] for kc in range(8)])
                for kc in range(8):
                    C.op("dve", lambda: dve.scalar_tensor_tensor(out=xt[:, kc, :], in0=xt[:, kc, :],
                                                                 scalar=P_("fng", None, 1, kc), in1=rs[:],
                                                                 op0=ALU.mult, op1=ALU.mult),
                         [("xt", kc), "rs", "par"], [("xt", kc)])
                    C.dma("sp", outT[:, kc, tok], xt[:, kc, :], [("xt", kc)], [("OUT", T, kc)], f"xt{kc}")
    C.finish("sp")
    return nc


def _prep_shared(inp):
    f = np.float32
    w_in = np.asarray(inp["w_in"], f)
    offs = np.cumsum([0, 384, 384, 384, 256, 32, 384, 256, 256])
    za, ga, cq, ckv, kr, gb, zc, gc = [np.arange(offs[i], offs[i + 1]) for i in range(8)]
    krs = np.concatenate([kr[16:32], kr[0:16]])
    perm = np.concatenate([za, ga, cq, ckv, gb, zc, gc, kr, krs])
    assert perm.size == WINC

    def kmaj(w, nk):
        L, _, N = w.shape
        return np.ascontiguousarray(w.reshape(L, nk, 128, N).transpose(0, 2, 1, 3))

    d = {}
    d["w_in"] = kmaj(w_in[:, :, perm], 8)
    d["w_out"] = kmaj(np.asarray(inp["w_out"], f), 8)
    w_uq = np.asarray(inp["w_uq"], f).reshape(DEPTH, 384, 6, 96)
    nope, rope = w_uq[..., :64], w_uq[..., 64:]
    A = np.concatenate([nope, rope], axis=-1).reshape(DEPTH, 384, 576)
    Bm = np.concatenate([nope, rope[..., 16:], rope[..., :16]], axis=-1).reshape(DEPTH, 384, 576)
    d["w_uqA"] = kmaj(A, 3)
    d["w_uqB"] = kmaj(Bm, 3)
    w_ukv = np.asarray(inp["w_ukv"], f).reshape(DEPTH, 256, 6, 128)
    d["w_uk"] = kmaj(np.ascontiguousarray(w_ukv[..., :64]).reshape(DEPTH, 256, 384), 2)
    v = w_ukv[..., 64:]
    v = np.concatenate([v[:, :, 0::2, :], v[:, :, 1::2, :]], axis=2).reshape(DEPTH, 256, 384)
    d["w_uv"] = kmaj(v, 2)

    def bdiag(w, n):
        L = w.shape[0]
        o = np.zeros((L, 128, n, 128), f)
        for c in range(n):
            o[:, 0:64, c, 0:64] = w[:, 2 * c]
            o[:, 64:128, c, 64:128] = w[:, 2 * c + 1]
        return o

    d["w_rg"] = bdiag(np.asarray(inp["w_rg"], f), 3)
    d["w_ig"] = bdiag(np.asarray(inp["w_ig"], f), 3)
    d["w_pool"] = bdiag(np.asarray(inp["w_pool"], f), 2)

    par = np.zeros((128, NPAR), f)

    def put(name, arr):
        a = np.asarray(arr, f)
        lead = a.shape[:-1]
        nch = a.shape[-1] // 128
        a = a.reshape(lead + (nch, 128))
        a = np.moveaxis(a, -1, 0).reshape(128, -1)
        par[:, _P[name]:_P[name] + a.shape[1]] = a

    put("normg", inp["norm_g"])
    put("fng", inp["final_norm_g"])
    cw = np.asarray(inp["conv_w"], f)
    cw = cw.reshape(DEPTH, 4, 3, 128).transpose(3, 0, 2, 1).reshape(128, DEPTH * 12)
    par[:, _P["convw"]:_P["convw"] + DEPTH * 12] = cw
    put("convb", inp["conv_b"])
    put("brg", inp["b_rg"])
    put("big", inp["b_ig"])
    put("lam", inp["lru_lambda"])
    put("qng", inp["q_norm_g"])
    put("kvng", inp["kv_norm_g"])
    put("pscale", inp["pool_scale"])
    wins = np.array([[2, 4], [8, 16]], f)
    for c in range(2):
        for hf in range(2):
            p0 = 64 * hf
            par[p0:p0 + 64, _P["invw"] + c] = f(1.0) / wins[c, hf]
    d["params"] = par
    cc = np.arange(896)[None, :] - 384
    d["emask"] = np.where(np.arange(128)[:, None] <= cc, 0.0, -30000.0).astype(ml_dtypes.bfloat16)
    d["ident"] = np.eye(128, dtype=np.float32).astype(ml_dtypes.bfloat16)
    return d


def _prep_core(j):
    f = np.float32
    d = {}
    pos = np.concatenate([np.arange((NG * m + j) * TT, (NG * m + j + 1) * TT) for m in range(NW)]).astype(f)
    inv_freq = (f(10000.0) ** (-np.arange(0, 32, 2, dtype=f) / f(32))).astype(f)
    ang = (pos[None, :] * inv_freq[:, None]).astype(f)
    cs, sn = np.cos(ang).astype(f), np.sin(ang).astype(f)
    d["cosT"] = np.ascontiguousarray(np.concatenate([cs, cs], 0))
    d["sinT"] = np.ascontiguousarray(np.concatenate([-sn, sn], 0))
    cp = np.zeros((128, NCPAR), f)
    cp[:, CP_W + (j - 1) % NG] = 1.0
    for jp in range(NG):
        cp[:, CP_EB + jp] = 0.0 if jp <= j else -30000.0
        cp[:, CP_F + jp] = 1.0 if jp < j else 0.0
        cp[:, CP_G + jp] = 1.0 if jp == j else 0.0
    wins = np.array([[2, 4], [8, 16]], f)
    t = np.arange(16, dtype=f)
    for c in range(2):
        for hf in range(2):
            p0 = 64 * hf
            if j == 0:
                cp[p0:p0 + 64, CP_IT + 16 * c:CP_IT + 16 * c + 16] = f(1.0) / np.minimum(t + 1, wins[c, hf])
            else:
                cp[p0:p0 + 64, CP_IT + 16 * c:CP_IT + 16 * c + 16] = f(1.0) / wins[c, hf]
    d["cpar"] = cp
    return d


_CACHE = {}


def kernel(**inputs):
    x = np.asarray(inputs["x"], np.float32)
    shared = _prep_shared(inputs)
    if "nc" not in _CACHE:
        _CACHE["nc"] = build_program()
    nc = _CACHE["nc"]
    in_maps = []
    for c in range(NCORE):
        b, j = divmod(c, NG)
        m = dict(shared)
        m.update(_prep_core(j))
        xb = x[b].reshape(NW, NG, TT, 8, 128)[:, j]
        m["xT"] = np.ascontiguousarray(xb.transpose(3, 2, 0, 1).reshape(128, 8, TOKC))
        in_maps.append(m)
    res = run_bass_kernel_spmd(nc, in_maps, core_ids=list(range(NCORE)))
    out = np.empty((B, NW, NG, TT, D), np.float32)
    for c in range(NCORE):
        b, j = divmod(c, NG)
        o = np.asarray(res.results[c]["outT"], np.float32).reshape(128, 8, NW, TT)
        out[b, :, j] = o.transpose(2, 3, 1, 0).reshape(NW, TT, D)
    return out.reshape(B, S, D)
```

```python
import numpy as np
import ml_dtypes
import concourse.bass as bass
import concourse.mybir as mybir
from concourse.bass_utils import run_bass_kernel_spmd

F32 = mybir.dt.float32
BF16 = mybir.dt.bfloat16
AF = mybir.ActivationFunctionType
ALU = mybir.AluOpType

DEPTH = 4
D = 1024
S = 8192
B = 2
TT = 512
NTILE = S // TT
NG = 4
NW = NTILE // NG
TOKC = NW * TT
NCORE = B * NG
R3 = 6 * 96 + 128 * 6
GROUPS = [[0, 1, 2, 3], [4, 5, 6, 7]]
EPS = 1e-6
SCALE = 96.0 ** -0.5
WINC = 2368
O_ZA, O_GA, O_CQ, O_CKV, O_GB, O_ZC, O_GC, O_KR, O_KRS = 0, 384, 768, 1152, 1408, 1792, 2048, 2304, 2336

_P = {}
_off = 0
for _n, _w in [("normg", DEPTH * 8), ("fng", 8), ("convw", DEPTH * 12), ("convb", DEPTH * 3),
               ("brg", DEPTH * 3), ("big", DEPTH * 3), ("lam", DEPTH * 3), ("qng", DEPTH * 3),
               ("kvng", DEPTH * 2), ("pscale", DEPTH * 2), ("invw", 2)]:
    _P[_n] = _off
    _off += _w
NPAR = _off
CP_W, CP_EB, CP_F, CP_IT, CP_G, NCPAR = 0, 4, 8, 12, 44, 48


class Ctx:
    def __init__(self, nc):
        self.nc = nc
        self.sems = {}
        self.eng = {}
        for name, h in [("pe", nc.tensor), ("act", nc.scalar), ("dve", nc.vector),
                        ("pool", nc.gpsimd), ("sp", nc.sync)]:
            self.eng[name] = dict(h=h, sem="e_" + name, count=0, waited={})
            self._sem("e_" + name)
        self.res_w = {}
        self.res_r = {}
        self.dcount = {}

    def _sem(self, name):
        if name not in self.sems:
            self.sems[name] = self.nc.semaphore(name).__enter__()
        return self.sems[name]

    def _deps(self, reads, writes):
        deps = []
        for k in reads:
            if k in self.res_w:
                deps.append(self.res_w[k])
        for k in writes:
            if k in self.res_w:
                deps.append(self.res_w[k])
            deps.extend(self.res_r.get(k, ()))
        return deps

    def _wait(self, e, deps):
        E = self.eng[e]
        need = {}
        for (sem, val, src) in deps:
            if src == "pe" and e == "pe":
                continue
            if src == "dma":
                val = max(val, self.dcount[sem])
            if val > need.get(sem, 0):
                need[sem] = val
        for sem, val in need.items():
            if E["waited"].get(sem, 0) < val:
                E["h"].wait_ge(self.sems[sem], val)
                E["waited"][sem] = val

    def _record(self, tok, reads, writes):
        for k in reads:
            self.res_r.setdefault(k, []).append(tok)
        for k in writes:
            self.res_w[k] = tok
            self.res_r[k] = []

    def op(self, e, fn, reads=(), writes=(), sig=True):
        self._wait(e, self._deps(reads, writes))
        E = self.eng[e]
        ins = fn()
        if sig:
            E["count"] += 1
            ins.then_inc(self.sems[E["sem"]], 1)
            tok = (E["sem"], E["count"], e)
        else:
            tok = (E["sem"], E["count"] + 1, e)
        self._record(tok, reads, writes)

    def dma(self, q, out, in_, reads, writes, semkey):
        self._wait(q, self._deps(reads, writes))
        name = "d_" + semkey
        self._sem(name)
        self.dcount[name] = self.dcount.get(name, 0) + 16
        self.eng[q]["h"].dma_start(out=out, in_=in_).then_inc(self.sems[name], 16)
        self._record((name, self.dcount[name], "dma"), reads, writes)

    def finish(self, e):
        deps = list(self.res_w.values())
        for v in self.res_r.values():
            deps.extend(v)
        self._wait(e, deps)


def build_program(depth=DEPTH, nwave=NW, dbg=False):
    nc = bass.Bass("TRN2", target_bir_lowering=False)
    C = Ctx(nc)

    def din(name, shape, dt=F32):
        return nc.dram_tensor(name, shape, dt, kind="ExternalInput").ap()

    xT = din("xT", [128, 8, TOKC])
    w_in_d = din("w_in", [DEPTH, 128, 8, WINC])
    w_out_d = din("w_out", [DEPTH, 128, 8, D])
    w_uqA_d = din("w_uqA", [DEPTH, 128, 3, 576])
    w_uqB_d = din("w_uqB", [DEPTH, 128, 3, 576])
    w_uk_d = din("w_uk", [DEPTH, 128, 2, 384])
    w_uv_d = din("w_uv", [DEPTH, 128, 2, 384])
    w_rg_d = din("w_rg", [DEPTH, 128, 3, 128])
    w_ig_d = din("w_ig", [DEPTH, 128, 3, 128])
    w_pool_d = din("w_pool", [DEPTH, 128, 2, 128])
    params_d = din("params", [128, NPAR])
    cos_d = din("cosT", [32, TOKC])
    sin_d = din("sinT", [32, TOKC])
    cpar_d = din("cpar", [128, NCPAR])
    emask_d = din("emask", [128, 896], BF16)
    ident_d = din("ident", [128, 128], BF16)
    outT = nc.dram_tensor("outT", [128, 8, TOKC], F32, kind="ExternalOutput").ap()
    xs = [nc.dram_tensor(f"xs{i}", [128, 8, TOKC], F32).ap() for i in range(2)]
    send1 = [nc.dram_tensor(f"send1_{m}", [128, 64], F32).ap() for m in range(NW)]
    recv1 = [nc.dram_tensor(f"recv1_{m}", [NG * 128, 64], F32).ap() for m in range(NW)]
    send2 = [nc.dram_tensor(f"send2_{m}", [128, 8], F32).ap() for m in range(NW)]
    recv2 = [nc.dram_tensor(f"recv2_{m}", [NG * 128, 8], F32).ap() for m in range(NW)]
    send3 = [[[nc.dram_tensor(f"send3_{p}_{m}_{q}", [128, 2048], BF16).ap() for q in range(3)]
              for m in range(NW)] for p in range(2)]
    recv3 = [[[nc.dram_tensor(f"recv3_{p}_{m}_{q}", [NG * 128, 2048], BF16).ap() for q in range(3)]
              for m in range(NW)] for p in range(2)]

    def sb(name, shape, dt=F32):
        return nc.sbuf_tensor(name, shape, dt).__enter__()

    xt = sb("xt", [128, 8, TT])
    hT = sb("hT", [128, 8, TT], BF16)
    yg = sb("yg", [128, 8, TT], BF16)
    win = sb("win", [128, 8, WINC], BF16)
    wout = sb("wout", [128, 8, D], BF16)
    wuqA = sb("wuqA", [128, 3, 576], BF16)
    wuqB = sb("wuqB", [128, 3, 576], BF16)
    wuk = sb("wuk", [128, 2, 384], BF16)
    wuv = sb("wuv", [128, 2, 384], BF16)
    wrg = sb("wrg", [128, 3, 128], BF16)
    wig = sb("wig", [128, 3, 128], BF16)
    wpool = sb("wpool", [128, 2, 128], BF16)
    par = sb("par", [128, NPAR])
    emask = sb("emask_sb", [128, 896], BF16)
    ident = sb("ident_sb", [128, 128], BF16)
    Ig = sb("Ig", [128, NG, 128], BF16)
    ones_f = sb("ones_f", [128, 128])
    eps_t = sb("eps_t", [128, 1])
    sq = [sb(f"sq{i}", [128, TT]) for i in range(2)]
    rs = sb("rs", [128, TT])
    zapad = sb("zapad", [128, 3, TT + 3])
    zcpad = sb("zcpad", [128, 2, TT + 15])
    cqraw = sb("cqraw", [128, 3, TT])
    cqn = sb("cqn", [128, 3, TT], BF16)
    ckvraw = sb("ckvraw", [128, 2, TT])
    ckvn = sb("ckvn", [128, 2, TT], BF16)
    xa = sb("xa", [128, TT])
    xab = sb("xab", [128, TT], BF16)
    r_t = sb("r_t", [128, TT])
    i_t = sb("i_t", [128, TT])
    a_t = sb("a_t", [128, TT])
    a2_t = sb("a2_t", [128, TT])
    u_t = sb("u_t", [128, TT])
    hloc = sb("hloc", [128, 3, TT])
    Ab = sb("Ab", [128, 3, TT])
    zeros_t = sb("zeros_t", [128, TT])
    cpar = sb("cpar_sb", [128, NCPAR])
    s1 = sb("s1", [128, 64])
    H1 = sb("H1", [128, NG, 64])
    Hprev3 = sb("Hprev3", [128, 64])
    halo = sb("halo", [128, 64])
    s2 = sb("s2", [128, 8])
    C2 = sb("C2", [128, NG, 8])
    Sch = sb("Sch", [128, NG + 1, 3])
    carry = sb("carry", [128, 3])
    lsp = sb("lsp", [128, 12])
    nbias = sb("nbias", [128, 6])
    sA = sb("sA", [128, TT + 15])
    sB = sb("sB", [128, TT + 15])
    pooled = sb("pooled", [128, TT], BF16)
    tmp16 = sb("tmp16", [128, 16])
    QT = sb("QT", [128, 6, TT], BF16)
    cosb = sb("cosb", [128, TT])
    sinb = sb("sinb", [128, TT])
    t1, t2 = r_t, i_t
    kst = sb("kst", [128, 3, TT], BF16)
    krst = sb("krst", [128, TT], BF16)
    vst = sb("vst", [128, 2, 3, 4, 128], BF16)
    NR = 3
    kring = [sb(f"kring{i}", [128, 2 * TT], BF16) for i in range(NR)]
    vring = [sb(f"vring{i}", [128, 2 * TT], BF16) for i in range(NR)]
    NP = 4
    pring = [sb(f"pring{i}", [128, TT], BF16) for i in range(NP)]
    lsh, otmp = sq[0], sq[1]
    ps = [nc.psum_tensor(f"ps{i}", [128, TT], F32).__enter__() for i in range(8)]

    pe, act, dve, pool = nc.tensor, nc.scalar, nc.vector, nc.gpsimd
    state = dict(gen=0, sb=0, pr=0, ring=0, sqi=0)

    def gbank():
        if state.get("attn"):
            i = 5 + state["gen"] % 3
        else:
            i = state["gen"] % 8
        state["gen"] += 1
        return ps[i], ("ps", i)

    def P_(name, l=None, width=1, idx=0):
        o = _P[name] + (0 if l is None else l * width) + idx
        return par[:, o:o + 1]

    C.dma("sp", par[:], params_d, [], ["par"], "par")
    C.dma("sp", emask[:], emask_d, [], ["emask"], "emask")
    C.dma("sp", cpar[:], cpar_d, [], ["cpar"], "cpar")
    C.dma("sp", ident[:], ident_d, [], ["ident"], "ident")
    for jp in range(NG):
        C.op("dve", lambda: dve.tensor_scalar(out=Ig[:, jp, :], in0=ident[:], scalar1=cpar[:, CP_G + jp:CP_G + jp + 1],
                                              scalar2=None, op0=ALU.mult), ["ident", "cpar"], ["Ig"])
    C.op("dve", lambda: dve.memset(zeros_t[:], 0.0), [], ["zeros_t"])
    C.op("dve", lambda: dve.memset(xab[:], 0.0), [], ["xab"])
    for p_ in range(2):
        for m_ in range(nwave):
            for q_ in range(3):
                for hf in range(2):
                    C.dma("sp", send3[p_][m_][q_][96:128, hf * 512:(hf + 1) * 512], xab[96:128, :], ["xab"],
                          [("s3z", p_, m_, q_, hf)], "xab")
    C.op("dve", lambda: dve.memset(s1[:], 0.0), [], ["s1"])
    C.op("dve", lambda: dve.memset(s2[:], 0.0), [], ["s2"])
    C.op("dve", lambda: dve.memset(ones_f[:], 1.0), [], ["ones_f"])
    C.op("dve", lambda: dve.memset(eps_t[:], EPS), [], ["eps_t"])
    C.op("dve", lambda: dve.memset(vst[:, 0, :, :, 64:128], 1.0), [], ["vst"])
    C.op("dve", lambda: dve.memset(vst[:, 1, :, :, 0:64], 1.0), [], ["vst"])

    def sumsq_rstd(srcs, nfeat, rkeys):
        bank, bkey = gbank()
        n = len(srcs)
        for i, s_ap in enumerate(srcs):
            q = sq[state["sqi"] % 2]
            qk = ("sq", state["sqi"] % 2)
            state["sqi"] += 1
            rk = rkeys[i] if (len(rkeys) == n and isinstance(rkeys[0], list)) else rkeys
            C.op("act", lambda: act.activation(out=q[:], in_=s_ap, func=AF.Square), rk, [qk])
            C.op("pe", lambda: pe.matmul(bank[:], lhsT=ones_f[:], rhs=q[:], start=(i == 0), stop=(i == n - 1)),
                 [qk, "ones_f"], [bkey], sig=(i == n - 1) or True)
        C.op("act", lambda: act.activation(out=rs[:], in_=bank[:], func=AF.Ln, scale=1.0 / nfeat,
                                           bias=eps_t[:, 0:1]), [bkey, "eps_t"], ["rs"])
        C.op("act", lambda: act.activation(out=rs[:], in_=rs[:], func=AF.Exp, scale=-0.5), ["rs"], ["rs"])

    for l in range(depth):
        for kc in range(8):
            C.dma("pool", win[:, kc, :], w_in_d[l, :, kc, :], [], [("win", kc)], "win")
        for kc in range(8):
            C.dma("pool", wout[:, kc, :], w_out_d[l, :, kc, :], [], [("wout", kc)], "wout")
        for (t_sb, t_d, key) in [(wuqA, w_uqA_d, "wuqA"), (wuqB, w_uqB_d, "wuqB"), (wuk, w_uk_d, "wuk"),
                                 (wuv, w_uv_d, "wuv"), (wrg, w_rg_d, "wrg"), (wig, w_ig_d, "wig"),
                                 (wpool, w_pool_d, "wpool")]:
            C.dma("pool", t_sb[:], t_d[l], [], [key], key)
        C.op("act", lambda: act.activation(out=lsp[:, 0:3], in_=par[:, _P["lam"] + 3 * l:_P["lam"] + 3 * l + 3],
                                           func=AF.Exp, scale=-1.0), ["par"], ["lsp"])
        C.op("act", lambda: act.activation(out=lsp[:, 0:3], in_=lsp[:, 0:3], func=AF.Ln, bias=1.0),
             ["lsp"], ["lsp"])
        C.op("dve", lambda: dve.tensor_scalar(out=lsp[:, 3:6], in0=lsp[:, 0:3], scalar1=-8.0, scalar2=None,
                                              op0=ALU.mult), ["lsp"], ["lsp"])
        C.op("dve", lambda: dve.tensor_scalar(out=lsp[:, 6:9], in0=lsp[:, 0:3], scalar1=-16.0, scalar2=None,
                                              op0=ALU.mult), ["lsp"], ["lsp"])
        C.op("dve", lambda: dve.tensor_scalar(out=nbias[:, 0:3], in0=par[:, _P["brg"] + 3 * l:_P["brg"] + 3 * l + 3],
                                              scalar1=-1.0, scalar2=None, op0=ALU.mult), ["par", "nbias"], ["nbias"])
        C.op("dve", lambda: dve.tensor_scalar(out=nbias[:, 3:6], in0=par[:, _P["big"] + 3 * l:_P["big"] + 3 * l + 3],
                                              scalar1=-1.0, scalar2=None, op0=ALU.mult), ["par", "nbias"], ["nbias"])

        C.op("dve", lambda: dve.memset(Hprev3[:], 0.0), ["Hprev3"], ["Hprev3"])
        C.op("dve", lambda: dve.memset(Sch[:, 0, :], 0.0), ["Sch"], ["Sch"])
        for T in range(nwave):
            tok = slice(T * TT, (T + 1) * TT)
            pty = l % 2
            src = xT if l == 0 else xs[(l - 1) % 2]
            for kc in range(8):
                C.dma("sp", xt[:, kc, :], src[:, kc, tok], [("X", l, T, kc)], [("xt", kc)], f"xt{kc}")
            C.dma("sp", cosb[64:96, :], cos_d[:, tok], [], ["cosb"], "cosb")
            C.dma("sp", sinb[64:96, :], sin_d[:, tok], [], ["sinb"], "sinb")
            sumsq_rstd([xt[:, kc, :] for kc in range(8)], float(D), [[("xt", kc)] for kc in range(8)])
            for kc in range(8):
                C.op("dve", lambda: dve.scalar_tensor_tensor(out=hT[:, kc, :], in0=xt[:, kc, :],
                                                             scalar=P_("normg", l, 8, kc), in1=rs[:],
                                                             op0=ALU.mult, op1=ALU.mult),
                     [("xt", kc), "rs", "par"], [("hT", kc)])
            def inproj(col0, M):
                bank, bkey = gbank()
                for kc in range(8):
                    C.op("pe", lambda: pe.matmul(bank[0:M, :], lhsT=win[:, kc, col0:col0 + M], rhs=hT[:, kc, :],
                                                 start=(kc == 0), stop=(kc == 7)),
                         [("win", kc), ("hT", kc)], [bkey], sig=(kc == 7))
                return bank, bkey

            for c in range(3):
                bank, bkey = inproj(O_ZA + 128 * c, 128)
                C.op("act", lambda: act.activation(out=zapad[:, c, 3:TT + 3], in_=bank[:], func=AF.Copy),
                     [bkey], ["zapad"])
            for c in range(2):
                bank, bkey = inproj(O_ZC + 128 * c, 128)
                C.op("act", lambda: act.activation(out=zcpad[:, c, 15:TT + 15], in_=bank[:], func=AF.Copy),
                     [bkey], ["zcpad"])
            C.op("dve", lambda: dve.tensor_copy(out=s1[:, 0:9].rearrange("p (c k) -> p c k", k=3),
                                                in_=zapad[:, :, TT:TT + 3]), ["zapad"], ["s1"])
            C.op("dve", lambda: dve.tensor_copy(out=s1[:, 9:39].rearrange("p (c k) -> p c k", k=15),
                                                in_=zcpad[:, :, TT:TT + 15]), ["zcpad"], ["s1"])
            C.dma("sp", send1[T], s1[:], ["s1"], [("send1", T)], "s1")
            C.op("pool", lambda: pool.collective_compute("AllGather", ALU.bypass, replica_groups=GROUPS,
                                                         ins=[send1[T]], outs=[recv1[T]]),
                 [("send1", T)], [("recv1", T)])
            C.dma("pool", H1[:], recv1[T].rearrange("(r p) n -> p r n", p=128), [("recv1", T)], ["H1"], "H1")

            for c in range(2):
                bank, bkey = inproj(O_CKV + 128 * c, 128)
                C.op("act", lambda: act.activation(out=ckvraw[:, c, :], in_=bank[:], func=AF.Copy),
                     [bkey], ["ckvraw"])
            bankA, kA = inproj(O_KR - 64, 96)
            bankB, kB = inproj(O_KRS - 64, 96)
            C.op("dve", lambda: dve.tensor_tensor(out=t1[64:96, :], in0=bankA[64:96, :], in1=cosb[64:96, :], op=ALU.mult),
                 [kA, "cosb"], ["r_t"])
            C.op("dve", lambda: dve.tensor_tensor(out=t2[64:96, :], in0=bankB[64:96, :], in1=sinb[64:96, :], op=ALU.mult),
                 [kB, "sinb"], ["i_t"])
            C.op("dve", lambda: dve.tensor_tensor(out=krst[64:96, :], in0=t1[64:96, :], in1=t2[64:96, :], op=ALU.add),
                 ["r_t", "i_t"], ["krst"])
            for h in range(6):
                C.dma("sp", send3[pty][T][h // 2][64:96, (h % 2) * 512:(h % 2) * 512 + 512], krst[64:96, :], ["krst"],
                      [("s3", h // 2, "r", h % 2)], "krst")
            sumsq_rstd([ckvraw[:, c, :] for c in range(2)], 256.0, ["ckvraw"])
            for c in range(2):
                C.op("dve", lambda: dve.scalar_tensor_tensor(out=ckvn[:, c, :], in0=ckvraw[:, c, :],
                                                             scalar=P_("kvng", l, 2, c), in1=rs[:],
                                                             op0=ALU.mult, op1=ALU.mult),
                     ["ckvraw", "rs", "par"], ["ckvn"])
            for c in range(3):
                bank, bkey = inproj(O_CQ + 128 * c, 128)
                C.op("act", lambda: act.activation(out=cqraw[:, c, :], in_=bank[:], func=AF.Copy),
                     [bkey], ["cqraw"])
            for p in range(3):
                bank, bkey = gbank()
                for c in range(2):
                    C.op("pe", lambda: pe.matmul(bank[:], lhsT=wuk[:, c, p * 128:(p + 1) * 128], rhs=ckvn[:, c, :],
                                                 start=(c == 0), stop=(c == 1)), ["wuk", "ckvn"], [bkey], sig=(c == 1))
                C.op("act", lambda: act.activation(out=kst[:, p, :], in_=bank[:], func=AF.Copy), [bkey], ["kst"])
            for p in range(3):
                C.dma("sp", send3[pty][T][p][0:64, 0:512], kst[0:64, p, :], ["kst"], [("s3", p, "n", 0)], "kst")
                C.dma("sp", send3[pty][T][p][0:64, 512:1024], kst[64:128, p, :], ["kst"], [("s3", p, "n", 1)], "kst")
            for blk in range(4):
                bank, bkey = gbank()
                for c in range(2):
                    C.op("pe", lambda: pe.matmul(bank[:, 0:384], lhsT=ckvn[:, c, blk * 128:(blk + 1) * 128], rhs=wuv[:, c, :],
                                                 start=(c == 0), stop=(c == 1)), ["wuv", "ckvn"], [bkey], sig=(c == 1))
                C.op("act", lambda: act.activation(out=vst[:, 0, :, blk, 0:64],
                                                   in_=bank[:, 0:192].rearrange("p (a d) -> p a d", d=64), func=AF.Copy),
                     [bkey], ["vst"])
                C.op("act", lambda: act.activation(out=vst[:, 1, :, blk, 64:128],
                                                   in_=bank[:, 192:384].rearrange("p (a d) -> p a d", d=64), func=AF.Copy),
                     [bkey], ["vst"])
            for q in range(3):
                C.dma("sp", send3[pty][T][q][:, 1024:2048].rearrange("p (a c) -> p a c", a=2),
                      vst[:, :, q, :, :].rearrange("p a k d -> p a (k d)"), ["vst"], [("s3", q, "v")], "vst")
            for q in range(3):
                s3keys = [("s3", q, "r", 0), ("s3", q, "r", 1), ("s3", q, "n", 0), ("s3", q, "n", 1), ("s3", q, "v"),
                          ("s3z", pty, T, q, 0), ("s3z", pty, T, q, 1)]
                C.op("pool", lambda: pool.collective_compute("AllGather", ALU.bypass, replica_groups=GROUPS,
                                                             ins=[send3[pty][T][q]], outs=[recv3[pty][T][q]]),
                     s3keys, [("recv3", T, q)])

            sumsq_rstd([cqraw[:, c, :] for c in range(3)], 384.0, ["cqraw"])
            for c in range(3):
                C.op("dve", lambda: dve.scalar_tensor_tensor(out=cqn[:, c, :], in0=cqraw[:, c, :],
                                                             scalar=P_("qng", l, 3, c), in1=rs[:],
                                                             op0=ALU.mult, op1=ALU.mult),
                     ["cqraw", "rs", "par"], ["cqn"])
            for (goff, ych, n) in [(O_GA, 0, 3), (O_GB, 3, 3), (O_GC, 6, 2)]:
                for c in range(n):
                    bank, bkey = inproj(goff + 128 * c, 128)
                    C.op("act", lambda: act.activation(out=yg[:, ych + c, :], in_=bank[:], func=AF.Silu),
                         [bkey], [("yg", ych + c)])


            for h in range(6):
                bankA, kA = gbank()
                for c in range(3):
                    C.op("pe", lambda: pe.matmul(bankA[0:96, :], lhsT=wuqA[:, c, h * 96:(h + 1) * 96], rhs=cqn[:, c, :],
                                                 start=(c == 0), stop=(c == 2)), ["wuqA", "cqn"], [kA], sig=(c == 2))
                bankB, kB = gbank()
                for c in range(3):
                    C.op("pe", lambda: pe.matmul(bankB[0:96, :], lhsT=wuqB[:, c, h * 96:(h + 1) * 96], rhs=cqn[:, c, :],
                                                 start=(c == 0), stop=(c == 2)), ["wuqB", "cqn"], [kB], sig=(c == 2))
                C.op("act", lambda: act.activation(out=QT[0:64, h, :], in_=bankA[0:64, :], func=AF.Copy),
                     [kA], [("QT", h)])
                C.op("dve", lambda: dve.tensor_tensor(out=t1[64:96, :], in0=bankA[64:96, :], in1=cosb[64:96, :], op=ALU.mult),
                     [kA, "cosb"], ["r_t"])
                C.op("dve", lambda: dve.tensor_tensor(out=t2[64:96, :], in0=bankB[64:96, :], in1=sinb[64:96, :], op=ALU.mult),
                     [kB, "sinb"], ["i_t"])
                C.op("dve", lambda: dve.tensor_tensor(out=QT[64:96, h, :], in0=t1[64:96, :], in1=t2[64:96, :], op=ALU.add),
                     ["r_t", "i_t"], [("QT", h)])
            def side_gen():
                C.op("dve", lambda: dve.tensor_scalar(out=halo[:], in0=Hprev3[:], scalar1=cpar[:, CP_W + 3:CP_W + 4],
                                                      scalar2=None, op0=ALU.mult), ["Hprev3", "cpar"], ["halo"])
                yield
                for k in range(3):
                    C.op("dve", lambda: dve.scalar_tensor_tensor(out=halo[:], in0=H1[:, k, :],
                                                                 scalar=cpar[:, CP_W + k:CP_W + k + 1], in1=halo[:],
                                                                 op0=ALU.mult, op1=ALU.add), ["H1", "cpar", "halo"], ["halo"])
                    yield
                C.op("dve", lambda: dve.tensor_copy(out=Hprev3[:], in_=H1[:, 3, :]), ["H1", "halo"], ["Hprev3"])
                yield
                C.op("dve", lambda: dve.tensor_copy(out=zapad[:, :, 0:3], in_=halo[:, 0:9].rearrange("p (c k) -> p c k", k=3)),
                     ["halo", "s1"], ["zapad"])
                yield
                C.op("dve", lambda: dve.tensor_copy(out=zcpad[:, :, 0:15], in_=halo[:, 9:39].rearrange("p (c k) -> p c k", k=15)),
                     ["halo", "s1"], ["zcpad"])
                yield

                for c in range(3):
                    cw = _P["convw"] + l * 12 + c * 4
                    C.op("dve", lambda: dve.tensor_scalar(out=xa[:], in0=zapad[:, c, 0:TT], scalar1=par[:, cw:cw + 1],
                                                          scalar2=P_("convb", l, 3, c), op0=ALU.mult, op1=ALU.add),
                         ["zapad", "par"], ["xa"])
                    yield
                    for k in range(1, 4):
                        C.op("dve", lambda: dve.scalar_tensor_tensor(out=xa[:], in0=zapad[:, c, k:k + TT],
                                                                     scalar=par[:, cw + k:cw + k + 1], in1=xa[:],
                                                                     op0=ALU.mult, op1=ALU.add),
                             ["zapad", "par", "xa"], ["xa"])
                        yield
                    C.op("dve", lambda: dve.tensor_copy(out=xab[:], in_=xa[:]), ["xa"], ["xab"])
                    yield
                    bank_r, kr_ = gbank()
                    C.op("pe", lambda: pe.matmul(bank_r[:], lhsT=wrg[:, c, :], rhs=xab[:], start=True, stop=True),
                         ["wrg", "xab"], [kr_])
                    yield
                    bank_i, ki_ = gbank()
                    C.op("pe", lambda: pe.matmul(bank_i[:], lhsT=wig[:, c, :], rhs=xab[:], start=True, stop=True),
                         ["wig", "xab"], [ki_])
                    yield
                    for (dst, bnk, bk, col) in ((r_t, bank_r, kr_, c), (i_t, bank_i, ki_, 3 + c)):
                        dk = "r_t" if dst is r_t else "i_t"
                        C.op("act", lambda: act.activation(out=dst[:], in_=bnk[:], func=AF.Exp, scale=-1.0,
                                                           bias=nbias[:, col:col + 1]), [bk, "nbias"], [dk])
                        yield
                        C.op("act", lambda: act.activation(out=dst[:], in_=dst[:], func=AF.Ln, bias=1.0), [dk], [dk])
                        yield
                        C.op("act", lambda: act.activation(out=dst[:], in_=dst[:], func=AF.Exp, scale=-1.0), [dk], [dk])
                        yield
                    C.op("act", lambda: act.activation(out=a_t[:], in_=r_t[:], func=AF.Exp, scale=lsp[:, 3 + c:4 + c]),
                         ["r_t", "lsp"], ["a_t"])
                    yield
                    C.op("act", lambda: act.activation(out=a2_t[:], in_=r_t[:], func=AF.Exp, scale=lsp[:, 6 + c:7 + c]),
                         ["r_t", "lsp"], ["a2_t"])
                    yield
                    C.op("act", lambda: act.activation(out=a2_t[:], in_=a2_t[:], func=AF.Ln, scale=-1.0, bias=1.0),
                         ["a2_t"], ["a2_t"])
                    yield
                    C.op("act", lambda: act.activation(out=a2_t[:], in_=a2_t[:], func=AF.Exp, scale=0.5),
                         ["a2_t"], ["a2_t"])
                    yield
                    C.op("dve", lambda: dve.tensor_tensor(out=u_t[:], in0=i_t[:], in1=xa[:], op=ALU.mult),
                         ["i_t", "xa"], ["u_t"])
                    yield
                    C.op("dve", lambda: dve.tensor_tensor(out=u_t[:], in0=u_t[:], in1=a2_t[:], op=ALU.mult),
                         ["u_t", "a2_t"], ["u_t"])
                    yield
                    C.op("dve", lambda: dve.tensor_tensor_scan(out=hloc[:, c, :], data0=a_t[:], data1=u_t[:], initial=0.0,
                                                               op0=ALU.mult, op1=ALU.add),
                         ["a_t", "u_t"], [("hloc", c)])
                    yield
                    C.op("dve", lambda: dve.tensor_tensor_scan(out=Ab[:, c, :], data0=a_t[:], data1=zeros_t[:], initial=1.0,
                                                               op0=ALU.mult, op1=ALU.add),
                         ["a_t", "zeros_t"], [("Ab", c)])
                    yield
                    C.op("dve", lambda: dve.tensor_copy(out=s2[:, c:c + 1], in_=hloc[:, c, TT - 1:TT]), [("hloc", c)], ["s2"])
                    yield
                    C.op("dve", lambda: dve.tensor_copy(out=s2[:, 3 + c:4 + c], in_=Ab[:, c, TT - 1:TT]), [("Ab", c)], ["s2"])
                    yield

                C.dma("pool", send2[T], s2[:], ["s2"], [("send2", T)], "s2")
                yield
                C.op("pool", lambda: pool.collective_compute("AllGather", ALU.bypass, replica_groups=GROUPS,
                                                             ins=[send2[T]], outs=[recv2[T]]),
                     [("send2", T)], [("recv2", T)])
                yield
                C.dma("pool", C2[:], recv2[T].rearrange("(r p) n -> p r n", p=128), [("recv2", T)], ["C2"], "C2")
                yield
                for c in range(2):
                    z = zcpad[:, c, :]
                    W = TT + 15
                    C.op("dve", lambda: dve.tensor_tensor(out=sA[:, 1:W], in0=z[:, 1:W], in1=z[:, 0:W - 1], op=ALU.add),
                         ["zcpad"], ["sA"])
                    yield
                    C.op("dve", lambda: dve.tensor_tensor(out=sB[:, 3:W], in0=sA[:, 3:W], in1=sA[:, 1:W - 2], op=ALU.add),
                         ["sA"], ["sB"])
                    yield
                    if c == 0:
                        lo, hi = sA, sB
                    else:
                        C.op("dve", lambda: dve.tensor_tensor(out=sA[:, 7:W], in0=sB[:, 7:W], in1=sB[:, 3:W - 4], op=ALU.add),
                             ["sB"], ["sA"])
                        yield
                        C.op("dve", lambda: dve.tensor_tensor(out=sB[:, 15:W], in0=sA[:, 15:W], in1=sA[:, 7:W - 8], op=ALU.add),
                             ["sA"], ["sB"])
                        yield
                        lo, hi = sA, sB
                    iw = _P["invw"] + c
                    for (p0, p1, stg) in [(0, 64, lo), (64, 128, hi)]:
                        C.op("dve", lambda: dve.scalar_tensor_tensor(out=pooled[p0:p1, :], in0=stg[p0:p1, 15:W],
                                                                     scalar=par[p0:p1, iw:iw + 1], in1=z[p0:p1, 15:W],
                                                                     op0=ALU.mult, op1=ALU.subtract),
                             ["sA", "sB", "zcpad", "par"], ["pooled"])
                        yield
                        if T == 0:
                            it = CP_IT + 16 * c
                            C.op("dve", lambda: dve.tensor_tensor(out=tmp16[p0:p1, :], in0=stg[p0:p1, 15:31],
                                                                  in1=cpar[p0:p1, it:it + 16], op=ALU.mult),
                                 ["sA", "sB", "cpar"], ["tmp16"])
                            yield
                            C.op("dve", lambda: dve.tensor_tensor(out=pooled[p0:p1, 0:16], in0=tmp16[p0:p1, :],
                                                                  in1=z[p0:p1, 15:31], op=ALU.subtract),
                                 ["tmp16", "zcpad"], ["pooled"])
                            yield
                    bank, bkey = gbank()
                    C.op("pe", lambda: pe.matmul(bank[:], lhsT=wpool[:, c, :], rhs=pooled[:], start=True, stop=True),
                         ["wpool", "pooled"], [bkey])
                    yield
                    C.op("dve", lambda: dve.scalar_tensor_tensor(out=yg[:, 6 + c, :], in0=bank[:],
                                                                 scalar=P_("pscale", l, 2, c), in1=yg[:, 6 + c, :],
                                                                 op0=ALU.mult, op1=ALU.mult),
                         [bkey, "par", ("yg", 6 + c)], [("yg", 6 + c)])
                    yield


            steps = []
            for ph, mps in ((1, list(range(T))), (2, [T])):
                for q in range(3):
                    kts = [(mp, jp) for mp in mps for jp in range(NG)]
                    for n_, kt in enumerate(kts):
                        for hh in range(2):
                            for kb in range(4):
                                steps.append((2 * q + hh, kt, kb, n_ == 0 and kb == 0, n_ == len(kts) - 1 and kb == 3, ph))
            part = [cqraw[:, 0, :], cqraw[:, 1, :], cqraw[:, 2, :], ckvraw[:, 0, :], ckvraw[:, 1, :], rs[:]]
            pkey = ["cqraw", "cqraw", "cqraw", "ckvraw", "ckvraw", "rs"]
            LA = 2
            info = {}

            def emit_qk(i):
                h, kt, kb, first, last, ph = steps[i]
                mp, jp = kt
                par_, pair = h % 2, h // 2
                if kb == 0 and par_ == 0:
                    slot = state["ring"] % NR
                    state["ring"] += 1
                    rkey = ("ring", slot)
                    src = recv3[pty][mp][pair]
                    C.dma("sp", kring[slot][0:96, :], src[jp * 128:jp * 128 + 96, 0:1024],
                          [("recv3", mp, pair)], [rkey], f"ring{slot}")
                    C.dma("sp", vring[slot][:], src[jp * 128:(jp + 1) * 128, 1024:2048],
                          [("recv3", mp, pair)], [rkey], f"ring{slot}")
                    info[(pair, kt, ph)] = slot
                slot = info[(h // 2, kt, ph)]
                rkey = ("ring", slot)
                si = 2 + state["sb"] % 3
                state["sb"] += 1
                pi = state["pr"] % NP
                state["pr"] += 1
                info[i] = (si, pi)
                if mp == T:
                    C.op("pe", lambda: pe.matmul(ps[si][:], lhsT=kring[slot][0:96, par_ * 512 + kb * 128:par_ * 512 + (kb + 1) * 128],
                                                 rhs=QT[0:96, h, :], start=True, stop=False),
                         [rkey, ("QT", h)], [("ps", si)], sig=False)
                    C.op("pe", lambda: pe.matmul(ps[si][:], lhsT=Ig[:, jp, :], rhs=emask[:, 384 - 128 * kb:896 - 128 * kb],
                                                 start=False, stop=True),
                         ["Ig", "emask"], [("ps", si)])
                    C.op("act", lambda: act.activation(out=pring[pi][:], in_=ps[si][:], func=AF.Exp, scale=SCALE,
                                                       bias=cpar[:, CP_EB + jp:CP_EB + jp + 1]),
                         [("ps", si), "cpar"], [("P", pi)])
                else:
                    C.op("pe", lambda: pe.matmul(ps[si][:], lhsT=kring[slot][0:96, par_ * 512 + kb * 128:par_ * 512 + (kb + 1) * 128],
                                                 rhs=QT[0:96, h, :], start=True, stop=True),
                         [rkey, ("QT", h)], [("ps", si)])
                    C.op("act", lambda: act.activation(out=pring[pi][:], in_=ps[si][:], func=AF.Exp, scale=SCALE),
                         [("ps", si)], [("P", pi)])

            def emit_pv(i):
                h, kt, kb, first, last, ph = steps[i]
                par_, pair = h % 2, h // 2
                slot = info[(h // 2, kt, ph)]
                rkey = ("ring", slot)
                si, pi = info[i]
                ob = ps[h % 2]
                okey = ("ps", h % 2)
                C.op("pe", lambda: pe.matmul(ob[:], lhsT=vring[slot][:, par_ * 512 + kb * 128:par_ * 512 + (kb + 1) * 128], rhs=pring[pi][:],
                                             start=first, stop=last),
                     [rkey, ("P", pi)], [okey])
                if last and ph == 1:
                    C.op("dve", lambda: dve.tensor_copy(out=part[h], in_=ob[:]), [okey, pkey[h]], [pkey[h]])
                if last and ph == 2:
                    if T > 0:
                        C.op("dve", lambda: dve.tensor_tensor(out=part[h], in0=ob[:], in1=part[h], op=ALU.add),
                             [okey, pkey[h]], [pkey[h]])
                        src, skey = part[h], pkey[h]
                    else:
                        src, skey = ob, okey
                    if par_ == 0:
                        o0, o1, l0, l1 = 0, 64, 64, 128
                    else:
                        o0, o1, l0, l1 = 64, 128, 0, 64
                    C.op("dve", lambda: dve.tensor_copy(out=lsh[o0:o1, :], in_=src[l0:l1, :]), [skey], [("sq", 0)])
                    C.op("dve", lambda: dve.reciprocal(out=lsh[o0:o1, :], in_=lsh[o0:o1, :]), [("sq", 0)], [("sq", 0)])
                    C.op("dve", lambda: dve.tensor_tensor(out=otmp[o0:o1, :], in0=src[o0:o1, :], in1=lsh[o0:o1, :],
                                                          op=ALU.mult), [skey, ("sq", 0)], [("sq", 1)])
                    C.op("dve", lambda: dve.tensor_tensor(out=yg[o0:o1, 3 + pair, :], in0=otmp[o0:o1, :],
                                                          in1=yg[o0:o1, 3 + pair, :], op=ALU.mult),
                         [("sq", 1), ("yg", 3 + pair)], [("yg", 3 + pair)])

            nst = len(steps)
            state["attn"] = True
            sg = side_gen()
            for i in range(nst + LA):
                if i < nst:
                    emit_qk(i)
                for _ in range(3):
                    next(sg, None)
                if i >= LA:
                    emit_pv(i - LA)
            for _ in sg:
                pass
            state["attn"] = False

            for k in range(NG):
                C.op("dve", lambda: dve.tensor_tensor(out=Sch[:, k + 1, :], in0=C2[:, k, 3:6], in1=Sch[:, k, :], op=ALU.mult),
                     ["C2", "Sch"], ["Sch"])
                C.op("dve", lambda: dve.tensor_tensor(out=Sch[:, k + 1, :], in0=Sch[:, k + 1, :], in1=C2[:, k, 0:3], op=ALU.add),
                     ["C2", "Sch"], ["Sch"])
            C.op("dve", lambda: dve.tensor_scalar(out=carry[:], in0=Sch[:, 0, :], scalar1=cpar[:, CP_W + 3:CP_W + 4],
                                                  scalar2=None, op0=ALU.mult), ["Sch", "cpar"], ["carry"])
            for k in range(3):
                C.op("dve", lambda: dve.scalar_tensor_tensor(out=carry[:], in0=Sch[:, k + 1, :],
                                                             scalar=cpar[:, CP_W + k:CP_W + k + 1], in1=carry[:],
                                                             op0=ALU.mult, op1=ALU.add), ["Sch", "cpar", "carry"], ["carry"])
            C.op("dve", lambda: dve.tensor_copy(out=Sch[:, 0, :], in_=Sch[:, NG, :]), ["Sch", "carry"], ["Sch"])
            for c in range(3):
                C.op("dve", lambda: dve.scalar_tensor_tensor(out=hloc[:, c, :], in0=Ab[:, c, :], scalar=carry[:, c:c + 1],
                                                             in1=hloc[:, c, :], op0=ALU.mult, op1=ALU.add),
                     [("Ab", c), ("hloc", c), "carry"], [("hloc", c)])
                C.op("dve", lambda: dve.tensor_tensor(out=yg[:, c, :], in0=hloc[:, c, :], in1=yg[:, c, :], op=ALU.mult),
                     [("hloc", c), ("yg", c)], [("yg", c)])

            for oc in range(8):
                bank, bkey = gbank()
                for kc in range(8):
                    C.op("pe", lambda: pe.matmul(bank[:], lhsT=wout[:, kc, oc * 128:(oc + 1) * 128], rhs=yg[:, kc, :],
                                                 start=(kc == 0), stop=(kc == 7)),
                         [("wout", kc), ("yg", kc)], [bkey], sig=(kc == 7))
                C.op("dve", lambda: dve.tensor_tensor(out=xt[:, oc, :], in0=xt[:, oc, :], in1=bank[:], op=ALU.add),
                     [("xt", oc), bkey], [("xt", oc)])
                if l < depth - 1:
                    C.dma("sp", xs[l % 2][:, oc, tok], xt[:, oc, :], [("xt", oc)], [("X", l + 1, T, oc)], f"xt{oc}")
            if l == depth - 1:
                sumsq_rstd([xt[:, kc, :] for kc in range(8)], float(D), [[("xt", kc)] for kc in range(8)])
                for kc in range(8):
                    C.op("dve", lambda: dve.scalar_tensor_tensor(out=xt[:, kc, :], in0=xt[:, kc, :],
                                                                 scalar=P_("fng", None, 1, kc), in1=rs[:],
                                                                 op0=ALU.mult, op1=ALU.mult),
                         [("xt", kc), "rs", "par"], [("xt", kc)])
                    C.dma("sp", outT[:, kc, tok], xt[:, kc, :], [("xt", kc)], [("OUT", T, kc)], f"xt{kc}")
    C.finish("sp")
    return nc


def _prep_shared(inp):
    f = np.float32
    w_in = np.asarray(inp["w_in"], f)
    offs = np.cumsum([0, 384, 384, 384, 256, 32, 384, 256, 256])
    za, ga, cq, ckv, kr, gb, zc, gc = [np.arange(offs[i], offs[i + 1]) for i in range(8)]
    krs = np.concatenate([kr[16:32], kr[0:16]])
    perm = np.concatenate([za, ga, cq, ckv, gb, zc, gc, kr, krs])
    assert perm.size == WINC

    def kmaj(w, nk):
        L, _, N = w.shape
        return np.ascontiguousarray(w.reshape(L, nk, 128, N).transpose(0, 2, 1, 3))

    d = {}
    d["w_in"] = kmaj(w_in[:, :, perm], 8)
    d["w_out"] = kmaj(np.asarray(inp["w_out"], f), 8)
    w_uq = np.asarray(inp["w_uq"], f).reshape(DEPTH, 384, 6, 96)
    nope, rope = w_uq[..., :64], w_uq[..., 64:]
    A = np.concatenate([nope, rope], axis=-1).reshape(DEPTH, 384, 576)
    Bm = np.concatenate([nope, rope[..., 16:], rope[..., :16]], axis=-1).reshape(DEPTH, 384, 576)
    d["w_uqA"] = kmaj(A, 3)
    d["w_uqB"] = kmaj(Bm, 3)
    w_ukv = np.asarray(inp["w_ukv"], f).reshape(DEPTH, 256, 6, 128)
    d["w_uk"] = kmaj(np.ascontiguousarray(w_ukv[..., :64]).reshape(DEPTH, 256, 384), 2)
    v = w_ukv[..., 64:]
    v = np.concatenate([v[:, :, 0::2, :], v[:, :, 1::2, :]], axis=2).reshape(DEPTH, 256, 384)
    d["w_uv"] = kmaj(v, 2)

    def bdiag(w, n):
        L = w.shape[0]
        o = np.zeros((L, 128, n, 128), f)
        for c in range(n):
            o[:, 0:64, c, 0:64] = w[:, 2 * c]
            o[:, 64:128, c, 64:128] = w[:, 2 * c + 1]
        return o

    d["w_rg"] = bdiag(np.asarray(inp["w_rg"], f), 3)
    d["w_ig"] = bdiag(np.asarray(inp["w_ig"], f), 3)
    d["w_pool"] = bdiag(np.asarray(inp["w_pool"], f), 2)

    par = np.zeros((128, NPAR), f)

    def put(name, arr):
        a = np.asarray(arr, f)
        lead = a.shape[:-1]
        nch = a.shape[-1] // 128
        a = a.reshape(lead + (nch, 128))
        a = np.moveaxis(a, -1, 0).reshape(128, -1)
        par[:, _P[name]:_P[name] + a.shape[1]] = a

    put("normg", inp["norm_g"])
    put("fng", inp["final_norm_g"])
    cw = np.asarray(inp["conv_w"], f)
    cw = cw.reshape(DEPTH, 4, 3, 128).transpose(3, 0, 2, 1).reshape(128, DEPTH * 12)
    par[:, _P["convw"]:_P["convw"] + DEPTH * 12] = cw
    put("convb", inp["conv_b"])
    put("brg", inp["b_rg"])
    put("big", inp["b_ig"])
    put("lam", inp["lru_lambda"])
    put("qng", inp["q_norm_g"])
    put("kvng", inp["kv_norm_g"])
    put("pscale", inp["pool_scale"])
    wins = np.array([[2, 4], [8, 16]], f)
    for c in range(2):
        for hf in range(2):
            p0 = 64 * hf
            par[p0:p0 + 64, _P["invw"] + c] = f(1.0) / wins[c, hf]
    d["params"] = par
    cc = np.arange(896)[None, :] - 384
    d["emask"] = np.where(np.arange(128)[:, None] <= cc, 0.0, -30000.0).astype(ml_dtypes.bfloat16)
    d["ident"] = np.eye(128, dtype=np.float32).astype(ml_dtypes.bfloat16)
    return d


def _prep_core(j):
    f = np.float32
    d = {}
    pos = np.concatenate([np.arange((NG * m + j) * TT, (NG * m + j + 1) * TT) for m in range(NW)]).astype(f)
    inv_freq = (f(10000.0) ** (-np.arange(0, 32, 2, dtype=f) / f(32))).astype(f)
    ang = (pos[None, :] * inv_freq[:, None]).astype(f)
    cs, sn = np.cos(ang).astype(f), np.sin(ang).astype(f)
    d["cosT"] = np.ascontiguousarray(np.concatenate([cs, cs], 0))
    d["sinT"] = np.ascontiguousarray(np.concatenate([-sn, sn], 0))
    cp = np.zeros((128, NCPAR), f)
    cp[:, CP_W + (j - 1) % NG] = 1.0
    for jp in range(NG):
        cp[:, CP_EB + jp] = 0.0 if jp <= j else -30000.0
        cp[:, CP_F + jp] = 1.0 if jp < j else 0.0
        cp[:, CP_G + jp] = 1.0 if jp == j else 0.0
    wins = np.array([[2, 4], [8, 16]], f)
    t = np.arange(16, dtype=f)
    for c in range(2):
        for hf in range(2):
            p0 = 64 * hf
            if j == 0:
                cp[p0:p0 + 64, CP_IT + 16 * c:CP_IT + 16 * c + 16] = f(1.0) / np.minimum(t + 1, wins[c, hf])
            else:
                cp[p0:p0 + 64, CP_IT + 16 * c:CP_IT + 16 * c + 16] = f(1.0) / wins[c, hf]
    d["cpar"] = cp
    return d


_CACHE = {}


def kernel(**inputs):
    x = np.asarray(inputs["x"], np.float32)
    shared = _prep_shared(inputs)
    if "nc" not in _CACHE:
        _CACHE["nc"] = build_program()
    nc = _CACHE["nc"]
    in_maps = []
    for c in range(NCORE):
        b, j = divmod(c, NG)
        m = dict(shared)
        m.update(_prep_core(j))
        xb = x[b].reshape(NW, NG, TT, 8, 128)[:, j]
        m["xT"] = np.ascontiguousarray(xb.transpose(3, 2, 0, 1).reshape(128, 8, TOKC))
        in_maps.append(m)
    res = run_bass_kernel_spmd(nc, in_maps, core_ids=list(range(NCORE)))
    out = np.empty((B, NW, NG, TT, D), np.float32)
    for c in range(NCORE):
        b, j = divmod(c, NG)
        o = np.asarray(res.results[c]["outT"], np.float32).reshape(128, 8, NW, TT)
        out[b, :, j] = o.transpose(2, 3, 1, 0).reshape(NW, TT, D)
    return out.reshape(B, S, D)
```

```python
import numpy as np
import ml_dtypes
import concourse.bass as bass
import concourse.mybir as mybir
from concourse.bass_utils import run_bass_kernel_spmd

F32 = mybir.dt.float32
BF16 = mybir.dt.bfloat16
AF = mybir.ActivationFunctionType
ALU = mybir.AluOpType

DEPTH = 4
D = 1024
S = 8192
B = 2
TT = 512
NTILE = S // TT
NG = 4
NW = NTILE // NG
TOKC = NW * TT
NCORE = B * NG
R3 = 6 * 96 + 128 * 6
GROUPS = [[0, 1, 2, 3], [4, 5, 6, 7]]
EPS = 1e-6
SCALE = 96.0 ** -0.5
WINC = 2368
O_ZA, O_GA, O_CQ, O_CKV, O_GB, O_ZC, O_GC, O_KR, O_KRS = 0, 384, 768, 1152, 1408, 1792, 2048, 2304, 2336

_P = {}
_off = 0
for _n, _w in [("normg", DEPTH * 8), ("fng", 8), ("convw", DEPTH * 12), ("convb", DEPTH * 3),
               ("brg", DEPTH * 3), ("big", DEPTH * 3), ("lam", DEPTH * 3), ("qng", DEPTH * 3),
               ("kvng", DEPTH * 2), ("pscale", DEPTH * 2), ("invw", 2)]:
    _P[_n] = _off
    _off += _w
NPAR = _off
CP_W, CP_EB, CP_F, CP_IT, CP_G, NCPAR = 0, 4, 8, 12, 44, 48


class Ctx:
    def __init__(self, nc):
        self.nc = nc
        self.sems = {}
        self.eng = {}
        for name, h in [("pe", nc.tensor), ("act", nc.scalar), ("dve", nc.vector),
                        ("pool", nc.gpsimd), ("sp", nc.sync)]:
            self.eng[name] = dict(h=h, sem="e_" + name, count=0, waited={})
            self._sem("e_" + name)
        self.res_w = {}
        self.res_r = {}
        self.dcount = {}

    def _sem(self, name):
        if name not in self.sems:
            self.sems[name] = self.nc.semaphore(name).__enter__()
        return self.sems[name]

    def _deps(self, reads, writes):
        deps = []
        for k in reads:
            if k in self.res_w:
                deps.append(self.res_w[k])
        for k in writes:
            if k in self.res_w:
                deps.append(self.res_w[k])
            deps.extend(self.res_r.get(k, ()))
        return deps

    def _wait(self, e, deps):
        E = self.eng[e]
        need = {}
        for (sem, val, src) in deps:
            if src == "pe" and e == "pe":
                continue
            if src == "dma":
                val = max(val, self.dcount[sem])
            if val > need.get(sem, 0):
                need[sem] = val
        for sem, val in need.items():
            if E["waited"].get(sem, 0) < val:
                E["h"].wait_ge(self.sems[sem], val)
                E["waited"][sem] = val

    def _record(self, tok, reads, writes):
        for k in reads:
            self.res_r.setdefault(k, []).append(tok)
        for k in writes:
            self.res_w[k] = tok
            self.res_r[k] = []

    def op(self, e, fn, reads=(), writes=(), sig=True):
        self._wait(e, self._deps(reads, writes))
        E = self.eng[e]
        ins = fn()
        if sig:
            E["count"] += 1
            ins.then_inc(self.sems[E["sem"]], 1)
            tok = (E["sem"], E["count"], e)
        else:
            tok = (E["sem"], E["count"] + 1, e)
        self._record(tok, reads, writes)

    def dma(self, q, out, in_, reads, writes, semkey):
        self._wait(q, self._deps(reads, writes))
        name = "d_" + semkey
        self._sem(name)
        self.dcount[name] = self.dcount.get(name, 0) + 16
        self.eng[q]["h"].dma_start(out=out, in_=in_).then_inc(self.sems[name], 16)
        self._record((name, self.dcount[name], "dma"), reads, writes)

    def finish(self, e):
        deps = list(self.res_w.values())
        for v in self.res_r.values():
            deps.extend(v)
        self._wait(e, deps)


def build_program(depth=DEPTH, nwave=NW, dbg=False):
    nc = bass.Bass("TRN2", target_bir_lowering=False)
    C = Ctx(nc)

    def din(name, shape, dt=F32):
        return nc.dram_tensor(name, shape, dt, kind="ExternalInput").ap()

    xT = din("xT", [128, 8, TOKC])
    w_in_d = din("w_in", [DEPTH, 128, 8, WINC])
    w_out_d = din("w_out", [DEPTH, 128, 8, D])
    w_uqA_d = din("w_uqA", [DEPTH, 128, 3, 576])
    w_uqB_d = din("w_uqB", [DEPTH, 128, 3, 576])
    w_uk_d = din("w_uk", [DEPTH, 128, 2, 384])
    w_uv_d = din("w_uv", [DEPTH, 128, 2, 384])
    w_rg_d = din("w_rg", [DEPTH, 128, 3, 128])
    w_ig_d = din("w_ig", [DEPTH, 128, 3, 128])
    w_pool_d = din("w_pool", [DEPTH, 128, 2, 128])
    params_d = din("params", [128, NPAR])
    cos_d = din("cosT", [32, TOKC])
    sin_d = din("sinT", [32, TOKC])
    cpar_d = din("cpar", [128, NCPAR])
    emask_d = din("emask", [128, 896], BF16)
    ident_d = din("ident", [128, 128], BF16)
    outT = nc.dram_tensor("outT", [128, 8, TOKC], F32, kind="ExternalOutput").ap()
    xs = [nc.dram_tensor(f"xs{i}", [128, 8, TOKC], F32).ap() for i in range(2)]
    send1 = [nc.dram_tensor(f"send1_{m}", [128, 64], F32).ap() for m in range(NW)]
    recv1 = [nc.dram_tensor(f"recv1_{m}", [NG * 128, 64], F32).ap() for m in range(NW)]
    send2 = [nc.dram_tensor(f"send2_{m}", [128, 8], F32).ap() for m in range(NW)]
    recv2 = [nc.dram_tensor(f"recv2_{m}", [NG * 128, 8], F32).ap() for m in range(NW)]
    send3 = [[[nc.dram_tensor(f"send3_{p}_{m}_{q}", [128, 2048], BF16).ap() for q in range(3)]
              for m in range(NW)] for p in range(2)]
    recv3 = [[[nc.dram_tensor(f"recv3_{p}_{m}_{q}", [NG * 128, 2048], BF16).ap() for q in range(3)]
              for m in range(NW)] for p in range(2)]

    def sb(name, shape, dt=F32):
        return nc.sbuf_tensor(name, shape, dt).__enter__()

    xt = sb("xt", [128, 8, TT])
    hT = sb("hT", [128, 8, TT], BF16)
    yg = sb("yg", [128, 8, TT], BF16)
    win = sb("win", [128, 8, WINC], BF16)
    wout = sb("wout", [128, 8, D], BF16)
    wuqA = sb("wuqA", [128, 3, 576], BF16)
    wuqB = sb("wuqB", [128, 3, 576], BF16)
    wuk = sb("wuk", [128, 2, 384], BF16)
    wuv = sb("wuv", [128, 2, 384], BF16)
    wrg = sb("wrg", [128, 3, 128], BF16)
    wig = sb("wig", [128, 3, 128], BF16)
    wpool = sb("wpool", [128, 2, 128], BF16)
    par = sb("par", [128, NPAR])
    emask = sb("emask_sb", [128, 896], BF16)
    ident = sb("ident_sb", [128, 128], BF16)
    Ig = sb("Ig", [128, NG, 128], BF16)
    ones_f = sb("ones_f", [128, 128])
    eps_t = sb("eps_t", [128, 1])
    sq = [sb(f"sq{i}", [128, TT]) for i in range(2)]
    rs = sb("rs", [128, TT])
    zapad = sb("zapad", [128, 3, TT + 3])
    zcpad = sb("zcpad", [128, 2, TT + 15])
    cqraw = sb("cqraw", [128, 3, TT])
    cqn = sb("cqn", [128, 3, TT], BF16)
    ckvraw = sb("ckvraw", [128, 2, TT])
    ckvn = sb("ckvn", [128, 2, TT], BF16)
    xa = sb("xa", [128, TT])
    xab = sb("xab", [128, TT], BF16)
    r_t = sb("r_t", [128, TT])
    i_t = sb("i_t", [128, TT])
    a_t = sb("a_t", [128, TT])
    a2_t = sb("a2_t", [128, TT])
    u_t = sb("u_t", [128, TT])
    hloc = sb("hloc", [128, 3, TT])
    Ab = sb("Ab", [128, 3, TT])
    zeros_t = sb("zeros_t", [128, TT])
    cpar = sb("cpar_sb", [128, NCPAR])
    s1 = sb("s1", [128, 64])
    H1 = sb("H1", [128, NG, 64])
    Hprev3 = sb("Hprev3", [128, 64])
    halo = sb("halo", [128, 64])
    s2 = sb("s2", [128, 8])
    C2 = sb("C2", [128, NG, 8])
    Sch = sb("Sch", [128, NG + 1, 3])
    carry = sb("carry", [128, 3])
    lsp = sb("lsp", [128, 12])
    nbias = sb("nbias", [128, 6])
    sA = sb("sA", [128, TT + 15])
    sB = sb("sB", [128, TT + 15])
    pooled = sb("pooled", [128, TT], BF16)
    tmp16 = sb("tmp16", [128, 16])
    QT = sb("QT", [128, 6, TT], BF16)
    cosb = sb("cosb", [128, TT])
    sinb = sb("sinb", [128, TT])
    t1, t2 = r_t, i_t
    kst = sb("kst", [128, 3, TT], BF16)
    krst = sb("krst", [128, TT], BF16)
    vst = sb("vst", [128, 2, 3, 4, 128], BF16)
    NR = 3
    kring = [sb(f"kring{i}", [128, 2 * TT], BF16) for i in range(NR)]
    vring = [sb(f"vring{i}", [128, 2 * TT], BF16) for i in range(NR)]
    NP = 4
    pring = [sb(f"pring{i}", [128, TT], BF16) for i in range(NP)]
    lsh, otmp = sq[0], sq[1]
    ps = [nc.psum_tensor(f"ps{i}", [128, TT], F32).__enter__() for i in range(8)]

    pe, act, dve, pool = nc.tensor, nc.scalar, nc.vector, nc.gpsimd
    state = dict(gen=0, sb=0, pr=0, ring=0, sqi=0)

    def gbank():
        if state.get("attn"):
            i = 5 + state["gen"] % 3
        else:
            i = state["gen"] % 8
        state["gen"] += 1
        return ps[i], ("ps", i)

    def P_(name, l=None, width=1, idx=0):
        o = _P[name] + (0 if l is None else l * width) + idx
        return par[:, o:o + 1]

    C.dma("sp", par[:], params_d, [], ["par"], "par")
    C.dma("sp", emask[:], emask_d, [], ["emask"], "emask")
    C.dma("sp", cpar[:], cpar_d, [], ["cpar"], "cpar")
    C.dma("sp", ident[:], ident_d, [], ["ident"], "ident")
    for jp in range(NG):
        C.op("dve", lambda: dve.tensor_scalar(out=Ig[:, jp, :], in0=ident[:], scalar1=cpar[:, CP_G + jp:CP_G + jp + 1],
                                              scalar2=None, op0=ALU.mult), ["ident", "cpar"], ["Ig"])
    C.op("dve", lambda: dve.memset(zeros_t[:], 0.0), [], ["zeros_t"])
    C.op("dve", lambda: dve.memset(xab[:], 0.0), [], ["xab"])
    for p_ in range(2):
        for m_ in range(nwave):
            for q_ in range(3):
                for hf in range(2):
                    C.dma("sp", send3[p_][m_][q_][96:128, hf * 512:(hf + 1) * 512], xab[96:128, :], ["xab"],
                          [("s3z", p_, m_, q_, hf)], "xab")
    C.op("dve", lambda: dve.memset(s1[:], 0.0), [], ["s1"])
    C.op("dve", lambda: dve.memset(s2[:], 0.0), [], ["s2"])
    C.op("dve", lambda: dve.memset(ones_f[:], 1.0), [], ["ones_f"])
    C.op("dve", lambda: dve.memset(eps_t[:], EPS), [], ["eps_t"])
    C.op("dve", lambda: dve.memset(vst[:, 0, :, :, 64:128], 1.0), [], ["vst"])
    C.op("dve", lambda: dve.memset(vst[:, 1, :, :, 0:64], 1.0), [], ["vst"])

    def sumsq_rstd(srcs, nfeat, rkeys):
        bank, bkey = gbank()
        n = len(srcs)
        for i, s_ap in enumerate(srcs):
            q = sq[state["sqi"] % 2]
            qk = ("sq", state["sqi"] % 2)
            state["sqi"] += 1
            rk = rkeys[i] if (len(rkeys) == n and isinstance(rkeys[0], list)) else rkeys
            C.op("act", lambda: act.activation(out=q[:], in_=s_ap, func=AF.Square), rk, [qk])
            C.op("pe", lambda: pe.matmul(bank[:], lhsT=ones_f[:], rhs=q[:], start=(i == 0), stop=(i == n - 1)),
                 [qk, "ones_f"], [bkey], sig=(i == n - 1) or True)
        C.op("act", lambda: act.activation(out=rs[:], in_=bank[:], func=AF.Ln, scale=1.0 / nfeat,
                                           bias=eps_t[:, 0:1]), [bkey, "eps_t"], ["rs"])
        C.op("act", lambda: act.activation(out=rs[:], in_=rs[:], func=AF.Exp, scale=-0.5), ["rs"], ["rs"])

    for l in range(depth):
        for kc in range(8):
            C.dma("pool", win[:, kc, :], w_in_d[l, :, kc, :], [], [("win", kc)], "win")
        for kc in range(8):
            C.dma("pool", wout[:, kc, :], w_out_d[l, :, kc, :], [], [("wout", kc)], "wout")
        for (t_sb, t_d, key) in [(wuqA, w_uqA_d, "wuqA"), (wuqB, w_uqB_d, "wuqB"), (wuk, w_uk_d, "wuk"),
                                 (wuv, w_uv_d, "wuv"), (wrg, w_rg_d, "wrg"), (wig, w_ig_d, "wig"),
                                 (wpool, w_pool_d, "wpool")]:
            C.dma("pool", t_sb[:], t_d[l], [], [key], key)
        C.op("act", lambda: act.activation(out=lsp[:, 0:3], in_=par[:, _P["lam"] + 3 * l:_P["lam"] + 3 * l + 3],
                                           func=AF.Exp, scale=-1.0), ["par"], ["lsp"])
        C.op("act", lambda: act.activation(out=lsp[:, 0:3], in_=lsp[:, 0:3], func=AF.Ln, bias=1.0),
             ["lsp"], ["lsp"])
        C.op("dve", lambda: dve.tensor_scalar(out=lsp[:, 3:6], in0=lsp[:, 0:3], scalar1=-8.0, scalar2=None,
                                              op0=ALU.mult), ["lsp"], ["lsp"])
        C.op("dve", lambda: dve.tensor_scalar(out=lsp[:, 6:9], in0=lsp[:, 0:3], scalar1=-16.0, scalar2=None,
                                              op0=ALU.mult), ["lsp"], ["lsp"])
        C.op("dve", lambda: dve.tensor_scalar(out=nbias[:, 0:3], in0=par[:, _P["brg"] + 3 * l:_P["brg"] + 3 * l + 3],
                                              scalar1=-1.0, scalar2=None, op0=ALU.mult), ["par", "nbias"], ["nbias"])
        C.op("dve", lambda: dve.tensor_scalar(out=nbias[:, 3:6], in0=par[:, _P["big"] + 3 * l:_P["big"] + 3 * l + 3],
                                              scalar1=-1.0, scalar2=None, op0=ALU.mult), ["par", "nbias"], ["nbias"])

        C.op("dve", lambda: dve.memset(Hprev3[:], 0.0), ["Hprev3"], ["Hprev3"])
        C.op("dve", lambda: dve.memset(Sch[:, 0, :], 0.0), ["Sch"], ["Sch"])
        for T in range(nwave):
            tok = slice(T * TT, (T + 1) * TT)
            pty = l % 2
            src = xT if l == 0 else xs[(l - 1) % 2]
            for kc in range(8):
                C.dma("sp", xt[:, kc, :], src[:, kc, tok], [("X", l, T, kc)], [("xt", kc)], f"xt{kc}")
            C.dma("sp", cosb[64:96, :], cos_d[:, tok], [], ["cosb"], "cosb")
            C.dma("sp", sinb[64:96, :], sin_d[:, tok], [], ["sinb"], "sinb")
            sumsq_rstd([xt[:, kc, :] for kc in range(8)], float(D), [[("xt", kc)] for kc in range(8)])
            for kc in range(8):
                C.op("dve", lambda: dve.scalar_tensor_tensor(out=hT[:, kc, :], in0=xt[:, kc, :],
                                                             scalar=P_("normg", l, 8, kc), in1=rs[:],
                                                             op0=ALU.mult, op1=ALU.mult),
                     [("xt", kc), "rs", "par"], [("hT", kc)])
            def inproj(col0, M):
                bank, bkey = gbank()
                for kc in range(8):
                    C.op("pe", lambda: pe.matmul(bank[0:M, :], lhsT=win[:, kc, col0:col0 + M], rhs=hT[:, kc, :],
                                                 start=(kc == 0), stop=(kc == 7)),
                         [("win", kc), ("hT", kc)], [bkey], sig=(kc == 7))
                return bank, bkey

            for c in range(3):
                bank, bkey = inproj(O_ZA + 128 * c, 128)
                C.op("act", lambda: act.activation(out=zapad[:, c, 3:TT + 3], in_=bank[:], func=AF.Copy),
                     [bkey], ["zapad"])
            for c in range(2):
                bank, bkey = inproj(O_ZC + 128 * c, 128)
                C.op("act", lambda: act.activation(out=zcpad[:, c, 15:TT + 15], in_=bank[:], func=AF.Copy),
                     [bkey], ["zcpad"])
            C.op("dve", lambda: dve.tensor_copy(out=s1[:, 0:9].rearrange("p (c k) -> p c k", k=3),
                                                in_=zapad[:, :, TT:TT + 3]), ["zapad"], ["s1"])
            C.op("dve", lambda: dve.tensor_copy(out=s1[:, 9:39].rearrange("p (c k) -> p c k", k=15),
                                                in_=zcpad[:, :, TT:TT + 15]), ["zcpad"], ["s1"])
            C.dma("sp", send1[T], s1[:], ["s1"], [("send1", T)], "s1")
            C.op("pool", lambda: pool.collective_compute("AllGather", ALU.bypass, replica_groups=GROUPS,
                                                         ins=[send1[T]], outs=[recv1[T]]),
                 [("send1", T)], [("recv1", T)])
            C.dma("pool", H1[:], recv1[T].rearrange("(r p) n -> p r n", p=128), [("recv1", T)], ["H1"], "H1")

            for c in range(2):
                bank, bkey = inproj(O_CKV + 128 * c, 128)
                C.op("act", lambda: act.activation(out=ckvraw[:, c, :], in_=bank[:], func=AF.Copy),
                     [bkey], ["ckvraw"])
            bankA, kA = inproj(O_KR - 64, 96)
            bankB, kB = inproj(O_KRS - 64, 96)
            C.op("dve", lambda: dve.tensor_tensor(out=t1[64:96, :], in0=bankA[64:96, :], in1=cosb[64:96, :], op=ALU.mult),
                 [kA, "cosb"], ["r_t"])
            C.op("dve", lambda: dve.tensor_tensor(out=t2[64:96, :], in0=bankB[64:96, :], in1=sinb[64:96, :], op=ALU.mult),
                 [kB, "sinb"], ["i_t"])
            C.op("dve", lambda: dve.tensor_tensor(out=krst[64:96, :], in0=t1[64:96, :], in1=t2[64:96, :], op=ALU.add),
                 ["r_t", "i_t"], ["krst"])
            for h in range(6):
                C.dma("sp", send3[pty][T][h // 2][64:96, (h % 2) * 512:(h % 2) * 512 + 512], krst[64:96, :], ["krst"],
                      [("s3", h // 2, "r", h % 2)], "krst")
            sumsq_rstd([ckvraw[:, c, :] for c in range(2)], 256.0, ["ckvraw"])
            for c in range(2):
                C.op("dve", lambda: dve.scalar_tensor_tensor(out=ckvn[:, c, :], in0=ckvraw[:, c, :],
                                                             scalar=P_("kvng", l, 2, c), in1=rs[:],
                                                             op0=ALU.mult, op1=ALU.mult),
                     ["ckvraw", "rs", "par"], ["ckvn"])
            for c in range(3):
                bank, bkey = inproj(O_CQ + 128 * c, 128)
                C.op("act", lambda: act.activation(out=cqraw[:, c, :], in_=bank[:], func=AF.Copy),
                     [bkey], ["cqraw"])
            for p in range(3):
                bank, bkey = gbank()
                for c in range(2):
                    C.op("pe", lambda: pe.matmul(bank[:], lhsT=wuk[:, c, p * 128:(p + 1) * 128], rhs=ckvn[:, c, :],
                                                 start=(c == 0), stop=(c == 1)), ["wuk", "ckvn"], [bkey], sig=(c == 1))
                C.op("act", lambda: act.activation(out=kst[:, p, :], in_=bank[:], func=AF.Copy), [bkey], ["kst"])
            for p in range(3):
                C.dma("sp", send3[pty][T][p][0:64, 0:512], kst[0:64, p, :], ["kst"], [("s3", p, "n", 0)], "kst")
                C.dma("sp", send3[pty][T][p][0:64, 512:1024], kst[64:128, p, :], ["kst"], [("s3", p, "n", 1)], "kst")
            for blk in range(4):
                bank, bkey = gbank()
                for c in range(2):
                    C.op("pe", lambda: pe.matmul(bank[:, 0:384], lhsT=ckvn[:, c, blk * 128:(blk + 1) * 128], rhs=wuv[:, c, :],
                                                 start=(c == 0), stop=(c == 1)), ["wuv", "ckvn"], [bkey], sig=(c == 1))
                C.op("act", lambda: act.activation(out=vst[:, 0, :, blk, 0:64],
                                                   in_=bank[:, 0:192].rearrange("p (a d) -> p a d", d=64), func=AF.Copy),
                     [bkey], ["vst"])
                C.op("act", lambda: act.activation(out=vst[:, 1, :, blk, 64:128],
                                                   in_=bank[:, 192:384].rearrange("p (a d) -> p a d", d=64), func=AF.Copy),
                     [bkey], ["vst"])
            for q in range(3):
                C.dma("sp", send3[pty][T][q][:, 1024:2048].rearrange("p (a c) -> p a c", a=2),
                      vst[:, :, q, :, :].rearrange("p a k d -> p a (k d)"), ["vst"], [("s3", q, "v")], "vst")
            for q in range(3):
                s3keys = [("s3", q, "r", 0), ("s3", q, "r", 1), ("s3", q, "n", 0), ("s3", q, "n", 1), ("s3", q, "v"),
                          ("s3z", pty, T, q, 0), ("s3z", pty, T, q, 1)]
                C.op("pool", lambda: pool.collective_compute("AllGather", ALU.bypass, replica_groups=GROUPS,
                                                             ins=[send3[pty][T][q]], outs=[recv3[pty][T][q]]),
                     s3keys, [("recv3", T, q)])

            sumsq_rstd([cqraw[:, c, :] for c in range(3)], 384.0, ["cqraw"])
            for c in range(3):
                C.op("dve", lambda: dve.scalar_tensor_tensor(out=cqn[:, c, :], in0=cqraw[:, c, :],
                                                             scalar=P_("qng", l, 3, c), in1=rs[:],
                                                             op0=ALU.mult, op1=ALU.mult),
                     ["cqraw", "rs", "par"], ["cqn"])
            for h in range(6):
                bankA, kA = gbank()
                for c in range(3):
                    C.op("pe", lambda: pe.matmul(bankA[0:96, :], lhsT=wuqA[:, c, h * 96:(h + 1) * 96], rhs=cqn[:, c, :],
                                                 start=(c == 0), stop=(c == 2)), ["wuqA", "cqn"], [kA], sig=(c == 2))
                bankB, kB = gbank()
                for c in range(3):
                    C.op("pe", lambda: pe.matmul(bankB[0:96, :], lhsT=wuqB[:, c, h * 96:(h + 1) * 96], rhs=cqn[:, c, :],
                                                 start=(c == 0), stop=(c == 2)), ["wuqB", "cqn"], [kB], sig=(c == 2))
                C.op("act", lambda: act.activation(out=QT[0:64, h, :], in_=bankA[0:64, :], func=AF.Copy),
                     [kA], [("QT", h)])
                C.op("dve", lambda: dve.tensor_tensor(out=t1[64:96, :], in0=bankA[64:96, :], in1=cosb[64:96, :], op=ALU.mult),
                     [kA, "cosb"], ["r_t"])
                C.op("dve", lambda: dve.tensor_tensor(out=t2[64:96, :], in0=bankB[64:96, :], in1=sinb[64:96, :], op=ALU.mult),
                     [kB, "sinb"], ["i_t"])
                C.op("dve", lambda: dve.tensor_tensor(out=QT[64:96, h, :], in0=t1[64:96, :], in1=t2[64:96, :], op=ALU.add),
                     ["r_t", "i_t"], [("QT", h)])
            for (goff, ych, n) in [(O_GA, 0, 3), (O_GB, 3, 3), (O_GC, 6, 2)]:
                for c in range(n):
                    bank, bkey = inproj(goff + 128 * c, 128)
                    C.op("act", lambda: act.activation(out=yg[:, ych + c, :], in_=bank[:], func=AF.Silu),
                         [bkey], [("yg", ych + c)])


            def side_gen():
                C.op("dve", lambda: dve.tensor_scalar(out=halo[:], in0=Hprev3[:], scalar1=cpar[:, CP_W + 3:CP_W + 4],
                                                      scalar2=None, op0=ALU.mult), ["Hprev3", "cpar"], ["halo"])
                yield
                for k in range(3):
                    C.op("dve", lambda: dve.scalar_tensor_tensor(out=halo[:], in0=H1[:, k, :],
                                                                 scalar=cpar[:, CP_W + k:CP_W + k + 1], in1=halo[:],
                                                                 op0=ALU.mult, op1=ALU.add), ["H1", "cpar", "halo"], ["halo"])
                    yield
                C.op("dve", lambda: dve.tensor_copy(out=Hprev3[:], in_=H1[:, 3, :]), ["H1", "halo"], ["Hprev3"])
                yield
                C.op("dve", lambda: dve.tensor_copy(out=zapad[:, :, 0:3], in_=halo[:, 0:9].rearrange("p (c k) -> p c k", k=3)),
                     ["halo", "s1"], ["zapad"])
                yield
                C.op("dve", lambda: dve.tensor_copy(out=zcpad[:, :, 0:15], in_=halo[:, 9:39].rearrange("p (c k) -> p c k", k=15)),
                     ["halo", "s1"], ["zcpad"])
                yield

                for c in range(3):
                    cw = _P["convw"] + l * 12 + c * 4
                    C.op("dve", lambda: dve.tensor_scalar(out=xa[:], in0=zapad[:, c, 0:TT], scalar1=par[:, cw:cw + 1],
                                                          scalar2=P_("convb", l, 3, c), op0=ALU.mult, op1=ALU.add),
                         ["zapad", "par"], ["xa"])
                    yield
                    for k in range(1, 4):
                        C.op("dve", lambda: dve.scalar_tensor_tensor(out=xa[:], in0=zapad[:, c, k:k + TT],
                                                                     scalar=par[:, cw + k:cw + k + 1], in1=xa[:],
                                                                     op0=ALU.mult, op1=ALU.add),
                             ["zapad", "par", "xa"], ["xa"])
                        yield
                    C.op("dve", lambda: dve.tensor_copy(out=xab[:], in_=xa[:]), ["xa"], ["xab"])
                    yield
                    bank_r, kr_ = gbank()
                    C.op("pe", lambda: pe.matmul(bank_r[:], lhsT=wrg[:, c, :], rhs=xab[:], start=True, stop=True),
                         ["wrg", "xab"], [kr_])
                    yield
                    bank_i, ki_ = gbank()
                    C.op("pe", lambda: pe.matmul(bank_i[:], lhsT=wig[:, c, :], rhs=xab[:], start=True, stop=True),
                         ["wig", "xab"], [ki_])
                    yield
                    for (dst, bnk, bk, col) in ((r_t, bank_r, kr_, c), (i_t, bank_i, ki_, 3 + c)):
                        dk = "r_t" if dst is r_t else "i_t"
                        C.op("act", lambda: act.activation(out=dst[:], in_=bnk[:], func=AF.Exp, scale=-1.0,
                                                           bias=nbias[:, col:col + 1]), [bk, "nbias"], [dk])
                        yield
                        C.op("act", lambda: act.activation(out=dst[:], in_=dst[:], func=AF.Ln, bias=1.0), [dk], [dk])
                        yield
                        C.op("act", lambda: act.activation(out=dst[:], in_=dst[:], func=AF.Exp, scale=-1.0), [dk], [dk])
                        yield
                    C.op("act", lambda: act.activation(out=a_t[:], in_=r_t[:], func=AF.Exp, scale=lsp[:, 3 + c:4 + c]),
                         ["r_t", "lsp"], ["a_t"])
                    yield
                    C.op("act", lambda: act.activation(out=a2_t[:], in_=r_t[:], func=AF.Exp, scale=lsp[:, 6 + c:7 + c]),
                         ["r_t", "lsp"], ["a2_t"])
                    yield
                    C.op("act", lambda: act.activation(out=a2_t[:], in_=a2_t[:], func=AF.Ln, scale=-1.0, bias=1.0),
                         ["a2_t"], ["a2_t"])
                    yield
                    C.op("act", lambda: act.activation(out=a2_t[:], in_=a2_t[:], func=AF.Exp, scale=0.5),
                         ["a2_t"], ["a2_t"])
                    yield
                    C.op("dve", lambda: dve.tensor_tensor(out=u_t[:], in0=i_t[:], in1=xa[:], op=ALU.mult),
                         ["i_t", "xa"], ["u_t"])
                    yield
                    C.op("dve", lambda: dve.tensor_tensor(out=u_t[:], in0=u_t[:], in1=a2_t[:], op=ALU.mult),
                         ["u_t", "a2_t"], ["u_t"])
                    yield
                    C.op("dve", lambda: dve.tensor_tensor_scan(out=hloc[:, c, :], data0=a_t[:], data1=u_t[:], initial=0.0,
                                                               op0=ALU.mult, op1=ALU.add),
                         ["a_t", "u_t"], [("hloc", c)])
                    yield
                    C.op("dve", lambda: dve.tensor_tensor_scan(out=Ab[:, c, :], data0=a_t[:], data1=zeros_t[:], initial=1.0,
                                                               op0=ALU.mult, op1=ALU.add),
                         ["a_t", "zeros_t"], [("Ab", c)])
                    yield
                    C.op("dve", lambda: dve.tensor_copy(out=s2[:, c:c + 1], in_=hloc[:, c, TT - 1:TT]), [("hloc", c)], ["s2"])
                    yield
                    C.op("dve", lambda: dve.tensor_copy(out=s2[:, 3 + c:4 + c], in_=Ab[:, c, TT - 1:TT]), [("Ab", c)], ["s2"])
                    yield

                C.dma("pool", send2[T], s2[:], ["s2"], [("send2", T)], "s2")
                yield
                C.op("pool", lambda: pool.collective_compute("AllGather", ALU.bypass, replica_groups=GROUPS,
                                                             ins=[send2[T]], outs=[recv2[T]]),
                     [("send2", T)], [("recv2", T)])
                yield
                C.dma("pool", C2[:], recv2[T].rearrange("(r p) n -> p r n", p=128), [("recv2", T)], ["C2"], "C2")
                yield
                for c in range(2):
                    z = zcpad[:, c, :]
                    W = TT + 15
                    C.op("dve", lambda: dve.tensor_tensor(out=sA[:, 1:W], in0=z[:, 1:W], in1=z[:, 0:W - 1], op=ALU.add),
                         ["zcpad"], ["sA"])
                    yield
                    C.op("dve", lambda: dve.tensor_tensor(out=sB[:, 3:W], in0=sA[:, 3:W], in1=sA[:, 1:W - 2], op=ALU.add),
                         ["sA"], ["sB"])
                    yield
                    if c == 0:
                        lo, hi = sA, sB
                    else:
                        C.op("dve", lambda: dve.tensor_tensor(out=sA[:, 7:W], in0=sB[:, 7:W], in1=sB[:, 3:W - 4], op=ALU.add),
                             ["sB"], ["sA"])
                        yield
                        C.op("dve", lambda: dve.tensor_tensor(out=sB[:, 15:W], in0=sA[:, 15:W], in1=sA[:, 7:W - 8], op=ALU.add),
                             ["sA"], ["sB"])
                        yield
                        lo, hi = sA, sB
                    iw = _P["invw"] + c
                    for (p0, p1, stg) in [(0, 64, lo), (64, 128, hi)]:
                        C.op("dve", lambda: dve.scalar_tensor_tensor(out=pooled[p0:p1, :], in0=stg[p0:p1, 15:W],
                                                                     scalar=par[p0:p1, iw:iw + 1], in1=z[p0:p1, 15:W],
                                                                     op0=ALU.mult, op1=ALU.subtract),
                             ["sA", "sB", "zcpad", "par"], ["pooled"])
                        yield
                        if T == 0:
                            it = CP_IT + 16 * c
                            C.op("dve", lambda: dve.tensor_tensor(out=tmp16[p0:p1, :], in0=stg[p0:p1, 15:31],
                                                                  in1=cpar[p0:p1, it:it + 16], op=ALU.mult),
                                 ["sA", "sB", "cpar"], ["tmp16"])
                            yield
                            C.op("dve", lambda: dve.tensor_tensor(out=pooled[p0:p1, 0:16], in0=tmp16[p0:p1, :],
                                                                  in1=z[p0:p1, 15:31], op=ALU.subtract),
                                 ["tmp16", "zcpad"], ["pooled"])
                            yield
                    bank, bkey = gbank()
                    C.op("pe", lambda: pe.matmul(bank[:], lhsT=wpool[:, c, :], rhs=pooled[:], start=True, stop=True),
                         ["wpool", "pooled"], [bkey])
                    yield
                    C.op("dve", lambda: dve.scalar_tensor_tensor(out=yg[:, 6 + c, :], in0=bank[:],
                                                                 scalar=P_("pscale", l, 2, c), in1=yg[:, 6 + c, :],
                                                                 op0=ALU.mult, op1=ALU.mult),
                         [bkey, "par", ("yg", 6 + c)], [("yg", 6 + c)])
                    yield


            steps = []
            for ph, mps in ((1, list(range(T))), (2, [T])):
                for q in range(3):
                    kts = [(mp, jp) for mp in mps for jp in range(NG)]
                    for n_, kt in enumerate(kts):
                        for hh in range(2):
                            for kb in range(4):
                                steps.append((2 * q + hh, kt, kb, n_ == 0 and kb == 0, n_ == len(kts) - 1 and kb == 3, ph))
            part = [cqraw[:, 0, :], cqraw[:, 1, :], cqraw[:, 2, :], ckvraw[:, 0, :], ckvraw[:, 1, :], rs[:]]
            pkey = ["cqraw", "cqraw", "cqraw", "ckvraw", "ckvraw", "rs"]
            LA = 2
            info = {}

            def emit_qk(i):
                h, kt, kb, first, last, ph = steps[i]
                mp, jp = kt
                par_, pair = h % 2, h // 2
                if kb == 0 and par_ == 0:
                    slot = state["ring"] % NR
                    state["ring"] += 1
                    rkey = ("ring", slot)
                    src = recv3[pty][mp][pair]
                    C.dma("sp", kring[slot][0:96, :], src[jp * 128:jp * 128 + 96, 0:1024],
                          [("recv3", mp, pair)], [rkey], f"ring{slot}")
                    C.dma("sp", vring[slot][:], src[jp * 128:(jp + 1) * 128, 1024:2048],
                          [("recv3", mp, pair)], [rkey], f"ring{slot}")
                    info[(pair, kt, ph)] = slot
                slot = info[(h // 2, kt, ph)]
                rkey = ("ring", slot)
                si = 2 + state["sb"] % 3
                state["sb"] += 1
                pi = state["pr"] % NP
                state["pr"] += 1
                info[i] = (si, pi)
                if mp == T:
                    C.op("pe", lambda: pe.matmul(ps[si][:], lhsT=kring[slot][0:96, par_ * 512 + kb * 128:par_ * 512 + (kb + 1) * 128],
                                                 rhs=QT[0:96, h, :], start=True, stop=False),
                         [rkey, ("QT", h)], [("ps", si)], sig=False)
                    C.op("pe", lambda: pe.matmul(ps[si][:], lhsT=Ig[:, jp, :], rhs=emask[:, 384 - 128 * kb:896 - 128 * kb],
                                                 start=False, stop=True),
                         ["Ig", "emask"], [("ps", si)])
                    C.op("act", lambda: act.activation(out=pring[pi][:], in_=ps[si][:], func=AF.Exp, scale=SCALE,
                                                       bias=cpar[:, CP_EB + jp:CP_EB + jp + 1]),
                         [("ps", si), "cpar"], [("P", pi)])
                else:
                    C.op("pe", lambda: pe.matmul(ps[si][:], lhsT=kring[slot][0:96, par_ * 512 + kb * 128:par_ * 512 + (kb + 1) * 128],
                                                 rhs=QT[0:96, h, :], start=True, stop=True),
                         [rkey, ("QT", h)], [("ps", si)])
                    C.op("act", lambda: act.activation(out=pring[pi][:], in_=ps[si][:], func=AF.Exp, scale=SCALE),
                         [("ps", si)], [("P", pi)])

            def emit_pv(i):
                h, kt, kb, first, last, ph = steps[i]
                par_, pair = h % 2, h // 2
                slot = info[(h // 2, kt, ph)]
                rkey = ("ring", slot)
                si, pi = info[i]
                ob = ps[h % 2]
                okey = ("ps", h % 2)
                C.op("pe", lambda: pe.matmul(ob[:], lhsT=vring[slot][:, par_ * 512 + kb * 128:par_ * 512 + (kb + 1) * 128], rhs=pring[pi][:],
                                             start=first, stop=last),
                     [rkey, ("P", pi)], [okey])
                if last and ph == 1:
                    C.op("dve", lambda: dve.tensor_copy(out=part[h], in_=ob[:]), [okey, pkey[h]], [pkey[h]])
                if last and ph == 2:
                    if T > 0:
                        C.op("dve", lambda: dve.tensor_tensor(out=part[h], in0=ob[:], in1=part[h], op=ALU.add),
                             [okey, pkey[h]], [pkey[h]])
                        src, skey = part[h], pkey[h]
                    else:
                        src, skey = ob, okey
                    if par_ == 0:
                        o0, o1, l0, l1 = 0, 64, 64, 128
                    else:
                        o0, o1, l0, l1 = 64, 128, 0, 64
                    C.op("dve", lambda: dve.tensor_copy(out=lsh[o0:o1, :], in_=src[l0:l1, :]), [skey], [("sq", 0)])
                    C.op("dve", lambda: dve.reciprocal(out=lsh[o0:o1, :], in_=lsh[o0:o1, :]), [("sq", 0)], [("sq", 0)])
                    C.op("dve", lambda: dve.tensor_tensor(out=otmp[o0:o1, :], in0=src[o0:o1, :], in1=lsh[o0:o1, :],
                                                          op=ALU.mult), [skey, ("sq", 0)], [("sq", 1)])
                    C.op("dve", lambda: dve.tensor_tensor(out=yg[o0:o1, 3 + pair, :], in0=otmp[o0:o1, :],
                                                          in1=yg[o0:o1, 3 + pair, :], op=ALU.mult),
                         [("sq", 1), ("yg", 3 + pair)], [("yg", 3 + pair)])

            nst = len(steps)
            state["attn"] = True
            sg = side_gen()
            for i in range(nst + LA):
                if i < nst:
                    emit_qk(i)
                for _ in range(3):
                    next(sg, None)
                if i >= LA:
                    emit_pv(i - LA)
            for _ in sg:
                pass
            state["attn"] = False

            for k in range(NG):
                C.op("dve", lambda: dve.tensor_tensor(out=Sch[:, k + 1, :], in0=C2[:, k, 3:6], in1=Sch[:, k, :], op=ALU.mult),
                     ["C2", "Sch"], ["Sch"])
                C.op("dve", lambda: dve.tensor_tensor(out=Sch[:, k + 1, :], in0=Sch[:, k + 1, :], in1=C2[:, k, 0:3], op=ALU.add),
                     ["C2", "Sch"], ["Sch"])
            C.op("dve", lambda: dve.tensor_scalar(out=carry[:], in0=Sch[:, 0, :], scalar1=cpar[:, CP_W + 3:CP_W + 4],
                                                  scalar2=None, op0=ALU.mult), ["Sch", "cpar"], ["carry"])
            for k in range(3):
                C.op("dve", lambda: dve.scalar_tensor_tensor(out=carry[:], in0=Sch[:, k + 1, :],
                                                             scalar=cpar[:, CP_W + k:CP_W + k + 1], in1=carry[:],
                                                             op0=ALU.mult, op1=ALU.add), ["Sch", "cpar", "carry"], ["carry"])
            C.op("dve", lambda: dve.tensor_copy(out=Sch[:, 0, :], in_=Sch[:, NG, :]), ["Sch", "carry"], ["Sch"])
            for c in range(3):
                C.op("dve", lambda: dve.scalar_tensor_tensor(out=hloc[:, c, :], in0=Ab[:, c, :], scalar=carry[:, c:c + 1],
                                                             in1=hloc[:, c, :], op0=ALU.mult, op1=ALU.add),
                     [("Ab", c), ("hloc", c), "carry"], [("hloc", c)])
                C.op("dve", lambda: dve.tensor_tensor(out=yg[:, c, :], in0=hloc[:, c, :], in1=yg[:, c, :], op=ALU.mult),
                     [("hloc", c), ("yg", c)], [("yg", c)])

            early, late = [6, 7, 3, 4], [5, 0, 1, 2]
            obanks = [gbank() for _ in range(8)]
            for oc in range(8):
                bank, bkey = obanks[oc]
                for n_, kc in enumerate(early):
                    C.op("pe", lambda: pe.matmul(bank[:], lhsT=wout[:, kc, oc * 128:(oc + 1) * 128], rhs=yg[:, kc, :],
                                                 start=(n_ == 0), stop=False),
                         [("wout", kc), ("yg", kc)], [bkey], sig=(n_ == len(early) - 1))
            for oc in range(8):
                bank, bkey = obanks[oc]
                for n_, kc in enumerate(late):
                    C.op("pe", lambda: pe.matmul(bank[:], lhsT=wout[:, kc, oc * 128:(oc + 1) * 128], rhs=yg[:, kc, :],
                                                 start=False, stop=(n_ == len(late) - 1)),
                         [("wout", kc), ("yg", kc)], [bkey], sig=(n_ == len(late) - 1))
                C.op("dve", lambda: dve.tensor_tensor(out=xt[:, oc, :], in0=xt[:, oc, :], in1=bank[:], op=ALU.add),
                     [("xt", oc), bkey], [("xt", oc)])
                if l < depth - 1:
                    C.dma("sp", xs[l % 2][:, oc, tok], xt[:, oc, :], [("xt", oc)], [("X", l + 1, T, oc)], f"xt{oc}")
            if l == depth - 1:
                sumsq_rstd([xt[:, kc, :] for kc in range(8)], float(D), [[("xt", kc)] for kc in range(8)])
                for kc in range(8):
                    C.op("dve", lambda: dve.scalar_tensor_tensor(out=xt[:, kc, :], in0=xt[:, kc, :],
                                                                 scalar=P_("fng", None, 1, kc), in1=rs[:],
                                                                 op0=ALU.mult, op1=ALU.mult),
                         [("xt", kc), "rs", "par"], [("xt", kc)])
                    C.dma("sp", outT[:, kc, tok], xt[:, kc, :], [("xt", kc)], [("OUT", T, kc)], f"xt{kc}")
    C.finish("sp")
    return nc


def _prep_shared(inp):
    f = np.float32
    w_in = np.asarray(inp["w_in"], f)
    offs = np.cumsum([0, 384, 384, 384, 256, 32, 384, 256, 256])
    za, ga, cq, ckv, kr, gb, zc, gc = [np.arange(offs[i], offs[i + 1]) for i in range(8)]
    krs = np.concatenate([kr[16:32], kr[0:16]])
    perm = np.concatenate([za, ga, cq, ckv, gb, zc, gc, kr, krs])
    assert perm.size == WINC

    def kmaj(w, nk):
        L, _, N = w.shape
        return np.ascontiguousarray(w.reshape(L, nk, 128, N).transpose(0, 2, 1, 3))

    d = {}
    d["w_in"] = kmaj(w_in[:, :, perm], 8)
    d["w_out"] = kmaj(np.asarray(inp["w_out"], f), 8)
    w_uq = np.asarray(inp["w_uq"], f).reshape(DEPTH, 384, 6, 96)
    nope, rope = w_uq[..., :64], w_uq[..., 64:]
    A = np.concatenate([nope, rope], axis=-1).reshape(DEPTH, 384, 576)
    Bm = np.concatenate([nope, rope[..., 16:], rope[..., :16]], axis=-1).reshape(DEPTH, 384, 576)
    d["w_uqA"] = kmaj(A, 3)
    d["w_uqB"] = kmaj(Bm, 3)
    w_ukv = np.asarray(inp["w_ukv"], f).reshape(DEPTH, 256, 6, 128)
    d["w_uk"] = kmaj(np.ascontiguousarray(w_ukv[..., :64]).reshape(DEPTH, 256, 384), 2)
    v = w_ukv[..., 64:]
    v = np.concatenate([v[:, :, 0::2, :], v[:, :, 1::2, :]], axis=2).reshape(DEPTH, 256, 384)
    d["w_uv"] = kmaj(v, 2)

    def bdiag(w, n):
        L = w.shape[0]
        o = np.zeros((L, 128, n, 128), f)
        for c in range(n):
            o[:, 0:64, c, 0:64] = w[:, 2 * c]
            o[:, 64:128, c, 64:128] = w[:, 2 * c + 1]
        return o

    d["w_rg"] = bdiag(np.asarray(inp["w_rg"], f), 3)
    d["w_ig"] = bdiag(np.asarray(inp["w_ig"], f), 3)
    d["w_pool"] = bdiag(np.asarray(inp["w_pool"], f), 2)

    par = np.zeros((128, NPAR), f)

    def put(name, arr):
        a = np.asarray(arr, f)
        lead = a.shape[:-1]
        nch = a.shape[-1] // 128
        a = a.reshape(lead + (nch, 128))
        a = np.moveaxis(a, -1, 0).reshape(128, -1)
        par[:, _P[name]:_P[name] + a.shape[1]] = a

    put("normg", inp["norm_g"])
    put("fng", inp["final_norm_g"])
    cw = np.asarray(inp["conv_w"], f)
    cw = cw.reshape(DEPTH, 4, 3, 128).transpose(3, 0, 2, 1).reshape(128, DEPTH * 12)
    par[:, _P["convw"]:_P["convw"] + DEPTH * 12] = cw
    put("convb", inp["conv_b"])
    put("brg", inp["b_rg"])
    put("big", inp["b_ig"])
    put("lam", inp["lru_lambda"])
    put("qng", inp["q_norm_g"])
    put("kvng", inp["kv_norm_g"])
    put("pscale", inp["pool_scale"])
    wins = np.array([[2, 4], [8, 16]], f)
    for c in range(2):
        for hf in range(2):
            p0 = 64 * hf
            par[p0:p0 + 64, _P["invw"] + c] = f(1.0) / wins[c, hf]
    d["params"] = par
    cc = np.arange(896)[None, :] - 384
    d["emask"] = np.where(np.arange(128)[:, None] <= cc, 0.0, -30000.0).astype(ml_dtypes.bfloat16)
    d["ident"] = np.eye(128, dtype=np.float32).astype(ml_dtypes.bfloat16)
    return d


def _prep_core(j):
    f = np.float32
    d = {}
    pos = np.concatenate([np.arange((NG * m + j) * TT, (NG * m + j + 1) * TT) for m in range(NW)]).astype(f)
    inv_freq = (f(10000.0) ** (-np.arange(0, 32, 2, dtype=f) / f(32))).astype(f)
    ang = (pos[None, :] * inv_freq[:, None]).astype(f)
    cs, sn = np.cos(ang).astype(f), np.sin(ang).astype(f)
    d["cosT"] = np.ascontiguousarray(np.concatenate([cs, cs], 0))
    d["sinT"] = np.ascontiguousarray(np.concatenate([-sn, sn], 0))
    cp = np.zeros((128, NCPAR), f)
    cp[:, CP_W + (j - 1) % NG] = 1.0
    for jp in range(NG):
        cp[:, CP_EB + jp] = 0.0 if jp <= j else -30000.0
        cp[:, CP_F + jp] = 1.0 if jp < j else 0.0
        cp[:, CP_G + jp] = 1.0 if jp == j else 0.0
    wins = np.array([[2, 4], [8, 16]], f)
    t = np.arange(16, dtype=f)
    for c in range(2):
        for hf in range(2):
            p0 = 64 * hf
            if j == 0:
                cp[p0:p0 + 64, CP_IT + 16 * c:CP_IT + 16 * c + 16] = f(1.0) / np.minimum(t + 1, wins[c, hf])
            else:
                cp[p0:p0 + 64, CP_IT + 16 * c:CP_IT + 16 * c + 16] = f(1.0) / wins[c, hf]
    d["cpar"] = cp
    return d


_CACHE = {}


def kernel(**inputs):
    x = np.asarray(inputs["x"], np.float32)
    shared = _prep_shared(inputs)
    if "nc" not in _CACHE:
        _CACHE["nc"] = build_program()
    nc = _CACHE["nc"]
    in_maps = []
    for c in range(NCORE):
        b, j = divmod(c, NG)
        m = dict(shared)
        m.update(_prep_core(j))
        xb = x[b].reshape(NW, NG, TT, 8, 128)[:, j]
        m["xT"] = np.ascontiguousarray(xb.transpose(3, 2, 0, 1).reshape(128, 8, TOKC))
        in_maps.append(m)
    res = run_bass_kernel_spmd(nc, in_maps, core_ids=list(range(NCORE)))
    out = np.empty((B, NW, NG, TT, D), np.float32)
    for c in range(NCORE):
        b, j = divmod(c, NG)
        o = np.asarray(res.results[c]["outT"], np.float32).reshape(128, 8, NW, TT)
        out[b, :, j] = o.transpose(2, 3, 1, 0).reshape(NW, TT, D)
    return out.reshape(B, S, D)
```

```python
import numpy as np
import ml_dtypes
import concourse.bass as bass
import concourse.mybir as mybir
from concourse.bass_utils import run_bass_kernel_spmd

F32 = mybir.dt.float32
BF16 = mybir.dt.bfloat16
AF = mybir.ActivationFunctionType
ALU = mybir.AluOpType

DEPTH = 4
D = 1024
S = 8192
B = 2
TT = 512
NTILE = S // TT
NG = 4
NW = NTILE // NG
TOKC = NW * TT
NCORE = B * NG
R3 = 6 * 96 + 128 * 6
GROUPS = [[0, 1, 2, 3], [4, 5, 6, 7]]
EPS = 1e-6
SCALE = 96.0 ** -0.5
WINC = 2368
O_ZA, O_GA, O_CQ, O_CKV, O_GB, O_ZC, O_GC, O_KR, O_KRS = 0, 384, 768, 1152, 1408, 1792, 2048, 2304, 2336

_P = {}
_off = 0
for _n, _w in [("normg", DEPTH * 8), ("fng", 8), ("convw", DEPTH * 12), ("convb", DEPTH * 3),
               ("brg", DEPTH * 3), ("big", DEPTH * 3), ("lam", DEPTH * 3), ("qng", DEPTH * 3),
               ("kvng", DEPTH * 2), ("pscale", DEPTH * 2), ("invw", 2)]:
    _P[_n] = _off
    _off += _w
NPAR = _off
CP_W, CP_EB, CP_F, CP_IT, CP_G, NCPAR = 0, 4, 8, 12, 44, 48


class Ctx:
    def __init__(self, nc):
        self.nc = nc
        self.sems = {}
        self.eng = {}
        for name, h in [("pe", nc.tensor), ("act", nc.scalar), ("dve", nc.vector),
                        ("pool", nc.gpsimd), ("sp", nc.sync)]:
            self.eng[name] = dict(h=h, sem="e_" + name, count=0, waited={})
            self._sem("e_" + name)
        self.res_w = {}
        self.res_r = {}
        self.dcount = {}

    def _sem(self, name):
        if name not in self.sems:
            self.sems[name] = self.nc.semaphore(name).__enter__()
        return self.sems[name]

    def _deps(self, reads, writes):
        deps = []
        for k in reads:
            if k in self.res_w:
                deps.append(self.res_w[k])
        for k in writes:
            if k in self.res_w:
                deps.append(self.res_w[k])
            deps.extend(self.res_r.get(k, ()))
        return deps

    def _wait(self, e, deps):
        E = self.eng[e]
        need = {}
        for (sem, val, src) in deps:
            if src == "pe" and e == "pe":
                continue
            if src == "dma":
                val = max(val, self.dcount[sem])
            if val > need.get(sem, 0):
                need[sem] = val
        for sem, val in need.items():
            if E["waited"].get(sem, 0) < val:
                E["h"].wait_ge(self.sems[sem], val)
                E["waited"][sem] = val

    def _record(self, tok, reads, writes):
        for k in reads:
            self.res_r.setdefault(k, []).append(tok)
        for k in writes:
            self.res_w[k] = tok
            self.res_r[k] = []

    def op(self, e, fn, reads=(), writes=(), sig=True):
        self._wait(e, self._deps(reads, writes))
        E = self.eng[e]
        ins = fn()
        if sig:
            E["count"] += 1
            ins.then_inc(self.sems[E["sem"]], 1)
            tok = (E["sem"], E["count"], e)
        else:
            tok = (E["sem"], E["count"] + 1, e)
        self._record(tok, reads, writes)

    def dma(self, q, out, in_, reads, writes, semkey):
        self._wait(q, self._deps(reads, writes))
        name = "d_" + semkey
        self._sem(name)
        self.dcount[name] = self.dcount.get(name, 0) + 16
        self.eng[q]["h"].dma_start(out=out, in_=in_).then_inc(self.sems[name], 16)
        self._record((name, self.dcount[name], "dma"), reads, writes)

    def finish(self, e):
        deps = list(self.res_w.values())
        for v in self.res_r.values():
            deps.extend(v)
        self._wait(e, deps)


def build_program(depth=DEPTH, nwave=NW, dbg=False):
    nc = bass.Bass("TRN2", target_bir_lowering=False)
    C = Ctx(nc)

    def din(name, shape, dt=F32):
        return nc.dram_tensor(name, shape, dt, kind="ExternalInput").ap()

    xT = din("xT", [128, 8, TOKC])
    w_in_d = din("w_in", [DEPTH, 128, 8, WINC])
    w_out_d = din("w_out", [DEPTH, 128, 8, D])
    w_uqA_d = din("w_uqA", [DEPTH, 128, 3, 576])
    w_uqB_d = din("w_uqB", [DEPTH, 128, 3, 576])
    w_uk_d = din("w_uk", [DEPTH, 128, 2, 384])
    w_uv_d = din("w_uv", [DEPTH, 128, 2, 384])
    w_rg_d = din("w_rg", [DEPTH, 128, 3, 128])
    w_ig_d = din("w_ig", [DEPTH, 128, 3, 128])
    w_pool_d = din("w_pool", [DEPTH, 128, 2, 128])
    params_d = din("params", [128, NPAR])
    cos_d = din("cosT", [32, TOKC])
    sin_d = din("sinT", [32, TOKC])
    cpar_d = din("cpar", [128, NCPAR])
    emask_d = din("emask", [128, 896], BF16)
    ident_d = din("ident", [128, 128], BF16)
    outT = nc.dram_tensor("outT", [128, 8, TOKC], F32, kind="ExternalOutput").ap()
    xs = [nc.dram_tensor(f"xs{i}", [128, 8, TOKC], F32).ap() for i in range(2)]
    send1 = [nc.dram_tensor(f"send1_{m}", [128, 64], F32).ap() for m in range(NW)]
    recv1 = [nc.dram_tensor(f"recv1_{m}", [NG * 128, 64], F32).ap() for m in range(NW)]
    send2 = [nc.dram_tensor(f"send2_{m}", [128, 8], F32).ap() for m in range(NW)]
    recv2 = [nc.dram_tensor(f"recv2_{m}", [NG * 128, 8], F32).ap() for m in range(NW)]
    send3 = [[[nc.dram_tensor(f"send3_{p}_{m}_{q}", [128, 2048], BF16).ap() for q in range(3)]
              for m in range(NW)] for p in range(2)]
    recv3 = [[[nc.dram_tensor(f"recv3_{p}_{m}_{q}", [NG * 128, 2048], BF16).ap() for q in range(3)]
              for m in range(NW)] for p in range(2)]

    def sb(name, shape, dt=F32):
        return nc.sbuf_tensor(name, shape, dt).__enter__()

    xt = sb("xt", [128, 8, TT])
    hT = sb("hT", [128, 8, TT], BF16)
    yg = sb("yg", [128, 8, TT], BF16)
    win = sb("win", [128, 8, WINC], BF16)
    wout = sb("wout", [128, 8, D], BF16)
    wuqA = sb("wuqA", [128, 3, 576], BF16)
    wuqB = sb("wuqB", [128, 3, 576], BF16)
    wuk = sb("wuk", [128, 2, 384], BF16)
    wuv = sb("wuv", [128, 2, 384], BF16)
    wrg = sb("wrg", [128, 3, 128], BF16)
    wig = sb("wig", [128, 3, 128], BF16)
    wpool = sb("wpool", [128, 2, 128], BF16)
    par = sb("par", [128, NPAR])
    emask = sb("emask_sb", [128, 896], BF16)
    ident = sb("ident_sb", [128, 128], BF16)
    Ig = sb("Ig", [128, NG, 128], BF16)
    ones_f = sb("ones_f", [128, 128])
    eps_t = sb("eps_t", [128, 1])
    sq = [sb(f"sq{i}", [128, TT]) for i in range(2)]
    rs = sb("rs", [128, TT])
    zapad = sb("zapad", [128, 3, TT + 3])
    zcpad = sb("zcpad", [128, 2, TT + 15])
    cqraw = sb("cqraw", [128, 3, TT])
    cqn = sb("cqn", [128, 3, TT], BF16)
    ckvraw = sb("ckvraw", [128, 2, TT])
    ckvn = sb("ckvn", [128, 2, TT], BF16)
    xa = sb("xa", [128, TT])
    xab = sb("xab", [128, TT], BF16)
    r_t = sb("r_t", [128, TT])
    i_t = sb("i_t", [128, TT])
    a_t = sb("a_t", [128, TT])
    a2_t = sb("a2_t", [128, TT])
    u_t = sb("u_t", [128, TT])
    hloc = sb("hloc", [128, 3, TT])
    Ab = sb("Ab", [128, 3, TT])
    zeros_t = sb("zeros_t", [128, TT])
    cpar = sb("cpar_sb", [128, NCPAR])
    s1 = sb("s1", [128, 64])
    H1 = sb("H1", [128, NG, 64])
    Hprev3 = sb("Hprev3", [128, 64])
    halo = sb("halo", [128, 64])
    s2 = sb("s2", [128, 8])
    C2 = sb("C2", [128, NG, 8])
    Sch = sb("Sch", [128, NG + 1, 3])
    carry = sb("carry", [128, 3])
    lsp = sb("lsp", [128, 12])
    nbias = sb("nbias", [128, 6])
    sA = sb("sA", [128, TT + 15])
    sB = sb("sB", [128, TT + 15])
    pooled = sb("pooled", [128, TT], BF16)
    tmp16 = sb("tmp16", [128, 16])
    QT = sb("QT", [128, 6, TT], BF16)
    cosb = sb("cosb", [128, TT])
    sinb = sb("sinb", [128, TT])
    t1, t2 = r_t, i_t
    kst = sb("kst", [128, 3, TT], BF16)
    krst = sb("krst", [128, TT], BF16)
    vst = sb("vst", [128, 2, 3, 4, 128], BF16)
    NR = 3
    kring = [sb(f"kring{i}", [128, 2 * TT], BF16) for i in range(NR)]
    vring = [sb(f"vring{i}", [128, 2 * TT], BF16) for i in range(NR)]
    NP = 4
    pring = [sb(f"pring{i}", [128, TT], BF16) for i in range(NP)]
    lsh, otmp = sq[0], sq[1]
    ps = [nc.psum_tensor(f"ps{i}", [128, TT], F32).__enter__() for i in range(8)]

    pe, act, dve, pool = nc.tensor, nc.scalar, nc.vector, nc.gpsimd
    state = dict(gen=0, sb=0, pr=0, ring=0, sqi=0)

    def gbank():
        if state.get("attn"):
            i = 5 + state["gen"] % 3
        else:
            i = state["gen"] % 8
        state["gen"] += 1
        return ps[i], ("ps", i)

    def P_(name, l=None, width=1, idx=0):
        o = _P[name] + (0 if l is None else l * width) + idx
        return par[:, o:o + 1]

    C.dma("sp", par[:], params_d, [], ["par"], "par")
    C.dma("sp", emask[:], emask_d, [], ["emask"], "emask")
    C.dma("sp", cpar[:], cpar_d, [], ["cpar"], "cpar")
    C.dma("sp", ident[:], ident_d, [], ["ident"], "ident")
    for jp in range(NG):
        C.op("dve", lambda: dve.tensor_scalar(out=Ig[:, jp, :], in0=ident[:], scalar1=cpar[:, CP_G + jp:CP_G + jp + 1],
                                              scalar2=None, op0=ALU.mult), ["ident", "cpar"], ["Ig"])
    C.op("dve", lambda: dve.memset(zeros_t[:], 0.0), [], ["zeros_t"])
    C.op("dve", lambda: dve.memset(xab[:], 0.0), [], ["xab"])
    for p_ in range(2):
        for m_ in range(nwave):
            for q_ in range(3):
                for hf in range(2):
                    C.dma("sp", send3[p_][m_][q_][96:128, hf * 512:(hf + 1) * 512], xab[96:128, :], ["xab"],
                          [("s3z", p_, m_, q_, hf)], "xab")
    C.op("dve", lambda: dve.memset(s1[:], 0.0), [], ["s1"])
    C.op("dve", lambda: dve.memset(s2[:], 0.0), [], ["s2"])
    C.op("dve", lambda: dve.memset(ones_f[:], 1.0), [], ["ones_f"])
    C.op("dve", lambda: dve.memset(eps_t[:], EPS), [], ["eps_t"])
    C.op("dve", lambda: dve.memset(vst[:, 0, :, :, 64:128], 1.0), [], ["vst"])
    C.op("dve", lambda: dve.memset(vst[:, 1, :, :, 0:64], 1.0), [], ["vst"])

    def sumsq_rstd(srcs, nfeat, rkeys):
        bank, bkey = gbank()
        n = len(srcs)
        for i, s_ap in enumerate(srcs):
            q = sq[state["sqi"] % 2]
            qk = ("sq", state["sqi"] % 2)
            state["sqi"] += 1
            rk = rkeys[i] if (len(rkeys) == n and isinstance(rkeys[0], list)) else rkeys
            C.op("act", lambda: act.activation(out=q[:], in_=s_ap, func=AF.Square), rk, [qk])
            C.op("pe", lambda: pe.matmul(bank[:], lhsT=ones_f[:], rhs=q[:], start=(i == 0), stop=(i == n - 1)),
                 [qk, "ones_f"], [bkey], sig=(i == n - 1) or True)
        C.op("act", lambda: act.activation(out=rs[:], in_=bank[:], func=AF.Ln, scale=1.0 / nfeat,
                                           bias=eps_t[:, 0:1]), [bkey, "eps_t"], ["rs"])
        C.op("act", lambda: act.activation(out=rs[:], in_=rs[:], func=AF.Exp, scale=-0.5), ["rs"], ["rs"])

    for l in range(depth):
        for kc in range(8):
            C.dma("pool", win[:, kc, :], w_in_d[l, :, kc, :], [], [("win", kc)], "win")
        for kc in range(8):
            C.dma("pool", wout[:, kc, :], w_out_d[l, :, kc, :], [], [("wout", kc)], "wout")
        for (t_sb, t_d, key) in [(wuqA, w_uqA_d, "wuqA"), (wuqB, w_uqB_d, "wuqB"), (wuk, w_uk_d, "wuk"),
                                 (wuv, w_uv_d, "wuv"), (wrg, w_rg_d, "wrg"), (wig, w_ig_d, "wig"),
                                 (wpool, w_pool_d, "wpool")]:
            C.dma("pool", t_sb[:], t_d[l], [], [key], key)
        C.op("act", lambda: act.activation(out=lsp[:, 0:3], in_=par[:, _P["lam"] + 3 * l:_P["lam"] + 3 * l + 3],
                                           func=AF.Exp, scale=-1.0), ["par"], ["lsp"])
        C.op("act", lambda: act.activation(out=lsp[:, 0:3], in_=lsp[:, 0:3], func=AF.Ln, bias=1.0),
             ["lsp"], ["lsp"])
        C.op("dve", lambda: dve.tensor_scalar(out=lsp[:, 3:6], in0=lsp[:, 0:3], scalar1=-8.0, scalar2=None,
                                              op0=ALU.mult), ["lsp"], ["lsp"])
        C.op("dve", lambda: dve.tensor_scalar(out=lsp[:, 6:9], in0=lsp[:, 0:3], scalar1=-16.0, scalar2=None,
                                              op0=ALU.mult), ["lsp"], ["lsp"])
        C.op("dve", lambda: dve.tensor_scalar(out=nbias[:, 0:3], in0=par[:, _P["brg"] + 3 * l:_P["brg"] + 3 * l + 3],
                                              scalar1=-1.0, scalar2=None, op0=ALU.mult), ["par", "nbias"], ["nbias"])
        C.op("dve", lambda: dve.tensor_scalar(out=nbias[:, 3:6], in0=par[:, _P["big"] + 3 * l:_P["big"] + 3 * l + 3],
                                              scalar1=-1.0, scalar2=None, op0=ALU.mult), ["par", "nbias"], ["nbias"])

        C.op("dve", lambda: dve.memset(Hprev3[:], 0.0), ["Hprev3"], ["Hprev3"])
        C.op("dve", lambda: dve.memset(Sch[:, 0, :], 0.0), ["Sch"], ["Sch"])
        for T in range(nwave):
            tok = slice(T * TT, (T + 1) * TT)
            pty = l % 2
            src = xT if l == 0 else xs[(l - 1) % 2]
            for kc in range(8):
                C.dma("sp", xt[:, kc, :], src[:, kc, tok], [("X", l, T, kc)], [("xt", kc)], f"xt{kc}")
            C.dma("sp", cosb[64:96, :], cos_d[:, tok], [], ["cosb"], "cosb")
            C.dma("sp", sinb[64:96, :], sin_d[:, tok], [], ["sinb"], "sinb")
            sumsq_rstd([xt[:, kc, :] for kc in range(8)], float(D), [[("xt", kc)] for kc in range(8)])
            for kc in range(8):
                C.op("dve", lambda: dve.scalar_tensor_tensor(out=hT[:, kc, :], in0=xt[:, kc, :],
                                                             scalar=P_("normg", l, 8, kc), in1=rs[:],
                                                             op0=ALU.mult, op1=ALU.mult),
                     [("xt", kc), "rs", "par"], [("hT", kc)])
            def inproj(col0, M):
                bank, bkey = gbank()
                for kc in range(8):
                    C.op("pe", lambda: pe.matmul(bank[0:M, :], lhsT=win[:, kc, col0:col0 + M], rhs=hT[:, kc, :],
                                                 start=(kc == 0), stop=(kc == 7)),
                         [("win", kc), ("hT", kc)], [bkey], sig=(kc == 7))
                return bank, bkey

            for c in range(3):
                bank, bkey = inproj(O_ZA + 128 * c, 128)
                C.op("act", lambda: act.activation(out=zapad[:, c, 3:TT + 3], in_=bank[:], func=AF.Copy),
                     [bkey], ["zapad"])
            for c in range(2):
                bank, bkey = inproj(O_ZC + 128 * c, 128)
                C.op("act", lambda: act.activation(out=zcpad[:, c, 15:TT + 15], in_=bank[:], func=AF.Copy),
                     [bkey], ["zcpad"])
            C.op("dve", lambda: dve.tensor_copy(out=s1[:, 0:9].rearrange("p (c k) -> p c k", k=3),
                                                in_=zapad[:, :, TT:TT + 3]), ["zapad"], ["s1"])
            C.op("dve", lambda: dve.tensor_copy(out=s1[:, 9:39].rearrange("p (c k) -> p c k", k=15),
                                                in_=zcpad[:, :, TT:TT + 15]), ["zcpad"], ["s1"])
            C.dma("sp", send1[T], s1[:], ["s1"], [("send1", T)], "s1")
            C.op("pool", lambda: pool.collective_compute("AllGather", ALU.bypass, replica_groups=GROUPS,
                                                         ins=[send1[T]], outs=[recv1[T]]),
                 [("send1", T)], [("recv1", T)])
            C.dma("pool", H1[:], recv1[T].rearrange("(r p) n -> p r n", p=128), [("recv1", T)], ["H1"], "H1")

            for c in range(2):
                bank, bkey = inproj(O_CKV + 128 * c, 128)
                C.op("act", lambda: act.activation(out=ckvraw[:, c, :], in_=bank[:], func=AF.Copy),
                     [bkey], ["ckvraw"])
            bankA, kA = inproj(O_KR - 64, 96)
            bankB, kB = inproj(O_KRS - 64, 96)
            C.op("dve", lambda: dve.tensor_tensor(out=t1[64:96, :], in0=bankA[64:96, :], in1=cosb[64:96, :], op=ALU.mult),
                 [kA, "cosb"], ["r_t"])
            C.op("dve", lambda: dve.tensor_tensor(out=t2[64:96, :], in0=bankB[64:96, :], in1=sinb[64:96, :], op=ALU.mult),
                 [kB, "sinb"], ["i_t"])
            C.op("dve", lambda: dve.tensor_tensor(out=krst[64:96, :], in0=t1[64:96, :], in1=t2[64:96, :], op=ALU.add),
                 ["r_t", "i_t"], ["krst"])
            for h in range(6):
                C.dma("sp", send3[pty][T][h // 2][64:96, (h % 2) * 512:(h % 2) * 512 + 512], krst[64:96, :], ["krst"],
                      [("s3", h // 2, "r", h % 2)], "krst")
            sumsq_rstd([ckvraw[:, c, :] for c in range(2)], 256.0, ["ckvraw"])
            for c in range(2):
                C.op("dve", lambda: dve.scalar_tensor_tensor(out=ckvn[:, c, :], in0=ckvraw[:, c, :],
                                                             scalar=P_("kvng", l, 2, c), in1=rs[:],
                                                             op0=ALU.mult, op1=ALU.mult),
                     ["ckvraw", "rs", "par"], ["ckvn"])
            for c in range(3):
                bank, bkey = inproj(O_CQ + 128 * c, 128)
                C.op("act", lambda: act.activation(out=cqraw[:, c, :], in_=bank[:], func=AF.Copy),
                     [bkey], ["cqraw"])
            for p in range(3):
                bank, bkey = gbank()
                for c in range(2):
                    C.op("pe", lambda: pe.matmul(bank[:], lhsT=wuk[:, c, p * 128:(p + 1) * 128], rhs=ckvn[:, c, :],
                                                 start=(c == 0), stop=(c == 1)), ["wuk", "ckvn"], [bkey], sig=(c == 1))
                C.op("act", lambda: act.activation(out=kst[:, p, :], in_=bank[:], func=AF.Copy), [bkey], ["kst"])
            for p in range(3):
                C.dma("sp", send3[pty][T][p][0:64, 0:512], kst[0:64, p, :], ["kst"], [("s3", p, "n", 0)], "kst")
                C.dma("sp", send3[pty][T][p][0:64, 512:1024], kst[64:128, p, :], ["kst"], [("s3", p, "n", 1)], "kst")
            for blk in range(4):
                bank, bkey = gbank()
                for c in range(2):
                    C.op("pe", lambda: pe.matmul(bank[:, 0:384], lhsT=ckvn[:, c, blk * 128:(blk + 1) * 128], rhs=wuv[:, c, :],
                                                 start=(c == 0), stop=(c == 1)), ["wuv", "ckvn"], [bkey], sig=(c == 1))
                C.op("act", lambda: act.activation(out=vst[:, 0, :, blk, 0:64],
                                                   in_=bank[:, 0:192].rearrange("p (a d) -> p a d", d=64), func=AF.Copy),
                     [bkey], ["vst"])
                C.op("act", lambda: act.activation(out=vst[:, 1, :, blk, 64:128],
                                                   in_=bank[:, 192:384].rearrange("p (a d) -> p a d", d=64), func=AF.Copy),
                     [bkey], ["vst"])
            for q in range(3):
                C.dma("sp", send3[pty][T][q][:, 1024:2048].rearrange("p (a c) -> p a c", a=2),
                      vst[:, :, q, :, :].rearrange("p a k d -> p a (k d)"), ["vst"], [("s3", q, "v")], "vst")
            for q in range(3):
                s3keys = [("s3", q, "r", 0), ("s3", q, "r", 1), ("s3", q, "n", 0), ("s3", q, "n", 1), ("s3", q, "v"),
                          ("s3z", pty, T, q, 0), ("s3z", pty, T, q, 1)]
                C.op("pool", lambda: pool.collective_compute("AllGather", ALU.bypass, replica_groups=GROUPS,
                                                             ins=[send3[pty][T][q]], outs=[recv3[pty][T][q]]),
                     s3keys, [("recv3", T, q)])

            sumsq_rstd([cqraw[:, c, :] for c in range(3)], 384.0, ["cqraw"])
            for c in range(3):
                C.op("dve", lambda: dve.scalar_tensor_tensor(out=cqn[:, c, :], in0=cqraw[:, c, :],
                                                             scalar=P_("qng", l, 3, c), in1=rs[:],
                                                             op0=ALU.mult, op1=ALU.mult),
                     ["cqraw", "rs", "par"], ["cqn"])
            for h in range(6):
                bankA, kA = gbank()
                for c in range(3):
                    C.op("pe", lambda: pe.matmul(bankA[0:96, :], lhsT=wuqA[:, c, h * 96:(h + 1) * 96], rhs=cqn[:, c, :],
                                                 start=(c == 0), stop=(c == 2)), ["wuqA", "cqn"], [kA], sig=(c == 2))
                bankB, kB = gbank()
                for c in range(3):
                    C.op("pe", lambda: pe.matmul(bankB[0:96, :], lhsT=wuqB[:, c, h * 96:(h + 1) * 96], rhs=cqn[:, c, :],
                                                 start=(c == 0), stop=(c == 2)), ["wuqB", "cqn"], [kB], sig=(c == 2))
                C.op("act", lambda: act.activation(out=QT[0:64, h, :], in_=bankA[0:64, :], func=AF.Copy),
                     [kA], [("QT", h)])
                C.op("dve", lambda: dve.tensor_tensor(out=t1[64:96, :], in0=bankA[64:96, :], in1=cosb[64:96, :], op=ALU.mult),
                     [kA, "cosb"], ["r_t"])
                C.op("dve", lambda: dve.tensor_tensor(out=t2[64:96, :], in0=bankB[64:96, :], in1=sinb[64:96, :], op=ALU.mult),
                     [kB, "sinb"], ["i_t"])
                C.op("dve", lambda: dve.tensor_tensor(out=QT[64:96, h, :], in0=t1[64:96, :], in1=t2[64:96, :], op=ALU.add),
                     ["r_t", "i_t"], [("QT", h)])
            for (goff, ych, n) in [(O_GA, 0, 3), (O_GB, 3, 3), (O_GC, 6, 2)]:
                for c in range(n):
                    bank, bkey = inproj(goff + 128 * c, 128)
                    C.op("act", lambda: act.activation(out=yg[:, ych + c, :], in_=bank[:], func=AF.Silu),
                         [bkey], [("yg", ych + c)])


            def side_gen():
                C.op("dve", lambda: dve.tensor_scalar(out=halo[:], in0=Hprev3[:], scalar1=cpar[:, CP_W + 3:CP_W + 4],
                                                      scalar2=None, op0=ALU.mult), ["Hprev3", "cpar"], ["halo"])
                yield
                for k in range(3):
                    C.op("dve", lambda: dve.scalar_tensor_tensor(out=halo[:], in0=H1[:, k, :],
                                                                 scalar=cpar[:, CP_W + k:CP_W + k + 1], in1=halo[:],
                                                                 op0=ALU.mult, op1=ALU.add), ["H1", "cpar", "halo"], ["halo"])
                    yield
                C.op("dve", lambda: dve.tensor_copy(out=Hprev3[:], in_=H1[:, 3, :]), ["H1", "halo"], ["Hprev3"])
                yield
                C.op("dve", lambda: dve.tensor_copy(out=zapad[:, :, 0:3], in_=halo[:, 0:9].rearrange("p (c k) -> p c k", k=3)),
                     ["halo", "s1"], ["zapad"])
                yield
                C.op("dve", lambda: dve.tensor_copy(out=zcpad[:, :, 0:15], in_=halo[:, 9:39].rearrange("p (c k) -> p c k", k=15)),
                     ["halo", "s1"], ["zcpad"])
                yield

                for c in range(3):
                    cw = _P["convw"] + l * 12 + c * 4
                    C.op("dve", lambda: dve.tensor_scalar(out=xa[:], in0=zapad[:, c, 0:TT], scalar1=par[:, cw:cw + 1],
                                                          scalar2=P_("convb", l, 3, c), op0=ALU.mult, op1=ALU.add),
                         ["zapad", "par"], ["xa"])
                    yield
                    for k in range(1, 4):
                        C.op("dve", lambda: dve.scalar_tensor_tensor(out=xa[:], in0=zapad[:, c, k:k + TT],
                                                                     scalar=par[:, cw + k:cw + k + 1], in1=xa[:],
                                                                     op0=ALU.mult, op1=ALU.add),
                             ["zapad", "par", "xa"], ["xa"])
                        yield
                    C.op("dve", lambda: dve.tensor_copy(out=xab[:], in_=xa[:]), ["xa"], ["xab"])
                    yield
                    bank_r, kr_ = gbank()
                    C.op("pe", lambda: pe.matmul(bank_r[:], lhsT=wrg[:, c, :], rhs=xab[:], start=True, stop=True),
                         ["wrg", "xab"], [kr_])
                    yield
                    bank_i, ki_ = gbank()
                    C.op("pe", lambda: pe.matmul(bank_i[:], lhsT=wig[:, c, :], rhs=xab[:], start=True, stop=True),
                         ["wig", "xab"], [ki_])
                    yield
                    for (dst, bnk, bk, col) in ((r_t, bank_r, kr_, c), (i_t, bank_i, ki_, 3 + c)):
                        dk = "r_t" if dst is r_t else "i_t"
                        C.op("act", lambda: act.activation(out=dst[:], in_=bnk[:], func=AF.Exp, scale=-1.0,
                                                           bias=nbias[:, col:col + 1]), [bk, "nbias"], [dk])
                        yield
                        C.op("act", lambda: act.activation(out=dst[:], in_=dst[:], func=AF.Ln, bias=1.0), [dk], [dk])
                        yield
                        C.op("act", lambda: act.activation(out=dst[:], in_=dst[:], func=AF.Exp, scale=-1.0), [dk], [dk])
                        yield
                    C.op("act", lambda: act.activation(out=a_t[:], in_=r_t[:], func=AF.Exp, scale=lsp[:, 3 + c:4 + c]),
                         ["r_t", "lsp"], ["a_t"])
                    yield
                    C.op("act", lambda: act.activation(out=a2_t[:], in_=r_t[:], func=AF.Exp, scale=lsp[:, 6 + c:7 + c]),
                         ["r_t", "lsp"], ["a2_t"])
                    yield
                    C.op("act", lambda: act.activation(out=a2_t[:], in_=a2_t[:], func=AF.Ln, scale=-1.0, bias=1.0),
                         ["a2_t"], ["a2_t"])
                    yield
                    C.op("act", lambda: act.activation(out=a2_t[:], in_=a2_t[:], func=AF.Exp, scale=0.5),
                         ["a2_t"], ["a2_t"])
                    yield
                    C.op("dve", lambda: dve.tensor_tensor(out=u_t[:], in0=i_t[:], in1=xa[:], op=ALU.mult),
                         ["i_t", "xa"], ["u_t"])
                    yield
                    C.op("dve", lambda: dve.tensor_tensor(out=u_t[:], in0=u_t[:], in1=a2_t[:], op=ALU.mult),
                         ["u_t", "a2_t"], ["u_t"])
                    yield
                    C.op("dve", lambda: dve.tensor_tensor_scan(out=hloc[:, c, :], data0=a_t[:], data1=u_t[:], initial=0.0,
                                                               op0=ALU.mult, op1=ALU.add),
                         ["a_t", "u_t"], [("hloc", c)])
                    yield
                    C.op("dve", lambda: dve.tensor_tensor_scan(out=Ab[:, c, :], data0=a_t[:], data1=zeros_t[:], initial=1.0,
                                                               op0=ALU.mult, op1=ALU.add),
                         ["a_t", "zeros_t"], [("Ab", c)])
                    yield
                    C.op("dve", lambda: dve.tensor_copy(out=s2[:, c:c + 1], in_=hloc[:, c, TT - 1:TT]), [("hloc", c)], ["s2"])
                    yield
                    C.op("dve", lambda: dve.tensor_copy(out=s2[:, 3 + c:4 + c], in_=Ab[:, c, TT - 1:TT]), [("Ab", c)], ["s2"])
                    yield

                C.dma("pool", send2[T], s2[:], ["s2"], [("send2", T)], "s2")
                yield
                C.op("pool", lambda: pool.collective_compute("AllGather", ALU.bypass, replica_groups=GROUPS,
                                                             ins=[send2[T]], outs=[recv2[T]]),
                     [("send2", T)], [("recv2", T)])
                yield
                C.dma("pool", C2[:], recv2[T].rearrange("(r p) n -> p r n", p=128), [("recv2", T)], ["C2"], "C2")
                yield
                for c in range(2):
                    z = zcpad[:, c, :]
                    W = TT + 15
                    C.op("dve", lambda: dve.tensor_tensor(out=sA[:, 1:W], in0=z[:, 1:W], in1=z[:, 0:W - 1], op=ALU.add),
                         ["zcpad"], ["sA"])
                    yield
                    C.op("dve", lambda: dve.tensor_tensor(out=sB[:, 3:W], in0=sA[:, 3:W], in1=sA[:, 1:W - 2], op=ALU.add),
                         ["sA"], ["sB"])
                    yield
                    if c == 0:
                        lo, hi = sA, sB
                    else:
                        C.op("dve", lambda: dve.tensor_tensor(out=sA[:, 7:W], in0=sB[:, 7:W], in1=sB[:, 3:W - 4], op=ALU.add),
                             ["sB"], ["sA"])
                        yield
                        C.op("dve", lambda: dve.tensor_tensor(out=sB[:, 15:W], in0=sA[:, 15:W], in1=sA[:, 7:W - 8], op=ALU.add),
                             ["sA"], ["sB"])
                        yield
                        lo, hi = sA, sB
                    iw = _P["invw"] + c
                    for (p0, p1, stg) in [(0, 64, lo), (64, 128, hi)]:
                        C.op("dve", lambda: dve.scalar_tensor_tensor(out=pooled[p0:p1, :], in0=stg[p0:p1, 15:W],
                                                                     scalar=par[p0:p1, iw:iw + 1], in1=z[p0:p1, 15:W],
                                                                     op0=ALU.mult, op1=ALU.subtract),
                             ["sA", "sB", "zcpad", "par"], ["pooled"])
                        yield
                        if T == 0:
                            it = CP_IT + 16 * c
                            C.op("dve", lambda: dve.tensor_tensor(out=tmp16[p0:p1, :], in0=stg[p0:p1, 15:31],
                                                                  in1=cpar[p0:p1, it:it + 16], op=ALU.mult),
                                 ["sA", "sB", "cpar"], ["tmp16"])
                            yield
                            C.op("dve", lambda: dve.tensor_tensor(out=pooled[p0:p1, 0:16], in0=tmp16[p0:p1, :],
                                                                  in1=z[p0:p1, 15:31], op=ALU.subtract),
                                 ["tmp16", "zcpad"], ["pooled"])
                            yield
                    bank, bkey = gbank()
                    C.op("pe", lambda: pe.matmul(bank[:], lhsT=wpool[:, c, :], rhs=pooled[:], start=True, stop=True),
                         ["wpool", "pooled"], [bkey])
                    yield
                    C.op("dve", lambda: dve.scalar_tensor_tensor(out=yg[:, 6 + c, :], in0=bank[:],
                                                                 scalar=P_("pscale", l, 2, c), in1=yg[:, 6 + c, :],
                                                                 op0=ALU.mult, op1=ALU.mult),
                         [bkey, "par", ("yg", 6 + c)], [("yg", 6 + c)])
                    yield


            steps = []
            for ph, mps in ((1, list(range(T))), (2, [T])):
                for q in range(3):
                    kts = [(mp, jp) for mp in mps for jp in range(NG)]
                    for n_, kt in enumerate(kts):
                        for hh in range(2):
                            for kb in range(4):
                                steps.append((2 * q + hh, kt, kb, n_ == 0 and kb == 0, n_ == len(kts) - 1 and kb == 3, ph))
            part = [cqraw[:, 0, :], cqraw[:, 1, :], cqraw[:, 2, :], ckvraw[:, 0, :], ckvraw[:, 1, :], rs[:]]
            pkey = ["cqraw", "cqraw", "cqraw", "ckvraw", "ckvraw", "rs"]
            LA = 2
            info = {}

            def emit_qk(i):
                h, kt, kb, first, last, ph = steps[i]
                mp, jp = kt
                par_, pair = h % 2, h // 2
                if kb == 0 and par_ == 0:
                    slot = state["ring"] % NR
                    state["ring"] += 1
                    rkey = ("ring", slot)
                    src = recv3[pty][mp][pair]
                    C.dma("sp", kring[slot][0:96, :], src[jp * 128:jp * 128 + 96, 0:1024],
                          [("recv3", mp, pair)], [rkey], f"ring{slot}")
                    C.dma("sp", vring[slot][:], src[jp * 128:(jp + 1) * 128, 1024:2048],
                          [("recv3", mp, pair)], [rkey], f"ring{slot}")
                    info[(pair, kt, ph)] = slot
                slot = info[(h // 2, kt, ph)]
                rkey = ("ring", slot)
                si = 2 + state["sb"] % 3
                state["sb"] += 1
                pi = state["pr"] % NP
                state["pr"] += 1
                info[i] = (si, pi)
                if mp == T:
                    C.op("pe", lambda: pe.matmul(ps[si][:], lhsT=kring[slot][0:96, par_ * 512 + kb * 128:par_ * 512 + (kb + 1) * 128],
                                                 rhs=QT[0:96, h, :], start=True, stop=False),
                         [rkey, ("QT", h)], [("ps", si)], sig=False)
                    C.op("pe", lambda: pe.matmul(ps[si][:], lhsT=Ig[:, jp, :], rhs=emask[:, 384 - 128 * kb:896 - 128 * kb],
                                                 start=False, stop=True),
                         ["Ig", "emask"], [("ps", si)])
                    C.op("act", lambda: act.activation(out=pring[pi][:], in_=ps[si][:], func=AF.Exp, scale=SCALE,
                                                       bias=cpar[:, CP_EB + jp:CP_EB + jp + 1]),
                         [("ps", si), "cpar"], [("P", pi)])
                else:
                    C.op("pe", lambda: pe.matmul(ps[si][:], lhsT=kring[slot][0:96, par_ * 512 + kb * 128:par_ * 512 + (kb + 1) * 128],
                                                 rhs=QT[0:96, h, :], start=True, stop=True),
                         [rkey, ("QT", h)], [("ps", si)])
                    C.op("act", lambda: act.activation(out=pring[pi][:], in_=ps[si][:], func=AF.Exp, scale=SCALE),
                         [("ps", si)], [("P", pi)])

            def emit_pv(i):
                h, kt, kb, first, last, ph = steps[i]
                par_, pair = h % 2, h // 2
                slot = info[(h // 2, kt, ph)]
                rkey = ("ring", slot)
                si, pi = info[i]
                ob = ps[h % 2]
                okey = ("ps", h % 2)
                C.op("pe", lambda: pe.matmul(ob[:], lhsT=vring[slot][:, par_ * 512 + kb * 128:par_ * 512 + (kb + 1) * 128], rhs=pring[pi][:],
                                             start=first, stop=last),
                     [rkey, ("P", pi)], [okey])
                if last and ph == 1:
                    C.op("dve", lambda: dve.tensor_copy(out=part[h], in_=ob[:]), [okey, pkey[h]], [pkey[h]])
                if last and ph == 2:
                    if T > 0:
                        C.op("dve", lambda: dve.tensor_tensor(out=part[h], in0=ob[:], in1=part[h], op=ALU.add),
                             [okey, pkey[h]], [pkey[h]])
                        src, skey = part[h], pkey[h]
                    else:
                        src, skey = ob, okey
                    if par_ == 0:
                        o0, o1, l0, l1 = 0, 64, 64, 128
                    else:
                        o0, o1, l0, l1 = 64, 128, 0, 64
                    C.op("dve", lambda: dve.tensor_copy(out=lsh[o0:o1, :], in_=src[l0:l1, :]), [skey], [("sq", 0)])
                    C.op("dve", lambda: dve.reciprocal(out=lsh[o0:o1, :], in_=lsh[o0:o1, :]), [("sq", 0)], [("sq", 0)])
                    C.op("dve", lambda: dve.tensor_tensor(out=otmp[o0:o1, :], in0=src[o0:o1, :], in1=lsh[o0:o1, :],
                                                          op=ALU.mult), [skey, ("sq", 0)], [("sq", 1)])
                    C.op("dve", lambda: dve.tensor_tensor(out=yg[o0:o1, 3 + pair, :], in0=otmp[o0:o1, :],
                                                          in1=yg[o0:o1, 3 + pair, :], op=ALU.mult),
                         [("sq", 1), ("yg", 3 + pair)], [("yg", 3 + pair)])

            nst = len(steps)
            state["attn"] = True
            sg = side_gen()
            for i in range(nst + LA):
                if i < nst:
                    emit_qk(i)
                for _ in range(5):
                    next(sg, None)
                if i >= LA:
                    emit_pv(i - LA)
            for _ in sg:
                pass
            state["attn"] = False

            for k in range(NG):
                C.op("dve", lambda: dve.tensor_tensor(out=Sch[:, k + 1, :], in0=C2[:, k, 3:6], in1=Sch[:, k, :], op=ALU.mult),
                     ["C2", "Sch"], ["Sch"])
                C.op("dve", lambda: dve.tensor_tensor(out=Sch[:, k + 1, :], in0=Sch[:, k + 1, :], in1=C2[:, k, 0:3], op=ALU.add),
                     ["C2", "Sch"], ["Sch"])
            C.op("dve", lambda: dve.tensor_scalar(out=carry[:], in0=Sch[:, 0, :], scalar1=cpar[:, CP_W + 3:CP_W + 4],
                                                  scalar2=None, op0=ALU.mult), ["Sch", "cpar"], ["carry"])
            for k in range(3):
                C.op("dve", lambda: dve.scalar_tensor_tensor(out=carry[:], in0=Sch[:, k + 1, :],
                                                             scalar=cpar[:, CP_W + k:CP_W + k + 1], in1=carry[:],
                                                             op0=ALU.mult, op1=ALU.add), ["Sch", "cpar", "carry"], ["carry"])
            C.op("dve", lambda: dve.tensor_copy(out=Sch[:, 0, :], in_=Sch[:, NG, :]), ["Sch", "carry"], ["Sch"])
            for c in range(3):
                C.op("dve", lambda: dve.scalar_tensor_tensor(out=hloc[:, c, :], in0=Ab[:, c, :], scalar=carry[:, c:c + 1],
                                                             in1=hloc[:, c, :], op0=ALU.mult, op1=ALU.add),
                     [("Ab", c), ("hloc", c), "carry"], [("hloc", c)])
                C.op("dve", lambda: dve.tensor_tensor(out=yg[:, c, :], in0=hloc[:, c, :], in1=yg[:, c, :], op=ALU.mult),
                     [("hloc", c), ("yg", c)], [("yg", c)])

            early, late = [6, 7, 3, 4], [5, 0, 1, 2]
            obanks = [gbank() for _ in range(8)]
            for oc in range(8):
                bank, bkey = obanks[oc]
                for n_, kc in enumerate(early):
                    C.op("pe", lambda: pe.matmul(bank[:], lhsT=wout[:, kc, oc * 128:(oc + 1) * 128], rhs=yg[:, kc, :],
                                                 start=(n_ == 0), stop=False),
                         [("wout", kc), ("yg", kc)], [bkey], sig=(n_ == len(early) - 1))
            for oc in range(8):
                bank, bkey = obanks[oc]
                for n_, kc in enumerate(late):
                    C.op("pe", lambda: pe.matmul(bank[:], lhsT=wout[:, kc, oc * 128:(oc + 1) * 128], rhs=yg[:, kc, :],
                                                 start=False, stop=(n_ == len(late) - 1)),
                         [("wout", kc), ("yg", kc)], [bkey], sig=(n_ == len(late) - 1))
                C.op("dve", lambda: dve.tensor_tensor(out=xt[:, oc, :], in0=xt[:, oc, :], in1=bank[:], op=ALU.add),
                     [("xt", oc), bkey], [("xt", oc)])
                if l < depth - 1:
                    C.dma("sp", xs[l % 2][:, oc, tok], xt[:, oc, :], [("xt", oc)], [("X", l + 1, T, oc)], f"xt{oc}")
            if l == depth - 1:
                sumsq_rstd([xt[:, kc, :] for kc in range(8)], float(D), [[("xt", kc)] for kc in range(8)])
                for kc in range(8):
                    C.op("dve", lambda: dve.scalar_tensor_tensor(out=xt[:, kc, :], in0=xt[:, kc, :],
                                                                 scalar=P_("fng", None, 1, kc), in1=rs[:],
                                                                 op0=ALU.mult, op1=ALU.mult),
                         [("xt", kc), "rs", "par"], [("xt", kc)])
                    C.dma("sp", outT[:, kc, tok], xt[:, kc, :], [("xt", kc)], [("OUT", T, kc)], f"xt{kc}")
    C.finish("sp")
    return nc


def _prep_shared(inp):
    f = np.float32
    w_in = np.asarray(inp["w_in"], f)
    offs = np.cumsum([0, 384, 384, 384, 256, 32, 384, 256, 256])
    za, ga, cq, ckv, kr, gb, zc, gc = [np.arange(offs[i], offs[i + 1]) for i in range(8)]
    krs = np.concatenate([kr[16:32], kr[0:16]])
    perm = np.concatenate([za, ga, cq, ckv, gb, zc, gc, kr, krs])
    assert perm.size == WINC

    def kmaj(w, nk):
        L, _, N = w.shape
        return np.ascontiguousarray(w.reshape(L, nk, 128, N).transpose(0, 2, 1, 3))

    d = {}
    d["w_in"] = kmaj(w_in[:, :, perm], 8)
    d["w_out"] = kmaj(np.asarray(inp["w_out"], f), 8)
    w_uq = np.asarray(inp["w_uq"], f).reshape(DEPTH, 384, 6, 96)
    nope, rope = w_uq[..., :64], w_uq[..., 64:]
    A = np.concatenate([nope, rope], axis=-1).reshape(DEPTH, 384, 576)
    Bm = np.concatenate([nope, rope[..., 16:], rope[..., :16]], axis=-1).reshape(DEPTH, 384, 576)
    d["w_uqA"] = kmaj(A, 3)
    d["w_uqB"] = kmaj(Bm, 3)
    w_ukv = np.asarray(inp["w_ukv"], f).reshape(DEPTH, 256, 6, 128)
    d["w_uk"] = kmaj(np.ascontiguousarray(w_ukv[..., :64]).reshape(DEPTH, 256, 384), 2)
    v = w_ukv[..., 64:]
    v = np.concatenate([v[:, :, 0::2, :], v[:, :, 1::2, :]], axis=2).reshape(DEPTH, 256, 384)
    d["w_uv"] = kmaj(v, 2)

    def bdiag(w, n):
        L = w.shape[0]
        o = np.zeros((L, 128, n, 128), f)
        for c in range(n):
            o[:, 0:64, c, 0:64] = w[:, 2 * c]
            o[:, 64:128, c, 64:128] = w[:, 2 * c + 1]
        return o

    d["w_rg"] = bdiag(np.asarray(inp["w_rg"], f), 3)
    d["w_ig"] = bdiag(np.asarray(inp["w_ig"], f), 3)
    d["w_pool"] = bdiag(np.asarray(inp["w_pool"], f), 2)

    par = np.zeros((128, NPAR), f)

    def put(name, arr):
        a = np.asarray(arr, f)
        lead = a.shape[:-1]
        nch = a.shape[-1] // 128
        a = a.reshape(lead + (nch, 128))
        a = np.moveaxis(a, -1, 0).reshape(128, -1)
        par[:, _P[name]:_P[name] + a.shape[1]] = a

    put("normg", inp["norm_g"])
    put("fng", inp["final_norm_g"])
    cw = np.asarray(inp["conv_w"], f)
    cw = cw.reshape(DEPTH, 4, 3, 128).transpose(3, 0, 2, 1).reshape(128, DEPTH * 12)
    par[:, _P["convw"]:_P["convw"] + DEPTH * 12] = cw
    put("convb", inp["conv_b"])
    put("brg", inp["b_rg"])
    put("big", inp["b_ig"])
    put("lam", inp["lru_lambda"])
    put("qng", inp["q_norm_g"])
    put("kvng", inp["kv_norm_g"])
    put("pscale", inp["pool_scale"])
    wins = np.array([[2, 4], [8, 16]], f)
    for c in range(2):
        for hf in range(2):
            p0 = 64 * hf
            par[p0:p0 + 64, _P["invw"] + c] = f(1.0) / wins[c, hf]
    d["params"] = par
    cc = np.arange(896)[None, :] - 384
    d["emask"] = np.where(np.arange(128)[:, None] <= cc, 0.0, -30000.0).astype(ml_dtypes.bfloat16)
    d["ident"] = np.eye(128, dtype=np.float32).astype(ml_dtypes.bfloat16)
    return d


def _prep_core(j):
    f = np.float32
    d = {}
    pos = np.concatenate([np.arange((NG * m + j) * TT, (NG * m + j + 1) * TT) for m in range(NW)]).astype(f)
    inv_freq = (f(10000.0) ** (-np.arange(0, 32, 2, dtype=f) / f(32))).astype(f)
    ang = (pos[None, :] * inv_freq[:, None]).astype(f)
    cs, sn = np.cos(ang).astype(f), np.sin(ang).astype(f)
    d["cosT"] = np.ascontiguousarray(np.concatenate([cs, cs], 0))
    d["sinT"] = np.ascontiguousarray(np.concatenate([-sn, sn], 0))
    cp = np.zeros((128, NCPAR), f)
    cp[:, CP_W + (j - 1) % NG] = 1.0
    for jp in range(NG):
        cp[:, CP_EB + jp] = 0.0 if jp <= j else -30000.0
        cp[:, CP_F + jp] = 1.0 if jp < j else 0.0
        cp[:, CP_G + jp] = 1.0 if jp == j else 0.0
    wins = np.array([[2, 4], [8, 16]], f)
    t = np.arange(16, dtype=f)
    for c in range(2):
        for hf in range(2):
            p0 = 64 * hf
            if j == 0:
                cp[p0:p0 + 64, CP_IT + 16 * c:CP_IT + 16 * c + 16] = f(1.0) / np.minimum(t + 1, wins[c, hf])
            else:
                cp[p0:p0 + 64, CP_IT + 16 * c:CP_IT + 16 * c + 16] = f(1.0) / wins[c, hf]
    d["cpar"] = cp
    return d


_CACHE = {}


def kernel(**inputs):
    x = np.asarray(inputs["x"], np.float32)
    shared = _prep_shared(inputs)
    if "nc" not in _CACHE:
        _CACHE["nc"] = build_program()
    nc = _CACHE["nc"]
    in_maps = []
    for c in range(NCORE):
        b, j = divmod(c, NG)
        m = dict(shared)
        m.update(_prep_core(j))
        xb = x[b].reshape(NW, NG, TT, 8, 128)[:, j]
        m["xT"] = np.ascontiguousarray(xb.transpose(3, 2, 0, 1).reshape(128, 8, TOKC))
        in_maps.append(m)
    res = run_bass_kernel_spmd(nc, in_maps, core_ids=list(range(NCORE)))
    out = np.empty((B, NW, NG, TT, D), np.float32)
    for c in range(NCORE):
        b, j = divmod(c, NG)
        o = np.asarray(res.results[c]["outT"], np.float32).reshape(128, 8, NW, TT)
        out[b, :, j] = o.transpose(2, 3, 1, 0).reshape(NW, TT, D)
    return out.reshape(B, S, D)
```
